# Optimizing a Trainium2 kernel written in Bass

```python
import jax, jax.numpy as jnp
from jax import lax
import numpy as np

D_MODEL = 1024
BATCH = 2
SEQ = 8192
DEPTH = 1
DEC_BATCH = 128
DEC_SEQ = 1
PAST_LEN = 8192
PAGE_SIZE = 128

RW_HEADS = 8
RW_HEAD_DIM = 64
RW_WIDTH = RW_HEADS * RW_HEAD_DIM
DECAY_LORA = 64
AAA_LORA = 64
GATE_LORA = 128
RW_COLS = 3 * RW_WIDTH + DECAY_LORA + AAA_LORA + GATE_LORA
RW_SPLITS = (RW_WIDTH, 2 * RW_WIDTH, 3 * RW_WIDTH, 3 * RW_WIDTH + DECAY_LORA,
             3 * RW_WIDTH + DECAY_LORA + AAA_LORA)
GN_EPS = 64e-5
SWA_HEADS = 8
SWA_KV_HEADS = 2
SWA_GROUPS = SWA_HEADS // SWA_KV_HEADS
SWA_HEAD_DIM = 64
SWA_Q = SWA_HEADS * SWA_HEAD_DIM
SWA_KV = SWA_KV_HEADS * SWA_HEAD_DIM
WINDOW = 128
BLOCK = 128
ROPE_THETA = 500000.0
ROPE_DIM = SWA_HEAD_DIM // 4
MEM_LEN = 256
MEM_HEADS = 4
MEM_HEAD_DIM = 128
MEM_WIDTH = MEM_HEADS * MEM_HEAD_DIM
N_BRANCH = 3
D_FF = 4 * D_MODEL
NORM_EPS = 1e-5
IN_COLS = RW_COLS + SWA_Q + 2 * SWA_KV + MEM_WIDTH + N_BRANCH * D_MODEL
IN_SPLITS = (RW_COLS, RW_COLS + SWA_Q, RW_COLS + SWA_Q + SWA_KV, RW_COLS + SWA_Q + 2 * SWA_KV,
             RW_COLS + SWA_Q + 2 * SWA_KV + MEM_WIDTH)

kernel_name = 'hybrid_rwkv7_swa_sink_memxattn_decode_step'

F32 = jnp.float32


def _rmsnorm(x, g):
    xf = x.astype(F32)
    y = xf * lax.rsqrt(jnp.mean(xf * xf, axis=-1, keepdims=True) + NORM_EPS)
    return (y * g.astype(F32)).astype(x.dtype)


def _rope(x, pos):
    half = ROPE_DIM // 2
    inv_freq = jnp.power(jnp.float32(ROPE_THETA), -jnp.arange(half, dtype=F32) * (2.0 / ROPE_DIM))
    ang = pos.astype(F32)[:, None] * inv_freq[None, :]
    cos = jnp.cos(ang)[:, None, :]
    sin = jnp.sin(ang)[:, None, :]
    xf = x.astype(F32)
    x1 = xf[..., :half]
    x2 = xf[..., half:ROPE_DIM]
    out = jnp.concatenate([x1 * cos - x2 * sin, x2 * cos + x1 * sin, xf[..., ROPE_DIM:]], axis=-1)
    return out.astype(x.dtype)


def _rwkv_scan(S0, r, decay, k, v, kk, a):
    def step(S, inp):
        r_t, w_t, k_t, v_t, kk_t, a_t = inp
        sa = jnp.einsum('bhvk,bhk->bhv', S, -kk_t)
        S = (S * w_t[:, :, None, :] + sa[..., None] * (kk_t * a_t)[:, :, None, :]
             + v_t[..., None] * k_t[:, :, None, :])
        y = jnp.einsum('bhvk,bhk->bhv', S, r_t)
        return S, y
    xs = tuple(jnp.moveaxis(t, 1, 0) for t in (r, decay, k, v, kk, a))
    S, ys = lax.scan(step, S0.astype(F32), xs)
    return S, jnp.moveaxis(ys, 0, 1)


def _rwkv_branch(z, z_prev, S0, p):
    B, T, _ = z.shape
    zs = z + (z_prev - z) * p['rw_mu']
    r, k, v, wd, ad, gd = jnp.split(zs, RW_SPLITS, axis=-1)
    w_log = -jax.nn.softplus(-(p['rw_w0'] + jnp.tanh(wd) @ p['rw_w2'])) - 0.5
    decay = jnp.exp(-jnp.exp(w_log.astype(F32)))
    a = jax.nn.sigmoid(p['rw_a0'] + ad @ p['rw_a2'])
    g = jax.nn.sigmoid(gd) @ p['rw_g2']

    def heads(t):
        return t.astype(F32).reshape(B, T, RW_HEADS, RW_HEAD_DIM)

    kk = heads(k * p['rw_k_k'])
    kk = kk / jnp.maximum(jnp.sqrt(jnp.sum(kk * kk, axis=-1, keepdims=True)), 1e-12)
    kh = heads(k * (1.0 + (a - 1.0) * p['rw_k_a']))
    rh, vh, ah, dh = heads(r), heads(v), heads(a), heads(decay)
    S_new, y = _rwkv_scan(S0, rh, dh, kh, vh, kk, ah)
    mu = jnp.mean(y, axis=-1, keepdims=True)
    var = jnp.mean(jnp.square(y - mu), axis=-1, keepdims=True)
    y = ((y - mu) * lax.rsqrt(var + GN_EPS)).reshape(B, T, RW_WIDTH)
    y = y * p['rw_ln_w'].astype(F32) + p['rw_ln_b'].astype(F32)
    bonus = jnp.sum(rh * kh * p['rw_r_k'].astype(F32), axis=-1, keepdims=True) * vh
    out = (y + bonus.reshape(B, T, RW_WIDTH)) * g.astype(F32)
    return out.astype(z.dtype), S_new


def _sink_attn(q, k, v, q_pos, k_pos, sinks):
    s = jnp.einsum('...qhgd,...shd->...hgqs', q.astype(F32), k.astype(F32)) * (SWA_HEAD_DIM ** -0.5)
    rel = q_pos[..., :, None] - k_pos[..., None, :]
    valid = (rel >= 0) & (rel <= WINDOW) & (k_pos[..., None, :] >= 0)
    s = jnp.where(valid[..., None, None, :, :], s, -jnp.inf)
    sink = jnp.broadcast_to(sinks.astype(F32).reshape(SWA_KV_HEADS, SWA_GROUPS, 1, 1), s.shape[:-1] + (1,))
    pr = jax.nn.softmax(jnp.concatenate([s, sink], axis=-1), axis=-1)[..., :-1]
    o = jnp.einsum('...hgqs,...shd->...qhgd', pr, v.astype(F32))
    return o.astype(v.dtype)


def _swa_prompt(q, k, v, sinks):
    B, T = q.shape[0], q.shape[1]
    nb = T // BLOCK
    qb = q.reshape(B, nb, BLOCK, SWA_KV_HEADS, SWA_GROUPS, SWA_HEAD_DIM)
    kb = k.reshape(B, nb, BLOCK, SWA_KV_HEADS, SWA_HEAD_DIM)
    vb = v.reshape(B, nb, BLOCK, SWA_KV_HEADS, SWA_HEAD_DIM)
    pad = ((0, 0), (1, 0), (0, 0), (0, 0), (0, 0))
    k_band = jnp.concatenate([jnp.pad(kb, pad)[:, :-1], kb], axis=2)
    v_band = jnp.concatenate([jnp.pad(vb, pad)[:, :-1], vb], axis=2)
    pos = jnp.arange(T, dtype=jnp.int32).reshape(nb, BLOCK)
    k_pos = jnp.concatenate([pos - BLOCK, pos], axis=1)
    o = _sink_attn(qb, k_band, v_band, pos, k_pos, sinks).reshape(B, T, SWA_Q)
    keep = min(WINDOW, T)
    return o, k[:, T - keep:], v[:, T - keep:]


def _swa_sample(q, k, v, k_past, v_past, sinks):
    B, T = q.shape[0], q.shape[1]
    W = k_past.shape[1]
    q_pos = PAST_LEN + jnp.arange(T, dtype=jnp.int32)
    k_pos = jnp.concatenate([PAST_LEN - W + jnp.arange(W, dtype=jnp.int32), q_pos])
    k_all = jnp.concatenate([k_past.astype(k.dtype), k], axis=1)
    v_all = jnp.concatenate([v_past.astype(v.dtype), v], axis=1)
    o = _sink_attn(q, k_all, v_all, q_pos, k_pos, sinks).reshape(B, T, SWA_Q)
    return o, k_all[:, -W:], v_all[:, -W:]


def _mem_kv(mem, mem_norm, w_mem_kv, xk_norm):
    B, M, _ = mem.shape
    kv = _rmsnorm(mem, mem_norm) @ w_mem_kv
    mk = kv[..., :MEM_WIDTH].reshape(B, M, MEM_HEADS, MEM_HEAD_DIM)
    mv = kv[..., MEM_WIDTH:].reshape(B, M, MEM_HEADS, MEM_HEAD_DIM)
    return _rmsnorm(mk, xk_norm), mv


def _mem_attn(q, mk, mv):
    s = jnp.einsum('bthd,bmhd->bhtm', q.astype(F32), mk.astype(F32)) * (MEM_HEAD_DIM ** -0.5)
    pr = jax.nn.softmax(s, axis=-1)
    o = jnp.einsum('bhtm,bmhd->bthd', pr, mv.astype(F32))
    return o.astype(q.dtype)


def _layer(x, pos, z_prev0, S0, k_past, v_past, mem_k, mem_v, p):
    B, T, _ = x.shape
    h = _rmsnorm(x, p['norm_mix'])
    proj = h @ p['w_in']
    z, q, k, v, xq, gates = jnp.split(proj, IN_SPLITS, axis=-1)
    z_prev = jnp.concatenate([z_prev0[:, None, :].astype(z.dtype), z[:, :-1]], axis=1)
    out_a, S_new = _rwkv_branch(z, z_prev, S0, p)
    qh = _rope(_rmsnorm(q.reshape(B, T, SWA_HEADS, SWA_HEAD_DIM), p['q_norm']), pos)
    kh = _rope(_rmsnorm(k.reshape(B, T, SWA_KV_HEADS, SWA_HEAD_DIM), p['k_norm']), pos)
    vh = v.reshape(B, T, SWA_KV_HEADS, SWA_HEAD_DIM)
    qh = qh.reshape(B, T, SWA_KV_HEADS, SWA_GROUPS, SWA_HEAD_DIM)
    if k_past is None:
        out_b, k_keep, v_keep = _swa_prompt(qh, kh, vh, p['swa_sinks'])
    else:
        out_b, k_keep, v_keep = _swa_sample(qh, kh, vh, k_past, v_past, p['swa_sinks'])
    xqh = _rmsnorm(xq.reshape(B, T, MEM_HEADS, MEM_HEAD_DIM), p['xq_norm'])
    out_c = _mem_attn(xqh, mem_k, mem_v).reshape(B, T, MEM_WIDTH)
    g = jax.nn.sigmoid(gates)
    merged = (g[..., :D_MODEL] * (out_a @ p['w_br_a'])
              + g[..., D_MODEL:2 * D_MODEL] * (out_b @ p['w_br_b'])
              + g[..., 2 * D_MODEL:] * (out_c @ p['w_br_c']))
    x = x + merged @ p['w_out']
    h2 = _rmsnorm(x, p['norm_ffn'])
    x = x + jnp.square(jax.nn.relu(h2 @ p['w_up'])) @ p['w_down']
    return x, S_new, z[:, -1], k_keep, v_keep


def setup_inputs(seed: int = 0) -> dict:
    key = jax.random.key(seed)
    ks = jax.random.split(key, 36)

    def nrm(i, shape, scale):
        return jax.random.normal(ks[i], shape, F32) * scale

    def gain(i, n):
        return 1.0 + nrm(i, (DEPTH, n), 0.02)

    w_swa = min(WINDOW, PAST_LEN)
    return {
        'x_prompt': nrm(0, (BATCH, SEQ, D_MODEL), 1.0),
        'x_sample': nrm(1, (DEC_BATCH, DEC_SEQ, D_MODEL), 1.0),
        'state_rwkv': nrm(2, (DEPTH, DEC_BATCH, RW_HEADS, RW_HEAD_DIM, RW_HEAD_DIM), 0.3),
        'state_rwkv_shift': nrm(3, (DEPTH, DEC_BATCH, RW_COLS), 1.0),
        'cache_swa_k': nrm(4, (DEPTH, DEC_BATCH, w_swa, SWA_KV_HEADS, SWA_HEAD_DIM), 1.0),
        'cache_swa_v': nrm(5, (DEPTH, DEC_BATCH, w_swa, SWA_KV_HEADS, SWA_HEAD_DIM), 1.0),
        'cache_mem_k': nrm(6, (DEPTH, DEC_BATCH, MEM_LEN, MEM_HEADS, MEM_HEAD_DIM), 1.0),
        'cache_mem_v': nrm(7, (DEPTH, DEC_BATCH, MEM_LEN, MEM_HEADS, MEM_HEAD_DIM), 1.0),
        'mem_prompt': nrm(8, (BATCH, MEM_LEN, D_MODEL), 1.0),
        'norm_mix': gain(9, D_MODEL),
        'w_in': nrm(10, (DEPTH, D_MODEL, IN_COLS), D_MODEL ** -0.5),
        'rw_mu': jax.random.uniform(ks[11], (DEPTH, RW_COLS), F32),
        'rw_w0': -2.0 + nrm(12, (DEPTH, RW_WIDTH), 0.5),
        'rw_w2': nrm(13, (DEPTH, DECAY_LORA, RW_WIDTH), 0.5 * DECAY_LORA ** -0.5),
        'rw_a0': nrm(14, (DEPTH, RW_WIDTH), 0.1),
        'rw_a2': nrm(15, (DEPTH, AAA_LORA, RW_WIDTH), AAA_LORA ** -0.5),
        'rw_g2': nrm(16, (DEPTH, GATE_LORA, RW_WIDTH), GATE_LORA ** -0.5),
        'rw_k_k': 0.85 + nrm(17, (DEPTH, RW_WIDTH), 0.05),
        'rw_k_a': 1.0 + nrm(18, (DEPTH, RW_WIDTH), 0.05),
        'rw_r_k': nrm(19, (DEPTH, RW_HEADS, RW_HEAD_DIM), 0.1),
        'rw_ln_w': gain(20, RW_WIDTH),
        'rw_ln_b': nrm(21, (DEPTH, RW_WIDTH), 0.01),
        'q_norm': gain(22, SWA_HEAD_DIM),
        'k_norm': gain(23, SWA_HEAD_DIM),
        'swa_sinks': nrm(24, (DEPTH, SWA_HEADS), 1.0),
        'mem_norm': gain(25, D_MODEL),
        'w_mem_kv': nrm(26, (DEPTH, D_MODEL, 2 * MEM_WIDTH), D_MODEL ** -0.5),
        'xq_norm': gain(27, MEM_HEAD_DIM),
        'xk_norm': gain(28, MEM_HEAD_DIM),
        'w_br_a': nrm(29, (DEPTH, RW_WIDTH, D_MODEL), RW_WIDTH ** -0.5),
        'w_br_b': nrm(30, (DEPTH, SWA_Q, D_MODEL), SWA_Q ** -0.5),
        'w_br_c': nrm(31, (DEPTH, MEM_WIDTH, D_MODEL), MEM_WIDTH ** -0.5),
        'w_out': nrm(32, (DEPTH, D_MODEL, D_MODEL), D_MODEL ** -0.5),
        'norm_ffn': gain(33, D_MODEL),
        'w_up': nrm(34, (DEPTH, D_MODEL, D_FF), D_MODEL ** -0.5),
        'w_down': nrm(35, (DEPTH, D_FF, D_MODEL), D_FF ** -0.5),
    }


def reference(x_prompt, x_sample, state_rwkv, state_rwkv_shift, cache_swa_k, cache_swa_v,
              cache_mem_k, cache_mem_v, mem_prompt, norm_mix, w_in, rw_mu, rw_w0, rw_w2,
              rw_a0, rw_a2, rw_g2, rw_k_k, rw_k_a, rw_r_k, rw_ln_w, rw_ln_b, q_norm, k_norm,
              swa_sinks, mem_norm, w_mem_kv, xq_norm, xk_norm, w_br_a, w_br_b, w_br_c, w_out,
              norm_ffn, w_up, w_down):
    B, T, _ = x_prompt.shape
    pos_p = jnp.arange(T, dtype=jnp.int32)
    pos_s = PAST_LEN + jnp.arange(x_sample.shape[1], dtype=jnp.int32)
    yp, ys = x_prompt, x_sample
    lp_S, lp_z, lp_k, lp_v, lp_mk, lp_mv = [], [], [], [], [], []
    ls_S, ls_z, ls_k, ls_v = [], [], [], []
    for l in range(DEPTH):
        p = dict(norm_mix=norm_mix[l], w_in=w_in[l], rw_mu=rw_mu[l], rw_w0=rw_w0[l],
                 rw_w2=rw_w2[l], rw_a0=rw_a0[l], rw_a2=rw_a2[l], rw_g2=rw_g2[l],
                 rw_k_k=rw_k_k[l], rw_k_a=rw_k_a[l], rw_r_k=rw_r_k[l], rw_ln_w=rw_ln_w[l],
                 rw_ln_b=rw_ln_b[l], q_norm=q_norm[l], k_norm=k_norm[l], swa_sinks=swa_sinks[l],
                 xq_norm=xq_norm[l], w_br_a=w_br_a[l], w_br_b=w_br_b[l], w_br_c=w_br_c[l],
                 w_out=w_out[l], norm_ffn=norm_ffn[l], w_up=w_up[l], w_down=w_down[l])
        mk_p, mv_p = _mem_kv(mem_prompt, mem_norm[l], w_mem_kv[l], xk_norm[l])
        z0 = jnp.zeros((B, RW_COLS), x_prompt.dtype)
        S0 = jnp.zeros((B, RW_HEADS, RW_HEAD_DIM, RW_HEAD_DIM), F32)
        yp, Sp, zp, kp, vp = _layer(yp, pos_p, z0, S0, None, None, mk_p, mv_p, p)
        ys, Ss, zs_, ks_, vs_ = _layer(ys, pos_s, state_rwkv_shift[l], state_rwkv[l],
                                       cache_swa_k[l], cache_swa_v[l],
                                       cache_mem_k[l], cache_mem_v[l], p)
        lp_S.append(Sp); lp_z.append(zp); lp_k.append(kp); lp_v.append(vp)
        lp_mk.append(mk_p); lp_mv.append(mv_p)
        ls_S.append(Ss); ls_z.append(zs_); ls_k.append(ks_); ls_v.append(vs_)
    y_prompt, y_sample = yp, ys
    new_state_rwkv_prompt = jnp.stack(lp_S)
    new_shift_prompt = jnp.stack(lp_z)
    new_swa_k_prompt = jnp.stack(lp_k)
    new_swa_v_prompt = jnp.stack(lp_v)
    new_mem_k_prompt = jnp.stack(lp_mk)
    new_mem_v_prompt = jnp.stack(lp_mv)
    new_state_rwkv_sample = jnp.stack(ls_S)
    new_shift_sample = jnp.stack(ls_z)
    new_swa_k_sample = jnp.stack(ls_k)
    new_swa_v_sample = jnp.stack(ls_v)
    return (y_prompt, y_sample, new_state_rwkv_prompt, new_shift_prompt, new_swa_k_prompt,
            new_swa_v_prompt, new_mem_k_prompt, new_mem_v_prompt, new_state_rwkv_sample,
            new_shift_sample, new_swa_k_sample, new_swa_v_sample)
```

```python
import numpy as np
from contextlib import ExitStack
import concourse.bass as bass
import concourse.mybir as mybir
from concourse.bass_utils import run_bass_kernel_spmd

F32 = mybir.dt.float32
BF16 = mybir.dt.bfloat16
ALU = mybir.AluOpType
AF = mybir.ActivationFunctionType
AX = mybir.AxisListType

D = 1024
NCORES = 8
C0H = float(np.exp(-0.5) / 2.0)
EPS = 1e-5
GN_EPS = 64e-5
SAFE_OPS = 10 ** 9


class Tok:
    __slots__ = ("w", "r")

    def __init__(self):
        self.w = None
        self.r = {}


class Tile:
    def __init__(self, h):
        self.h = h
        self.tok = Tok()
        self.subs = {}

    def sub(self, key):
        if key not in self.subs:
            self.subs[key] = Tok()
        return self.subs[key]

    def __getitem__(self, k):
        return self.h[k]


def _tok(x):
    return x.tok if isinstance(x, Tile) else x


class Sched:
    ENG = ("pe", "act", "dve", "pool", "sp")

    def __init__(self, nc, es, n_dsem=12):
        self.nc = nc
        self.h = {"pe": nc.tensor, "act": nc.scalar, "dve": nc.vector, "pool": nc.gpsimd, "sp": nc.sync}
        self.streams = {e: [] for e in self.ENG}
        self.cnt = {e: 0 for e in self.ENG}
        self.waited = {e: {} for e in self.ENG}
        self.esem = {e: es.enter_context(nc.semaphore("es_" + e)) for e in self.ENG}
        self.dsem = {}
        self.dcnt = {}
        self.dnext = {}
        for q in ("sp", "pool", "act"):
            self.dsem[q] = [es.enter_context(nc.semaphore("ds_%s%d" % (q, i))) for i in range(n_dsem)]
            self.dcnt[q] = [0] * n_dsem
            self.dnext[q] = 0
        self.ccsem = es.enter_context(nc.semaphore("ccsem"))
        self.gseq = 0
        self.oplog = []
        self.gidx = {e: [] for e in self.ENG}

    def _semobj(self, key):
        if key[0] == "e":
            return self.esem[key[1]]
        if key[0] == "d":
            return self.dsem[key[1]][key[2]]
        return self.ccsem

    def _resolve(self, eng, deps):
        waits = []
        best = {}
        for (key, val) in deps:
            if key == ("e", eng) and eng == "pe":
                continue
            if best.get(key, 0) < val:
                best[key] = val
        for key, val in best.items():
            if self.waited[eng].get(key, 0) < val:
                self.waited[eng][key] = val
                waits.append((self._semobj(key), val))
        return waits

    def _deps(self, reads, writes):
        deps = []
        for t in reads:
            t = _tok(t)
            if t.w is not None:
                deps.append(t.w)
        for t in writes:
            t = _tok(t)
            if t.w is not None:
                deps.append(t.w)
            deps.extend(t.r.items())
        return deps

    def _mark(self, me, reads, writes):
        key, val = me
        for t in reads:
            t = _tok(t)
            if t.r.get(key, 0) < val:
                t.r[key] = val
        for t in writes:
            t = _tok(t)
            t.w = me
            t.r = {}

    def op(self, eng, fn, r=(), w=()):
        waits = self._resolve(eng, self._deps(r, w))
        self.cnt[eng] += 1
        me = (("e", eng), self.cnt[eng])
        self._mark(me, r, w)
        self.streams[eng].append((waits, fn, self.esem[eng], 1))
        self.gseq += 1
        self.gidx[eng].append(self.gseq)
        self._log(eng)

    def _log(self, eng):
        import sys
        f = sys._getframe(2)
        while f.f_code.co_name in ("mm", "tr", "act", "tt", "ts", "stt", "red", "cp", "memset", "op", "dma"):
            f = f.f_back
        self.oplog.append((self.gseq, eng, f.f_lineno))

    def dma(self, q, out, in_, r=(), w=()):
        import os
        if q == "pool" and os.environ.get("KNOPOOL"):
            return
        i = self.dnext[q]
        self.dnext[q] = (i + 1) % len(self.dsem[q])
        deps = self._deps(r, w)
        if self.dcnt[q][i] > 0:
            deps.append((("d", q, i), self.dcnt[q][i]))
        waits = self._resolve(q, deps)
        self.dcnt[q][i] += 16
        me = (("d", q, i), self.dcnt[q][i])
        self._mark(me, r, w)
        self.streams[q].append((waits, lambda e, o=out, s=in_: e.dma_start(out=o, in_=s), self.dsem[q][i], 16))
        self.gseq += 1
        self.gidx[q].append(self.gseq)
        self._log("dma-" + q)

    def barrier(self, toks=()):
        deps = []
        for e in self.ENG:
            if self.cnt[e] > 0:
                deps.append((("e", e), self.cnt[e]))
        for q in self.dsem:
            for i, c in enumerate(self.dcnt[q]):
                if c > 0:
                    deps.append((("d", q, i), c))
        for e in self.ENG:
            waits = self._resolve(e, deps)
            if waits:
                self.streams[e].append((waits, None, None, 0))
                self.gidx[e].append(self.gseq)

    def emit(self):
        import os
        nc = self.nc
        lim = int(os.environ.get("KSTOP", "0")) or SAFE_OPS
        totals = {}
        for name in self.ENG:
            for (waits, fn, sem, inc), gi in zip(self.streams[name], self.gidx[name]):
                if gi > lim or fn is None:
                    continue
                k = id(sem)
                totals[k] = (sem, totals.get(k, (sem, 0))[1] + inc)
        with nc.Block() as block:
            def runner(name):
                def run(eng):
                    for (waits, fn, sem, inc), gi in zip(self.streams[name], self.gidx[name]):
                        if gi > lim:
                            break
                        for s, v in waits:
                            eng.wait_ge(s, v)
                        if fn is not None:
                            fn(eng).then_inc(sem, inc)
                    for s, v in totals.values():
                        eng.wait_ge(s, v)
                return run
            block.tensor(runner("pe"))
            block.scalar(runner("act"))
            block.vector(runner("dve"))
            block.gpsimd(runner("pool"))
            block.sync(runner("sp"))

    def mm(self, out, lhsT, rhs, start=True, stop=True, r=(), w=()):
        self.op("pe", lambda e: e.matmul(out, lhsT, rhs, start=start, stop=stop), r, w)

    def tr(self, out, in_, ident, r=(), w=()):
        self.op("pe", lambda e: e.transpose(out, in_, ident), r, w)

    def act(self, out, in_, func, bias=None, scale=None, r=(), w=()):
        kw = {}
        if bias is not None:
            kw["bias"] = bias
        if scale is not None:
            kw["scale"] = scale
        self.op("act", lambda e: e.activation(out, in_, func, **kw), r, w)

    def tt(self, eng, out, in0, in1, op, r=(), w=()):
        self.op(eng, lambda e: e.tensor_tensor(out, in0, in1, op), r, w)

    def ts(self, eng, out, in0, s1, op0, s2=None, op1=None, r=(), w=()):
        if op1 is None:
            self.op(eng, lambda e: e.tensor_scalar(out, in0, s1, None, op0), r, w)
        else:
            self.op(eng, lambda e: e.tensor_scalar(out, in0, s1, s2, op0, op1), r, w)

    def stt(self, out, in0, scalar, in1, op0, op1, r=(), w=()):
        self.op("dve", lambda e: e.scalar_tensor_tensor(out, in0, scalar, in1, op0, op1), r, w)

    def red(self, out, in_, r=(), w=(), op=ALU.add):
        self.op("dve", lambda e: e.tensor_reduce(out, in_, AX.X, op), r, w)

    def cp(self, eng, out, in_, r=(), w=()):
        if eng == "act":
            self.op("act", lambda e: e.activation(out, in_, AF.Copy), r, w)
        else:
            self.op(eng, lambda e: e.tensor_copy(out, in_), r, w)

    def memset(self, eng, ap, val, w=()):
        self.op(eng, lambda e: e.memset(ap, val), (), w)


def build_nc(NT):
    nc = bass.Bass("TRN2", target_bir_lowering=False)
    NP = NT + 1
    NR = NT + 2

    def din(name, shape):
        return nc.dram_tensor(name, list(shape), F32, kind="ExternalInput").ap()

    def dout(name, shape):
        return nc.dram_tensor(name, list(shape), F32, kind="ExternalOutput").ap()

    xp = din("xp", [NP * 128, D])
    xs = din("xs", [128, D])
    cmask = din("cmask", [128, 5, 128])
    ebias = din("ebias", [128, 4])
    rope = din("rope", [128, NR, 16])
    flags = din("flags", [128, 4])
    gains = din("gains", [128, 3, 8])
    vecA = din("vecA", [1, 4352])
    vecB = din("vecB", [1, 1416])
    s_state = din("s_state", [128, 4096])
    s_shift = din("s_shift", [16, 1792])
    s_swak = din("s_swak", [16, 128, 128])
    s_swav = din("s_swav", [16, 128, 128])
    selt = din("selt", [16, 128])
    sel2 = din("sel2", [128, 16])
    s_memk = din("s_memk", [16, 256, 512])
    s_memv = din("s_memv", [16, 256, 512])
    memp = din("memp", [256, D])
    w_in = din("w_in", [D, 6144])
    w2a = din("w2a", [128, 512])
    g2 = din("g2", [128, 512])
    w_mkv = din("w_mkv", [D, 1024])
    w_br = din("w_br", [1536, D])
    w_out = din("w_out", [D, D])
    w_up = din("w_up", [D, 4096])
    w_down = din("w_down", [4096, D])

    y_o = dout("y", [NT * 128, D])
    ys_o = dout("ys", [16, D])
    stp_o = dout("stp", [8, 64, 64])
    zlast_o = dout("zlast", [1, 1792])
    swakp_o = dout("swakp", [128, 128])
    swavp_o = dout("swavp", [128, 128])
    memk_o = dout("memk", [256, 512])
    memv_o = dout("memv", [256, 512])
    sts_o = dout("sts", [128, 4096])
    shifts_o = dout("shifts", [16, 1792])
    swaks_o = dout("swaks", [16, 128, 128])
    swavs_o = dout("swavs", [16, 128, 128])

    sc_yp = nc.dram_tensor("sc_yp", [NT + 1, 128, 512], F32).ap()
    sc_bv = nc.dram_tensor("sc_bv", [NT + 1, 128, 512], F32).ap()
    sc_g = nc.dram_tensor("sc_g", [NT + 1, 128, 512], F32).ap()
    sc_gt = nc.dram_tensor("sc_gt", [NT, 64, 1024], BF16).ap()
    sc_xm = nc.dram_tensor("sc_xm", [NT + 1, 128, D], F32).ap()
    sc_s1 = nc.dram_tensor("sc_s1", [6, 16, 512], F32).ap()
    sc_s2 = nc.dram_tensor("sc_s2", [16, 512], F32).ap()
    sc_q = nc.dram_tensor("sc_q", [16, 1024], F32).ap()
    cc_src = nc.dram_tensor("cc_src", [64, 1024], F32)
    cc_dst = nc.dram_tensor("cc_dst", [4 * 64, 1024], F32)

    es = ExitStack()
    with es:
        S = Sched(nc, es)

        uid = [0]

        def sb(es_, name, shape, dt=F32):
            uid[0] += 1
            return Tile(es_.enter_context(nc.sbuf_tensor("t%d_%s" % (uid[0], name), list(shape), dt)))

        def pst(es_, name, shape, dt=F32):
            return Tile(es_.enter_context(nc.psum_tensor("p_" + name, list(shape), dt)))

        PF = [pst(es, "pf%d" % i, [128, 512], F32) for i in range(6)]
        PB = [pst(es, "pb%d" % i, [128, 1024], BF16) for i in range(2)]
        pfi = [0]
        pbi = [0]

        def psf():
            t = PF[pfi[0] % 5]
            pfi[0] += 1
            return t

        def psb():
            t = PB[pbi[0] % 2]
            pbi[0] += 1
            return t

        MASK = sb(es, "mask", [128, 5, 128])
        EB = sb(es, "eb", [128, 4])
        ROPE = sb(es, "rope", [128, NR, 16])
        FLG = sb(es, "flg", [128, 4])
        GAIN = sb(es, "gain", [128, 3, 8])
        IDB = sb(es, "idb", [128, 128], BF16)
        NEGH = sb(es, "negh", [128, 16])
        ONES2 = sb(es, "ones2", [128, 2])
        ONESB = sb(es, "onesb", [128, 128], BF16)
        S.dma("sp", MASK[:], cmask, w=[MASK])
        S.dma("sp", EB[:], ebias, w=[EB])
        S.dma("sp", ROPE[:], rope, w=[ROPE])
        S.dma("sp", FLG[:], flags, w=[FLG])
        S.dma("sp", GAIN[:], gains, w=[GAIN])
        S.cp("dve", IDB[:], MASK[:, 4, :], r=[MASK], w=[IDB])
        S.memset("dve", NEGH[:], -0.5, w=[NEGH])
        S.memset("dve", ONES2[:], 1.0, w=[ONES2])
        ONESF = sb(es, "onesf", [128, 64])
        CB128 = sb(es, "cb128", [128, 1])
        CBH = sb(es, "cbh", [128, 1])
        S.memset("dve", CBH[:], -C0H, w=[CBH])
        S.memset("dve", CB128[:], -C0H * 128.0, w=[CB128])
        S.memset("dve", ONESF[:], 1.0, w=[ONESF])
        S.memset("dve", ONESB[:], 1.0, w=[ONESB])
        UPS, UPI, LOS, LOI, IDF = (MASK[:, i, :] for i in range(5))
        SINB = sb(es, "sinb", [64, 8, 64], BF16)

        def rstd_of(x_ap, n, gs, eps, scr, ss, rs, r_toks):
            S.act(scr[:, 0:n * gs], x_ap, AF.Square, r=r_toks, w=[scr])
            S.red(ss[:, 0:n], scr[:, 0:n * gs].rearrange("p (a b) -> p a b", b=gs), r=[scr], w=[ss])
            S.ts("dve", ss[:, 0:n], ss[:, 0:n], 1.0 / gs, ALU.mult, eps, ALU.add, r=[ss], w=[ss])
            S.tt("pool", rs[:, 0:n], ss[:, 0:n], NEGH[:, 0:n], ALU.pow, r=[ss, NEGH], w=[rs])

        def norm_T(xt, gi, hT, scr, hb, ss, rs):
            rstd_of(xt[:, :], 1, D, EPS, scr, ss, rs, [xt])
            S.act(hb[:, :], xt[:, :], AF.Copy, scale=rs[:, 0:1], r=[xt, rs], w=[hb])
            pb = psb()
            for kt in range(8):
                S.tr(pb[:, kt * 128:(kt + 1) * 128], hb[:, kt * 128:(kt + 1) * 128], IDB[:], r=[hb, IDB], w=[pb])
            S.tt("dve", hT[:, :, :], pb[:, :].rearrange("p (a b) -> p a b", b=128),
                 GAIN[:, gi, :].unsqueeze(2).to_broadcast([128, 8, 128]), ALU.mult, r=[pb, GAIN], w=[hT])

        es1 = ExitStack()
        with es1:
            WZ = sb(es1, "wz", [128, 8, 1792], BF16)
            W2A = sb(es1, "w2a", [128, 512], BF16)
            G2 = sb(es1, "g2", [128, 512], BF16)
            VA = sb(es1, "va", [128, 4352])
            es1w = ExitStack()
            es1w.__enter__()
            STG = [sb(es1w, "stg%d" % i, [128, 1792]) for i in range(4)]
            stg_i = [0]

            def load_cast(dst_ap, src_ap, ncol, wtok):
                st = STG[stg_i[0] % 4]
                ce = ("pool", "act", "dve")[stg_i[0] % 3]
                stg_i[0] += 1
                S.dma("sp", st[:, 0:ncol], src_ap, w=[st])
                S.cp(ce, dst_ap, st[:, 0:ncol], r=[st], w=[wtok])

            for kt in range(8):
                load_cast(WZ[:, kt, :], w_in[kt * 128:(kt + 1) * 128, 0:1792], 1792, WZ)
            load_cast(W2A[:, :], w2a, 512, W2A)
            load_cast(G2[:, :], g2, 512, G2)
            S.barrier()
            es1w.__exit__(None, None, None)
            S.dma("sp", VA[:], vecA.partition_broadcast(128).rearrange("p a n -> p (a n)"), w=[VA])
            MU = VA[:, 0:1792]
            W0 = VA[:, 1792:2304]
            A0 = VA[:, 2304:2816]
            K_K = VA[:, 2816:3328]
            K_A = VA[:, 3328:3840]
            R_K = VA[:, 3840:4352]

            XT = [sb(es1, "xt%d" % i, [128, D]) for i in range(2)]
            SCR = sb(es1, "scr", [128, D])
            HB = sb(es1, "hb", [128, D], BF16)
            SS = sb(es1, "ss", [128, 8])
            RS = sb(es1, "rs", [128, 8])
            HT = [sb(es1, "ht%d" % i, [128, 8, 128], BF16) for i in range(2)]
            Z = [sb(es1, "z%d" % i, [128, 1792]) for i in range(2)]
            ZP = sb(es1, "zp", [128, 1792])
            ZS = sb(es1, "zs", [128, 1792])
            LC = sb(es1, "lc", [128, 256], BF16)
            LCT = sb(es1, "lct", [128, 256], BF16)
            TW = sb(es1, "tw", [128, 512])
            AA = sb(es1, "aa", [128, 512])
            GO = sb(es1, "go", [128, 512])
            KK = sb(es1, "kk", [128, 512])
            KH = sb(es1, "kh", [128, 512])
            BB = sb(es1, "bb", [128, 512])
            T1 = sb(es1, "t1", [128, 512])
            T2 = sb(es1, "t2", [128, 512])
            BV = sb(es1, "bv", [128, 512])
            SS8 = sb(es1, "ss8", [128, 8])
            RN8 = sb(es1, "rn8", [128, 8])
            BS8 = sb(es1, "bs8", [128, 8])

            def rwkv_pre(zc, zp_ready_toks):
                S.tt("pool", ZS[:, :], ZP[:, :], zc[:, :], ALU.subtract, r=[ZP, zc], w=[ZS])
                S.tt("dve", ZS[:, :], ZS[:, :], MU, ALU.mult, r=[ZS, VA], w=[ZS])
                S.tt("pool", ZS[:, :], ZS[:, :], zc[:, :], ALU.add, r=[ZS, zc], w=[ZS])
                r_ = ZS[:, 0:512]
                k_ = ZS[:, 512:1024]
                S.act(LC[:, 0:64], ZS[:, 1536:1600], AF.Tanh, r=[ZS], w=[LC])
                S.act(LC[:, 64:128], ZS[:, 1600:1664], AF.Copy, r=[ZS], w=[LC])
                S.act(LC[:, 128:256], ZS[:, 1664:1792], AF.Tanh, scale=0.5, r=[ZS], w=[LC])
                S.ts("dve", LC[:, 128:256], LC[:, 128:256], 0.5, ALU.mult, 0.5, ALU.add, r=[LC], w=[LC])
                pb = psb()
                S.tr(pb[:, 0:128], LC[:, 0:128], IDB[:], r=[LC, IDB], w=[pb])
                S.tr(pb[:, 128:256], LC[:, 128:256], IDB[:], r=[LC, IDB], w=[pb])
                S.cp("act", LCT[:, :], pb[:, 0:256], r=[pb], w=[LCT])
                yield
                pw, pa, pg = psf(), psf(), psf()
                S.mm(pw[:, :], LCT[0:64, 0:128], W2A[0:64, :], r=[LCT, W2A], w=[pw])
                S.mm(pa[:, :], LCT[64:128, 0:128], W2A[64:128, :], r=[LCT, W2A], w=[pa])
                S.mm(pg[:, :], LCT[:, 128:256], G2[:, :], r=[LCT, G2], w=[pg])
                S.tt("dve", TW[:, :], pw[:, :], W0, ALU.add, r=[pw, VA], w=[TW])
                S.act(TW[:, :], TW[:, :], AF.Tanh, scale=0.5, r=[TW], w=[TW])
                S.tt("dve", AA[:, :], pa[:, :], A0, ALU.add, r=[pa, VA], w=[AA])
                S.act(AA[:, :], AA[:, :], AF.Tanh, scale=0.5, r=[AA], w=[AA])
                S.ts("dve", AA[:, :], AA[:, :], 0.5, ALU.mult, 0.5, ALU.add, r=[AA], w=[AA])
                S.cp("act", GO[:, :], pg[:, :], r=[pg], w=[GO])
                yield
                S.tt("dve", KK[:, :], k_, K_K, ALU.mult, r=[ZS, VA], w=[KK])
                S.act(T1[:, :], KK[:, :], AF.Square, r=[KK], w=[T1])
                S.red(SS8[:, :], T1[:, :].rearrange("p (a b) -> p a b", b=64), r=[T1], w=[SS8])
                S.ts("dve", SS8[:, :], SS8[:, :], 1e-24, ALU.max, r=[SS8], w=[SS8])
                S.tt("pool", RN8[:, :], SS8[:, :], NEGH[:, 0:8], ALU.pow, r=[SS8, NEGH], w=[RN8])
                S.tt("dve", KK[:, :].rearrange("p (a b) -> p a b", b=64), KK[:, :].rearrange("p (a b) -> p a b", b=64),
                     RN8[:, :].unsqueeze(2).to_broadcast([128, 8, 64]), ALU.mult, r=[KK, RN8], w=[KK])
                S.stt(T1[:, :], AA[:, :], -1.0, K_A, ALU.add, ALU.mult, r=[AA, VA], w=[T1])
                S.stt(KH[:, :], T1[:, :], 1.0, k_, ALU.add, ALU.mult, r=[T1, ZS], w=[KH])
                yield
                S.tt("pool", BB[:, :], KK[:, :], AA[:, :], ALU.mult, r=[KK, AA], w=[BB])
                S.tt("pool", T2[:, :], r_, KH[:, :], ALU.mult, r=[ZS, KH], w=[T2])
                S.tt("pool", T2[:, :], T2[:, :], R_K, ALU.mult, r=[T2, VA], w=[T2])
                S.red(BS8[:, :], T2[:, :].rearrange("p (a b) -> p a b", b=64), r=[T2], w=[BS8])
                S.tt("dve", BV[:, :].rearrange("p (a b) -> p a b", b=64), ZS[:, 1024:1536].rearrange("p (a b) -> p a b", b=64),
                     BS8[:, :].unsqueeze(2).to_broadcast([128, 8, 64]), ALU.mult, r=[ZS, BS8], w=[BV])

            def zproj(xsrc_ap, zc, ht, xt):
                S.dma("sp", xt[:, :], xsrc_ap, w=[xt])
                norm_T(xt, 0, ht, SCR, HB, SS, RS)
                for c, (lo, hi) in enumerate(((0, 512), (512, 1024), (1024, 1536), (1536, 1792))):
                    p = psf()
                    for kt in range(8):
                        S.mm(p[:, 0:hi - lo], ht[:, kt, :], WZ[:, kt, lo:hi], start=(kt == 0), stop=(kt == 7),
                             r=[ht, WZ], w=[p])
                    S.cp("act", zc[:, lo:hi], p[:, 0:hi - lo], r=[p], w=[zc])
                    yield

            es1p = ExitStack()
            with es1p:
                SP = sb(es1p, "sp", [64, 8, 128])
                SPB = sb(es1p, "spb", [64, 8, 128], BF16)
                es1q = ExitStack()
                es1q.__enter__()
                M4 = sb(es1q, "m4", [128, 4, 128])
                S.cp("dve", M4[:, 0:2, :], MASK[:, 0:1, :].to_broadcast([128, 2, 128]), r=[MASK], w=[M4])
                S.cp("dve", M4[:, 2:4, :], MASK[:, 1:2, :].to_broadcast([128, 2, 128]), r=[MASK], w=[M4])
                GI = sb(es1q, "gi", [128, 512])
                GV = sb(es1q, "gv", [128, 512])
                GX = sb(es1q, "gx", [128, 512])
                GE = sb(es1q, "ge", [128, 512])
                TM = sb(es1q, "tm", [128, 7, 512], BF16)
                FT = sb(es1q, "ft", [128, 4, 4, 128], BF16)
                GC = sb(es1q, "gc", [64, 512])
                HW = [sb(es1q, "hw%d" % i, [128, 5, 128], BF16) for i in range(8)]
                IW = [[sb(es1q, "iw%d_%d" % (i, j), [128, 3, 128], BF16) for j in range(2)] for i in range(8)]
                XF = [sb(es1q, "xf%d" % i, [128, 128], BF16) for i in range(8)]
                RH = [sb(es1q, "rh%d" % i, [128, 128], BF16) for i in range(8)]
                WU = [sb(es1q, "wu%d" % i, [128, 128], BF16) for i in range(8)]
                PTs = [sb(es1q, "pt%d" % i, [64, 64]) for i in range(8)]
                HS = [sb(es1q, "hs%d" % i, [64, 64]) for i in range(8)]
                QT = [sb(es1q, "qt%d" % i, [64, 128], BF16) for i in range(8)]
                GTS = [sb(es1q, "gts%d" % i, [64, 8, 128], BF16) for i in range(2)]
                YP = [sb(es1q, "yp%d" % i, [128, 512]) for i in range(2)]

                S.memset("dve", SP[:, :, 0:64], 0.0, w=[SP])
                for h in range(8):
                    S.cp("dve", SP[:, h, 64:128], MASK[0:64, 4, 0:64], r=[MASK], w=[SP])
                S.cp("act", SPB[:, :, :], SP[:, :, :], r=[SP], w=[SPB])

                TMs = [TM, sb(es1q, "tm_b", [128, 7, 512], BF16)]
                FTs = [FT, sb(es1q, "ft_b", [128, 4, 4, 128], BF16)]
                GCs = [GC, sb(es1q, "gc_b", [64, 512])]

                def pre_gen(j):
                    TM, FT, GC = TMs[j % 2], FTs[j % 2], GCs[j % 2]
                    zc = Z[j % 2]
                    yield from zproj(xp[j * 128:(j + 1) * 128, :], zc, HT[j % 2], XT[j % 2])
                    if j == 0:
                        return
                    zprev = Z[(j - 1) % 2]
                    S.dma("sp", ZP[1:128, :], zc[0:127, :], r=[zc], w=[ZP])
                    S.dma("sp", ZP[0:1, :], zprev[127:128, :], r=[zprev], w=[ZP])
                    if j == NT:
                        S.dma("sp", zlast_o, zc[127:128, :], r=[zc])
                    yield from rwkv_pre(zc, None)
                    jj = j - 1
                    S.dma("sp", sc_bv[jj], BV[:, :], r=[BV])
                    S.dma("sp", sc_g[jj], GO[:, :], r=[GO])
                    p_i, p_s, p_r = psf(), psf(), psf()
                    S.mm(p_i[:, :], UPI, TW[:, :], r=[MASK, TW], w=[p_i])
                    S.mm(p_s[:, :], UPS, TW[:, :], r=[MASK, TW], w=[p_s])
                    S.mm(p_r[:, :], LOS, TW[:, :], r=[MASK, TW], w=[p_r])
                    S.act(GI[:, :], p_i[:, :], AF.Exp, bias=EB[:, 0:1], scale=-C0H, r=[p_i, EB], w=[GI])
                    S.act(GV[:, :], p_i[:, :], AF.Exp, bias=EB[:, 2:3], scale=C0H, r=[p_i, EB], w=[GV])
                    S.act(GX[:, :], p_s[:, :], AF.Exp, bias=EB[:, 1:2], scale=-C0H, r=[p_s, EB], w=[GX])
                    S.act(GE[:, :], p_r[:, :], AF.Exp, bias=EB[:, 3:4], scale=-C0H, r=[p_r, EB], w=[GE])
                    yield
                    pc = psf()
                    for h in range(8):
                        S.mm(pc[0:64, h * 64:(h + 1) * 64], TW[:, h * 64:(h + 1) * 64], ONESF[:, :], r=[TW, ONESF], w=[pc])
                    S.act(GC[:, :], pc[0:64, :], AF.Exp, bias=CB128[0:64, 0:1], scale=-C0H, r=[pc, CB128], w=[GC])
                    S.tt("dve", TM[:, 0, :], ZS[:, 0:512], GI[:, :], ALU.mult, r=[ZS, GI], w=[TM.sub(0)])
                    S.stt(TM[:, 1, :], KK[:, :], -1.0, GX[:, :], ALU.mult, ALU.mult, r=[KK, GX], w=[TM.sub(1)])
                    S.tt("dve", TM[:, 2, :], BB[:, :], GV[:, :], ALU.mult, r=[BB, GV], w=[TM.sub(2)])
                    S.tt("pool", TM[:, 3, :], KH[:, :], GV[:, :], ALU.mult, r=[KH, GV], w=[TM.sub(3)])
                    S.tt("pool", TM[:, 4, :], BB[:, :], GE[:, :], ALU.mult, r=[BB, GE], w=[TM.sub(4)])
                    S.tt("pool", TM[:, 5, :], KH[:, :], GE[:, :], ALU.mult, r=[KH, GE], w=[TM.sub(5)])
                    S.cp("act", TM[:, 6, :], ZS[:, 1024:1536], r=[ZS], w=[TM.sub(6)])
                    yield
                    srcslot = (1, 0, 2, 3)
                    for half in range(2):
                        pb = psb()
                        for hpp in range(2):
                            hp = half * 2 + hpp
                            for sl in range(4):
                                o = (hpp * 4 + sl) * 128
                                S.tr(pb[:, o:o + 128], TM[:, srcslot[sl], hp * 128:(hp + 1) * 128], IDB[:],
                                     r=[TM.sub(srcslot[sl]), IDB], w=[pb])
                        S.cp("act" if half == 0 else "dve",
                             FT[:, half * 2:half * 2 + 2, :, :].rearrange("p a b c -> p (a b c)"), pb[:, :],
                             r=[pb], w=[FT.sub(half)])
                        yield

                def stages(j, step):
                    TM, FT, GC = TMs[j % 2], FTs[j % 2], GCs[j % 2]
                    jj = j - 1
                    ypar = YP[jj % 2]
                    gts = GTS[jj % 2]
                    p_y = PF[5]
                    H8 = range(8)

                    def hv(h):
                        hp, base = h // 2, 64 * (h % 2)
                        return hp, FT.sub(hp // 2), slice(h * 64, (h + 1) * 64), slice(base, base + 64)

                    for h in H8:
                        hp, fs, hsl, ps_ = hv(h)
                        hw = HW[h]
                        aT, rT, bT, kT = (FT[ps_, hp, i, :] for i in range(4))
                        pA = psf()
                        S.mm(pA[:, 0:128], bT, aT, r=[fs], w=[pA])
                        S.mm(pA[:, 128:256], kT, aT, r=[fs], w=[pA])
                        S.mm(pA[:, 256:384], bT, rT, r=[fs], w=[pA])
                        S.mm(pA[:, 384:512], kT, rT, r=[fs], w=[pA])
                        pN = psf()
                        S.mm(pN[:, 0:128], aT, bT, r=[fs], w=[pN])
                        S.tt("dve", hw[:, 1:5, :], pA[:, :].rearrange("p (a b) -> p a b", b=128), M4[:, :, :], ALU.mult,
                             r=[pA, M4], w=[hw])
                        S.tt("dve", hw[:, 0, :], pN[:, 0:128], LOS, ALU.mult, r=[pN, MASK], w=[hw])
                    step()
                    curs = {}
                    for h in H8:
                        hw = HW[h]
                        pI = psf()
                        S.mm(pI[:, 0:128], hw[:, 1, :], hw[:, 0, :], r=[hw], w=[pI])
                        S.mm(pI[:, 128:256], hw[:, 0, :], hw[:, 1, :], r=[hw], w=[pI])
                        cur = IW[h][0]
                        S.cp("act", cur[:, 0:2, :].rearrange("p a b -> p (a b)"), pI[:, 0:256], r=[pI], w=[cur])
                        S.tt("pool", cur[:, 2, :], hw[:, 1, :], IDF, ALU.add, r=[hw, MASK], w=[cur])
                        curs[h] = cur
                    step()
                    for lev in range(1, 6):
                        step()
                        for h in H8:
                            cur = curs[h]
                            nxt = IW[h][lev % 2]
                            pI = psf()
                            S.mm(pI[:, 0:128], cur[:, 1, :], cur[:, 0, :], r=[cur], w=[pI])
                            S.mm(pI[:, 128:384], cur[:, 0, :], cur[:, 1:3, :].rearrange("p a b -> p (a b)"), r=[cur], w=[pI])
                            S.cp("act", nxt[:, 0:2, :].rearrange("p a b -> p (a b)"), pI[:, 0:256], r=[pI], w=[nxt])
                            S.tt("dve", nxt[:, 2, :], cur[:, 2, :], pI[:, 256:384], ALU.add, r=[cur, pI], w=[nxt])
                            curs[h] = nxt
                    step()
                    for h in H8:
                        cur = curs[h]
                        pI = psf()
                        S.mm(pI[:, 0:128], cur[:, 0, :], cur[:, 2, :], r=[cur], w=[pI])
                        S.tt("dve", XF[h][:, :], cur[:, 2, :], pI[:, 0:128], ALU.add, r=[cur, pI], w=[XF[h]])
                    step()
                    for h in H8:
                        hp, fs, hsl, ps_ = hv(h)
                        hw, rh = HW[h], RH[h]
                        pV = psf()
                        S.mm(pV[:, 0:64], hw[:, 2, :], TM[:, 6, hsl], r=[hw, TM.sub(6)], w=[pV])
                        S.cp("pool", rh[:, 0:64], TM[:, 1, hsl], r=[TM.sub(1)], w=[rh])
                        S.cp("act", rh[:, 64:128], pV[:, 0:64], r=[pV], w=[rh])
                    step()
                    for h in H8:
                        pW = psf()
                        S.mm(pW[:, 0:128], XF[h][:, :], RH[h][:, :], r=[XF[h], RH[h]], w=[pW])
                        S.cp("act", WU[h][:, :], pW[:, 0:128], r=[pW], w=[WU[h]])
                    step()
                    for h in H8:
                        hp, fs, hsl, ps_ = hv(h)
                        hw, wu, pts, hs, qt = HW[h], WU[h], PTs[h], HS[h], QT[h]
                        pP = psf()
                        S.mm(pP[0:64, 0:64], wu[:, 0:64], TM[:, 4, hsl], r=[wu, TM.sub(4)], w=[pP])
                        S.mm(pP[0:64, 64:128], TM[:, 4, hsl], wu[:, 64:128], start=True, stop=False, r=[wu, TM.sub(4)], w=[pP])
                        S.mm(pP[0:64, 64:128], TM[:, 5, hsl], TM[:, 6, hsl], start=False, stop=True,
                             r=[TM.sub(5), TM.sub(6)], w=[pP])
                        S.mm(pP[0:64, 128:256], wu[:, 0:64], hw[:, 3, :], start=True, stop=False, r=[wu, hw], w=[pP])
                        S.mm(pP[0:64, 128:256], TM[:, 0, hsl], IDB[:, :], start=False, stop=True, r=[TM.sub(0), IDB], w=[pP])
                        S.stt(pts[:, :], MASK[0:64, 4, 0:64], GC[:, h * 64:h * 64 + 1], pP[0:64, 0:64], ALU.mult, ALU.add,
                              r=[MASK, GC, pP], w=[pts])
                        S.cp("dve", hs[:, :], pP[0:64, 64:128], r=[pP], w=[hs])
                        S.cp("dve", qt[:, :], pP[0:64, 128:256], r=[pP], w=[qt])
                    step()
                    for h in H8:
                        hp, fs, hsl, ps_ = hv(h)
                        hw, wu, qt = HW[h], WU[h], QT[h]
                        S.mm(p_y[:, hsl], hw[:, 3, :], wu[:, 64:128], start=True, stop=False, r=[hw, wu], w=[p_y])
                        S.mm(p_y[:, hsl], hw[:, 4, :], TM[:, 6, hsl], start=False, stop=False, r=[hw, TM.sub(6)], w=[p_y])
                        S.mm(p_y[:, hsl], qt[:, :], SPB[:, h, 0:64], start=False, stop=True, r=[qt, SPB.sub(h)], w=[p_y])
                    step()
                    for h in H8:
                        qt = QT[h]
                        pG = psf()
                        S.mm(pG[0:64, 0:128], SPB[:, h, 64:128], qt[:, :], r=[SPB.sub(h), qt], w=[pG])
                        S.cp("act", gts[:, h, :], pG[0:64, 0:128], r=[pG], w=[gts])
                    step()
                    for h in H8:
                        pts, hs = PTs[h], HS[h]
                        pS = psf()
                        S.mm(pS[0:64, 0:128], pts[:, :], SP[:, h, :], r=[pts, SP.sub(h)], w=[pS])
                        S.tt("dve", SP[:, h, 0:64], pS[0:64, 0:64], hs[:, :], ALU.add, r=[pS, hs], w=[SP.sub(h)])
                        S.cp("act", SP[:, h, 64:128], pS[0:64, 64:128], r=[pS], w=[SP.sub(h)])
                        S.cp("pool", SPB[:, h, :], SP[:, h, :], r=[SP.sub(h)], w=[SPB.sub(h)])
                    S.cp("act", ypar[:, :], p_y[:, :], r=[p_y], w=[ypar])
                    S.dma("sp", sc_yp[jj], ypar[:, :], r=[ypar])
                    S.dma("sp", sc_gt[jj], gts[:, :, :].rearrange("p a b -> p (a b)"), r=[gts])


                for _ in pre_gen(0):
                    pass
                for _ in pre_gen(1):
                    pass
                for j in range(1, NP):
                    g = pre_gen(j + 1) if j + 1 < NP else iter(())
                    stages(j, lambda g=g: next(g, None))
                    for _ in g:
                        pass
                S.barrier()
                es1q.__exit__(None, None, None)
                EX = sb(es1p, "ex", [64, 8, 128])
                EXA = sb(es1p, "exa", [64, 4, 8, 128])
                SIN = sb(es1p, "sin", [64, 8, 64])
                SC1 = sb(es1p, "sc1", [64, 8, 64])
                SFT = sb(es1p, "sft", [64, 8, 64])
                hall = [SP.sub(h) for h in range(8)]
                S.cp("dve", EX[:, :, 0:64], SP[:, :, 0:64], r=hall, w=[EX])
                pT = psf()
                for h in range(8):
                    S.tr(pT[0:64, h * 64:(h + 1) * 64], SP[:, h, 64:128], MASK[0:64, 4, 0:64], r=hall + [MASK], w=[pT])
                S.cp("dve", EX[:, :, 64:128], pT[0:64, :].rearrange("p (a b) -> p a b", b=64), r=[pT], w=[EX])
                S.dma("pool", cc_src.ap(), EX[:, :, :].rearrange("p a b -> p (a b)"), r=[EX], w=[EXA.sub("src")])
                deps = S._deps([EXA.sub("src")], [EXA.sub("dst")])
                waits = S._resolve("pool", deps)
                S.cnt["pool"] += 1
                me = (("e", "pool"), S.cnt["pool"])
                S._mark(me, [EXA.sub("src")], [EXA.sub("dst")])
                S.streams["pool"].append((waits, lambda e: e.collective_compute(
                    "AllGather", ALU.bypass, replica_groups=[[0, 1, 2, 3], [4, 5, 6, 7]],
                    ins=[cc_src.ap()], outs=[cc_dst.ap()]), S.esem["pool"], 1))
                S.gseq += 1
                S.gidx["pool"].append(S.gseq)
                S.dma("pool", EXA[:, :, :, :].rearrange("p r a b -> p r (a b)"),
                      cc_dst.ap().rearrange("(r p) n -> p r n", p=64), r=[EXA.sub("dst")], w=[EXA])
                S.memset("dve", SIN[:, :, :], 0.0, w=[SIN])
                for q in range(3):
                    pc_ = psf()
                    for h in range(8):
                        S.mm(pc_[0:64, h * 64:(h + 1) * 64], EXA[:, q, h, 64:128], SIN[:, h, :], r=[EXA, SIN], w=[pc_])
                    S.tt("dve", SC1[:, :, :], pc_[0:64, :].rearrange("p (a b) -> p a b", b=64), EXA[:, q, :, 0:64], ALU.add,
                         r=[pc_, EXA], w=[SC1])
                    S.tt("dve", SC1[:, :, :], SC1[:, :, :], SIN[:, :, :], ALU.subtract, r=[SC1, SIN], w=[SC1])
                    S.stt(SIN[:, :, :].rearrange("p a b -> p (a b)"), SC1[:, :, :].rearrange("p a b -> p (a b)"),
                          FLG[0:64, 1 + q:2 + q], SIN[:, :, :].rearrange("p a b -> p (a b)"), ALU.mult, ALU.add,
                          r=[SC1, SIN, FLG], w=[SIN])
                pf_ = psf()
                for h in range(8):
                    S.mm(pf_[0:64, h * 64:(h + 1) * 64], EX[:, h, 64:128], SIN[:, h, :], r=[EX, SIN], w=[pf_])
                S.tt("dve", SFT[:, :, :], pf_[0:64, :].rearrange("p (a b) -> p a b", b=64), EX[:, :, 0:64], ALU.add,
                     r=[pf_, EX], w=[SFT])
                pf2 = psf()
                for h in range(8):
                    S.tr(pf2[0:64, h * 64:(h + 1) * 64], SFT[:, h, :], MASK[0:64, 4, 0:64], r=[SFT, MASK], w=[pf2])
                S.cp("dve", SC1[:, :, :], pf2[0:64, :].rearrange("p (a b) -> p a b", b=64), r=[pf2], w=[SC1])
                S.dma("sp", stp_o.rearrange("h v k -> v h k"), SC1[:, :, :], r=[SC1])
                S.cp("act", SINB[:, :, :], SIN[:, :, :], r=[SIN], w=[SINB])
            S.barrier()
            es1s = ExitStack()
            with es1s:
                SX = sb(es1s, "sx", [128, 6, 512])
                R6 = sb(es1s, "r6", [128, 6, 64])
                ST = sb(es1s, "st", [128, 64, 64])
                TP = sb(es1s, "tp", [128, 64, 64])
                SA = sb(es1s, "sa", [128, 64])
                YV = sb(es1s, "yv", [128, 64])
                YS = sb(es1s, "ysr", [128, 512])
                WD = sb(es1s, "wd", [128, 512])
                zc = Z[0]
                S.dma("sp", ST[:, :, :].rearrange("p a b -> p (a b)"), s_state, w=[ST])
                for _ in zproj(xs, zc, HT[0], XT[0]):
                    pass
                S.dma("sp", shifts_o, zc[0:16, :], r=[zc])
                S.memset("dve", ZP[:, :], 0.0, w=[ZP])
                S.dma("sp", ZP[0:16, :], s_shift, w=[ZP])
                for _ in rwkv_pre(zc, None):
                    pass
                S.dma("sp", sc_bv[NT], BV[:, :], r=[BV])
                S.dma("sp", sc_g[NT], GO[:, :], r=[GO])
                S.act(WD[:, :], TW[:, :], AF.Exp, bias=CBH[:, 0:1], scale=-C0H, r=[TW, CBH], w=[WD])
                for i, src in enumerate((ZS[:, 0:512], WD[:, :], KH[:, :], ZS[:, 1024:1536], KK[:, :], BB[:, :])):
                    S.cp("dve" if i % 2 == 0 else "pool", SX[:, i, :], src, r=[ZS, WD, KH, KK, BB], w=[SX])
                S.dma("sp", sc_s1.rearrange("i b n -> b i n"), SX[0:16, :, :], r=[SX], w=[R6.sub("d")])
                S.dma("sp", R6[:, :, :], sc_s1.rearrange("i b (h d) -> (b h) i d", d=64), r=[R6.sub("d")], w=[R6])
                r_, w_, k_, v_, kk_, b_ = (R6[:, i, :] for i in range(6))

                def bv(ap):
                    return ap.unsqueeze(1).to_broadcast([128, 64, 64])

                def bk(ap):
                    return ap.unsqueeze(2).to_broadcast([128, 64, 64])

                S.tt("dve", TP[:, :, :], ST[:, :, :], bv(kk_), ALU.mult, r=[ST, R6], w=[TP])
                S.red(SA[:, :], TP[:, :, :], r=[TP], w=[SA])
                S.tt("pool", ST[:, :, :], ST[:, :, :], bv(w_), ALU.mult, r=[ST, R6, TP], w=[ST])
                S.tt("dve", TP[:, :, :], bk(SA[:, :]), bv(b_), ALU.mult, r=[SA, R6], w=[TP])
                S.tt("dve", ST[:, :, :], ST[:, :, :], TP[:, :, :], ALU.subtract, r=[ST, TP], w=[ST])
                S.tt("pool", TP[:, :, :], bk(v_), bv(k_), ALU.mult, r=[R6, ST], w=[TP])
                S.tt("dve", ST[:, :, :], ST[:, :, :], TP[:, :, :], ALU.add, r=[ST, TP], w=[ST])
                S.dma("sp", sts_o, ST[:, :, :].rearrange("p a b -> p (a b)"), r=[ST])
                S.tt("dve", TP[:, :, :], ST[:, :, :], bv(r_), ALU.mult, r=[ST, R6], w=[TP])
                S.red(YV[:, :], TP[:, :, :], r=[TP], w=[YV])
                S.dma("sp", sc_s2.rearrange("b (h d) -> (b h) d", d=64), YV[:, :], r=[YV], w=[YS.sub("d")])
                S.memset("dve", YS[:, :], 0.0, w=[YS])
                S.dma("sp", YS[0:16, :], sc_s2, r=[YS.sub("d")], w=[YS])
                S.dma("sp", sc_yp[NT], YS[:, :], r=[YS])
        S.barrier()

        MKT = sb(es, "mkt", [128, 4, 256], BF16)
        MVB = sb(es, "mvb", [128, 2, 512], BF16)
        VB = sb(es, "vb", [128, 1416])
        S.dma("sp", VB[:], vecB.partition_broadcast(128).rearrange("p a n -> p (a n)"), w=[VB])
        LN_W, LN_B = VB[:, 0:512], VB[:, 512:1024]
        XQG, XKG = VB[:, 1152:1280], VB[:, 1280:1408]
        es0 = ExitStack()
        with es0:
            WM = sb(es0, "wm", [128, 8, 1024], BF16)
            STG0 = [sb(es0, "stg0_%d" % i, [128, 1024]) for i in range(2)]
            for kt in range(8):
                st = STG0[kt % 2]
                S.dma("sp", st[:, :], w_mkv[kt * 128:(kt + 1) * 128, :], w=[st])
                S.cp(("pool", "act", "dve")[kt % 3], WM[:, kt, :], st[:, :], r=[st], w=[WM])
            XM0 = sb(es0, "xm0", [128, D])
            SCR0 = sb(es0, "scr0", [128, D])
            HB0 = sb(es0, "hb0", [128, D], BF16)
            SS0 = sb(es0, "ss0", [128, 8])
            RS0 = sb(es0, "rs0", [128, 8])
            HT0 = sb(es0, "ht0", [128, 8, 128], BF16)
            MKF = sb(es0, "mkf", [128, 512])
            MVF = sb(es0, "mvf", [128, 512])
            MKB = sb(es0, "mkb", [128, 512], BF16)
            for mt in range(2):
                S.dma("sp", XM0[:, :], memp[mt * 128:(mt + 1) * 128, :], w=[XM0])
                norm_T(XM0, 2, HT0, SCR0, HB0, SS0, RS0)
                pk, pv_ = psf(), psf()
                for kt in range(8):
                    S.mm(pk[:, :], HT0[:, kt, :], WM[:, kt, 0:512], start=(kt == 0), stop=(kt == 7), r=[HT0, WM], w=[pk])
                for kt in range(8):
                    S.mm(pv_[:, :], HT0[:, kt, :], WM[:, kt, 512:1024], start=(kt == 0), stop=(kt == 7), r=[HT0, WM], w=[pv_])
                S.cp("act", MVF[:, :], pv_[:, :], r=[pv_], w=[MVF])
                S.dma("sp", memv_o[mt * 128:(mt + 1) * 128, :], MVF[:, :], r=[MVF])
                S.cp("pool", MVB[:, mt, :], MVF[:, :], r=[MVF], w=[MVB])
                rstd_of(pk[:, :], 4, 128, EPS, SCR0, SS0, RS0, [pk])
                S.tt("dve", MKF[:, :].rearrange("p (a b) -> p a b", b=128), pk[:, :].rearrange("p (a b) -> p a b", b=128),
                     RS0[:, 0:4].unsqueeze(2).to_broadcast([128, 4, 128]), ALU.mult, r=[pk, RS0], w=[MKF])
                S.tt("dve", MKF[:, :].rearrange("p (a b) -> p a b", b=128), MKF[:, :].rearrange("p (a b) -> p a b", b=128),
                     XKG.unsqueeze(1).to_broadcast([128, 4, 128]), ALU.mult, r=[MKF, VB], w=[MKF])
                S.dma("sp", memk_o[mt * 128:(mt + 1) * 128, :], MKF[:, :], r=[MKF])
                S.cp("act", MKB[:, :], MKF[:, :], r=[MKF], w=[MKB])
                pb = psb()
                for h in range(4):
                    S.tr(pb[:, h * 128:(h + 1) * 128], MKB[:, h * 128:(h + 1) * 128], IDB[:], r=[MKB, IDB], w=[pb])
                S.cp("act", MKT[:, :, mt * 128:(mt + 1) * 128], pb[:, 0:512].rearrange("p (a b) -> p a b", b=128), r=[pb], w=[MKT])
        S.barrier()

        es2 = ExitStack()
        with es2:
            WR = sb(es2, "wr", [128, 8, 4352], BF16)
            WAC = sb(es2, "wac", [128, 8, 1024], BF16)
            WBB = sb(es2, "wbb", [64, 8, 1024], BF16)
            WO = sb(es2, "wo", [128, 8, 1024], BF16)
            es2w = ExitStack()
            es2w.__enter__()
            STG2 = [sb(es2w, "stg2_%d" % i, [128, 1088]) for i in range(6)]
            sgi = [0]

            def ldc(dst_ap, src_ap, np_, ncol, wtok):
                st = STG2[sgi[0] % 6]
                ce = ("pool", "act", "dve")[sgi[0] % 3]
                sgi[0] += 1
                S.dma("sp", st[0:np_, 0:ncol], src_ap, w=[st])
                S.cp(ce, dst_ap, st[0:np_, 0:ncol], r=[st], w=[wtok])

            for kt in range(8):
                for hf in range(4):
                    ldc(WR[:, kt, hf * 1088:(hf + 1) * 1088], w_in[kt * 128:(kt + 1) * 128, 1792 + hf * 1088:1792 + (hf + 1) * 1088], 128, 1088, WR)
            for i in range(4):
                ldc(WAC[:, i, :], w_br[i * 128:(i + 1) * 128, :], 128, 1024, WAC)
                ldc(WAC[:, 4 + i, :], w_br[1024 + i * 128:1024 + (i + 1) * 128, :], 128, 1024, WAC)
            for h in range(8):
                ldc(WBB[:, h, :], w_br[512 + h * 64:512 + (h + 1) * 64, :], 64, 1024, WBB)
            for kt in range(8):
                ldc(WO[:, kt, :], w_out[kt * 128:(kt + 1) * 128, :], 128, 1024, WO)
            S.barrier()
            es2w.__exit__(None, None, None)

            XT2 = [sb(es2, "xt2", [128, D])] * 2
            HB2 = sb(es2, "hb2", [128, D], BF16)
            SSx = sb(es2, "ssx", [128, 8])
            RSx = sb(es2, "rsx", [128, 8])
            HT2 = sb(es2, "ht2", [128, 8, 128], BF16)
            QKV = sb(es2, "qkv", [128, 1280])
            GTH = sb(es2, "gth", [128, 3072], BF16)
            QKG = sb(es2, "qkg", [128, 10, 64])
            SS10 = sb(es2, "ss10", [128, 16])
            RS10 = sb(es2, "rs10", [128, 16])
            RT = sb(es2, "rt", [128, 4, 10, 8])
            QKB = sb(es2, "qkb", [128, 640], BF16)
            XQB = sb(es2, "xqb", [128, 512], BF16)
            XQT = sb(es2, "xqt", [128, 4, 128], BF16)
            ESK = sb(es2, "esk", [128, 8])
            OBT = sb(es2, "obt", [64, 8, 128], BF16)
            OCT = sb(es2, "oct", [128, 4, 128], BF16)
            OAT = sb(es2, "oat", [128, 4, 128], BF16)
            YPL = sb(es2, "ypl", [128, 512])
            BVL = sb(es2, "bvl", [128, 512])
            GOL = sb(es2, "gol", [128, 512])
            OAB = sb(es2, "oab", [128, 512], BF16)
            ST8 = [sb(es2, "st8_%d" % i, [128, 8]) for i in range(4)]
            MG = sb(es2, "mg", [128, 1024])
            SCR2 = MG
            MT_ = sb(es2, "mt_", [128, 512])
            MGB = sb(es2, "mgb", [128, 1024], BF16)
            MGT = sb(es2, "mgt", [128, 8, 128], BF16)
            XMo = sb(es2, "xmo", [128, D])
            GNE = sb(es2, "gne", [128, 1])
            S.cp("dve", QKG[:, 0:8, :], VB[:, 1024:1088].unsqueeze(1).to_broadcast([128, 8, 64]), r=[VB], w=[QKG])
            S.cp("dve", QKG[:, 8:10, :], VB[:, 1088:1152].unsqueeze(1).to_broadcast([128, 2, 64]), r=[VB], w=[QKG])
            S.act(ESK[:, :], VB[:, 1408:1416], AF.Exp, r=[VB], w=[ESK])

            def inproj_rest(x_src, xt, ropej):
                S.dma("sp", xt[:, :], x_src, w=[xt])
                norm_T(xt, 0, HT2, SCR2, HB2, SSx, RSx)
                for (lo, hi, dlo) in ((0, 512, 0), (512, 768, 512), (768, 1280, 768)):
                    p = psf()
                    for kt in range(8):
                        S.mm(p[:, 0:hi - lo], HT2[:, kt, :], WR[:, kt, lo:hi], start=(kt == 0), stop=(kt == 7), r=[HT2, WR], w=[p])
                    S.cp("act", QKV[:, dlo:dlo + hi - lo], p[:, 0:hi - lo], r=[p], w=[QKV])
                for gc in range(6):
                    p = psf()
                    for kt in range(8):
                        S.mm(p[:, :], HT2[:, kt, :], WR[:, kt, 1280 + gc * 512:1280 + (gc + 1) * 512], start=(kt == 0), stop=(kt == 7),
                             r=[HT2, WR], w=[p])
                    S.act(GTH[:, gc * 512:(gc + 1) * 512], p[:, :], AF.Tanh, scale=0.5, r=[p], w=[GTH])
                qk3 = QKV[:, 0:640].rearrange("p (a b) -> p a b", b=64)
                rstd_of(QKV[:, 0:640], 10, 64, EPS, SCR2, SS10, RS10, [QKV])
                S.tt("dve", qk3, qk3, RS10[:, 0:10].unsqueeze(2).to_broadcast([128, 10, 64]), ALU.mult, r=[QKV, RS10], w=[QKV])
                S.tt("dve", qk3, qk3, QKG[:, :, :], ALU.mult, r=[QKV, QKG], w=[QKV])
                cosb = ROPE[:, ropej, 0:8].unsqueeze(1).to_broadcast([128, 10, 8])
                sinb = ROPE[:, ropej, 8:16].unsqueeze(1).to_broadcast([128, 10, 8])
                x1, x2 = qk3[:, :, 0:8], qk3[:, :, 8:16]
                S.tt("dve", RT[:, 0, :, :], x1, cosb, ALU.mult, r=[QKV, ROPE], w=[RT])
                S.tt("dve", RT[:, 1, :, :], x2, sinb, ALU.mult, r=[QKV, ROPE], w=[RT])
                S.tt("dve", RT[:, 2, :, :], x2, cosb, ALU.mult, r=[QKV, ROPE], w=[RT])
                S.tt("dve", RT[:, 3, :, :], x1, sinb, ALU.mult, r=[QKV, ROPE], w=[RT])
                S.tt("dve", x1, RT[:, 0, :, :], RT[:, 1, :, :], ALU.subtract, r=[RT], w=[QKV])
                S.tt("dve", x2, RT[:, 2, :, :], RT[:, 3, :, :], ALU.add, r=[RT], w=[QKV])
                xq3 = QKV[:, 768:1280].rearrange("p (a b) -> p a b", b=128)
                rstd_of(QKV[:, 768:1280], 4, 128, EPS, SCR2, SS10, RS10, [QKV])
                S.tt("dve", xq3, xq3, RS10[:, 0:4].unsqueeze(2).to_broadcast([128, 4, 128]), ALU.mult, r=[QKV, RS10], w=[QKV])
                S.tt("dve", XQB[:, :].rearrange("p (a b) -> p a b", b=128), xq3, XQG.unsqueeze(1).to_broadcast([128, 4, 128]),
                     ALU.mult, r=[QKV, VB], w=[XQB])

            def rwkv_out(jj, with_state):
                S.dma("sp", YPL[:, :], sc_yp[jj], w=[YPL])
                S.dma("sp", BVL[:, :], sc_bv[jj], w=[BVL])
                S.dma("sp", GOL[:, :], sc_g[jj], w=[GOL])
                if with_state:
                    S.dma("sp", GTL[:, :, :].rearrange("p a b -> p (a b)"), sc_gt[jj], w=[GTL])
                    p = psf()
                    for h in range(8):
                        S.mm(p[:, h * 64:(h + 1) * 64], GTL[:, h, :], SINB[:, h, :], r=[GTL, SINB], w=[p])
                    S.tt("dve", XMo[:, 0:512], p[:, :], YPL[:, :], ALU.add, r=[p, YPL], w=[XMo])
                    ysrc = XMo
                else:
                    ysrc = YPL
                y3 = ysrc[:, 0:512].rearrange("p (a b) -> p a b", b=64)
                sm, sq, mn, vr = ST8
                S.red(sm[:, :], y3, r=[ysrc], w=[sm])
                S.act(XMo[:, 512:1024], ysrc[:, 0:512], AF.Square, r=[ysrc], w=[XMo])
                S.red(sq[:, :], XMo[:, 512:1024].rearrange("p (a b) -> p a b", b=64), r=[XMo], w=[sq])
                S.ts("dve", mn[:, :], sm[:, :], 1.0 / 64, ALU.mult, r=[sm], w=[mn])
                S.tt("dve", vr[:, :], mn[:, :], mn[:, :], ALU.mult, r=[mn], w=[vr])
                S.stt(vr[:, :], sq[:, :], 1.0 / 64, vr[:, :], ALU.mult, ALU.subtract, r=[sq, vr], w=[vr])
                S.ts("dve", vr[:, :], vr[:, :], GN_EPS, ALU.add, r=[vr], w=[vr])
                S.tt("pool", sq[:, :], vr[:, :], NEGH[:, 0:8], ALU.pow, r=[vr, NEGH], w=[sq])
                yq3 = XMo[:, 512:1024].rearrange("p (a b) -> p a b", b=64)
                S.tt("dve", yq3, y3, mn[:, :].unsqueeze(2).to_broadcast([128, 8, 64]), ALU.subtract, r=[ysrc, mn], w=[XMo])
                S.tt("dve", yq3, yq3, sq[:, :].unsqueeze(2).to_broadcast([128, 8, 64]), ALU.mult, r=[XMo, sq], w=[XMo])
                S.tt("pool", XMo[:, 512:1024], XMo[:, 512:1024], LN_W, ALU.mult, r=[XMo, VB], w=[XMo])
                S.tt("pool", XMo[:, 512:1024], XMo[:, 512:1024], LN_B, ALU.add, r=[XMo, VB], w=[XMo])
                S.tt("dve", XMo[:, 512:1024], XMo[:, 512:1024], BVL[:, :], ALU.add, r=[XMo, BVL], w=[XMo])
                S.tt("dve", OAB[:, :], XMo[:, 512:1024], GOL[:, :], ALU.mult, r=[XMo, GOL], w=[OAB])
                pb = psb()
                for i in range(4):
                    S.tr(pb[:, i * 128:(i + 1) * 128], OAB[:, i * 128:(i + 1) * 128], IDB[:], r=[OAB, IDB], w=[pb])
                S.cp("act", OAT[:, :, :].rearrange("p a b -> p (a b)"), pb[:, 0:512], r=[pb], w=[OAT])

            def xq_transposes():
                pb = psb()
                for h in range(4):
                    S.tr(pb[:, h * 128:(h + 1) * 128], XQB[:, h * 128:(h + 1) * 128], IDB[:], r=[XQB, IDB], w=[pb])
                S.cp("act", XQT[:, :, :].rearrange("p a b -> p (a b)"), pb[:, 0:512], r=[pb], w=[XQT])

            def merge_out(xt, xm_dst_ap, extra_r=()):
                for cc in range(2):
                    cs = slice(cc * 512, (cc + 1) * 512)
                    pa_, pb_, pc_ = psf(), psf(), psf()
                    for k in range(4):
                        S.mm(pa_[:, :], OAT[:, k, :], WAC[:, k, cs], start=(k == 0), stop=(k == 3), r=[OAT, WAC], w=[pa_])
                    for h in range(8):
                        S.mm(pb_[:, :], OBT[:, h, :], WBB[:, h, cs], start=(h == 0), stop=(h == 7), r=[OBT, WBB], w=[pb_])
                    for k in range(4):
                        S.mm(pc_[:, :], OCT[:, k, :], WAC[:, 4 + k, cs], start=(k == 0), stop=(k == 3), r=[OCT, WAC], w=[pc_])
                    S.stt(MG[:, cs], GTH[:, cc * 512:(cc + 1) * 512], 1.0, pa_[:, :], ALU.add, ALU.mult, r=[GTH, pa_], w=[MG.sub(cc)])
                    S.stt(MT_[:, :], GTH[:, 1024 + cc * 512:1024 + (cc + 1) * 512], 1.0, pb_[:, :], ALU.add, ALU.mult, r=[GTH, pb_], w=[MT_])
                    S.tt("pool", MG[:, cs], MG[:, cs], MT_[:, :], ALU.add, r=[MG.sub(cc), MT_], w=[MG.sub(cc)])
                    S.stt(MT_[:, :], GTH[:, 2048 + cc * 512:2048 + (cc + 1) * 512], 1.0, pc_[:, :], ALU.add, ALU.mult, r=[GTH, pc_], w=[MT_])
                    S.tt("pool", MGB[:, cs], MG[:, cs], MT_[:, :], ALU.add, r=[MG.sub(cc), MT_], w=[MGB])
                pb = psb()
                for kt in range(8):
                    S.tr(pb[:, kt * 128:(kt + 1) * 128], MGB[:, kt * 128:(kt + 1) * 128], IDB[:], r=[MGB, IDB], w=[pb])
                S.cp("act", MGT[:, :, :].rearrange("p a b -> p (a b)"), pb[:, :], r=[pb], w=[MGT])
                for cc in range(2):
                    cs = slice(cc * 512, (cc + 1) * 512)
                    p = psf()
                    for kt in range(8):
                        S.mm(p[:, :], MGT[:, kt, :], WO[:, kt, cs], start=(kt == 0), stop=(kt == 7), r=[MGT, WO], w=[p])
                    S.stt(XMo[:, cs], p[:, :], 0.5, xt[:, cs], ALU.mult, ALU.add, r=[p, xt], w=[XMo])
                S.dma("sp", xm_dst_ap, XMo[:, :], r=[XMo])

            GTL = sb(es2, "gtl", [64, 8, 128], BF16)

            es2p = ExitStack()
            with es2p:
                QT2 = sb(es2p, "qt2", [128, 4, 128], BF16)
                KT = [sb(es2p, "kt%d" % i, [128, 128], BF16) for i in range(2)]
                VV = [sb(es2p, "vv%d" % i, [128, 128], BF16) for i in range(2)]
                PE_ = sb(es2p, "pe_", [128, 512])
                PT_ = [[sb(es2p, "pt_%d_%d" % (g, b), [128, 512], BF16) for b in range(2)] for g in range(2)]
                PM = [sb(es2p, "pm%d" % i, [128, 512], BF16) for i in range(2)]
                RD = sb(es2p, "rd", [128, 512])
                MASKH = sb(es2p, "maskh", [128, 128])
                SNK = sb(es2p, "snk", [64, 8, 128])
                S.cp("dve", SNK[:, :, :], ESK[0:64, :].unsqueeze(2).to_broadcast([64, 8, 128]), r=[ESK], w=[SNK])
                S.ts("dve", MASKH[:, :], LOI, FLG[:, 0:1], ALU.mult, r=[MASK, FLG], w=[MASKH])

                def swa_kv_prep(cur):
                    S.cp("dve", QKB[:, 0:512].rearrange("p (h g d) -> p h g d", h=4, g=2),
                         QKV[:, 0:512].rearrange("p (g h d) -> p h g d", g=2, h=4), r=[QKV], w=[QKB])
                    S.cp("act", QKB[:, 512:640], QKV[:, 512:640], r=[QKV], w=[QKB])
                    S.cp("act", VV[cur][:, :], QKV[:, 640:768], r=[QKV], w=[VV[cur]])
                    pb = psb()
                    for i in range(5):
                        S.tr(pb[:, i * 128:(i + 1) * 128], QKB[:, i * 128:(i + 1) * 128], IDB[:], r=[QKB, IDB], w=[pb])
                    S.cp("act", QT2[:, :, :].rearrange("p a b -> p (a b)"), pb[:, 0:512], r=[pb], w=[QT2])
                    S.cp("act", KT[cur][:, :], pb[:, 512:640], r=[pb], w=[KT[cur]])

                for j in range(NP):
                    cur, prv = j % 2, (j - 1) % 2
                    xt = XT2[j % 2]
                    if j == 0:
                        S.dma("sp", xt[:, :], xp[0:128, :], w=[xt])
                        norm_T(xt, 0, HT2, SCR2, HB2, SSx, RSx)
                        p = psf()
                        for kt in range(8):
                            S.mm(p[:, 0:256], HT2[:, kt, :], WR[:, kt, 512:768], start=(kt == 0), stop=(kt == 7), r=[HT2, WR], w=[p])
                        S.cp("act", QKV[:, 512:768], p[:, 0:256], r=[p], w=[QKV])
                        S.memset("dve", QKV[:, 0:512], 0.0, w=[QKV])
                        S.memset("dve", QKV[:, 768:1280], 0.0, w=[QKV])
                        qk3 = QKV[:, 0:640].rearrange("p (a b) -> p a b", b=64)
                        rstd_of(QKV[:, 0:640], 10, 64, EPS, SCR2, SS10, RS10, [QKV])
                        S.tt("dve", qk3, qk3, RS10[:, 0:10].unsqueeze(2).to_broadcast([128, 10, 64]), ALU.mult, r=[QKV, RS10], w=[QKV])
                        S.tt("dve", qk3, qk3, QKG[:, :, :], ALU.mult, r=[QKV, QKG], w=[QKV])
                        cosb = ROPE[:, 0, 0:8].unsqueeze(1).to_broadcast([128, 10, 8])
                        sinb = ROPE[:, 0, 8:16].unsqueeze(1).to_broadcast([128, 10, 8])
                        x1, x2 = qk3[:, :, 0:8], qk3[:, :, 8:16]
                        S.tt("dve", RT[:, 0, :, :], x1, cosb, ALU.mult, r=[QKV, ROPE], w=[RT])
                        S.tt("dve", RT[:, 1, :, :], x2, sinb, ALU.mult, r=[QKV, ROPE], w=[RT])
                        S.tt("dve", RT[:, 2, :, :], x2, cosb, ALU.mult, r=[QKV, ROPE], w=[RT])
                        S.tt("dve", RT[:, 3, :, :], x1, sinb, ALU.mult, r=[QKV, ROPE], w=[RT])
                        S.tt("dve", x1, RT[:, 0, :, :], RT[:, 1, :, :], ALU.subtract, r=[RT], w=[QKV])
                        S.tt("dve", x2, RT[:, 2, :, :], RT[:, 3, :, :], ALU.add, r=[RT], w=[QKV])
                        swa_kv_prep(cur)
                        continue
                    jj = j - 1
                    inproj_rest(xp[j * 128:(j + 1) * 128, :], xt, j)
                    if j == NT:
                        S.dma("sp", swakp_o, QKV[:, 512:640], r=[QKV])
                        S.dma("sp", swavp_o, QKV[:, 640:768], r=[QKV])
                    swa_kv_prep(cur)
                    xq_transposes()
                    for g in range(2):
                        gs_ = slice(g * 64, (g + 1) * 64)
                        for bi, kb in enumerate((prv, cur)):
                            p = psf()
                            S.mm(p[:, :], KT[kb][gs_, :], QT2[gs_, :, :].rearrange("p a b -> p (a b)"), r=[KT[kb], QT2], w=[p])
                            S.act(PE_[:, :], p[:, :], AF.Exp, scale=0.125, r=[p], w=[PE_])
                            if bi == 0:
                                m_ap = (MASKH[:, :] if j == 1 else LOI)
                            else:
                                m_ap = UPI
                            S.tt("dve", PT_[g][bi][:, :].rearrange("p (a b) -> p a b", b=128), PE_[:, :].rearrange("p (a b) -> p a b", b=128),
                                 m_ap.unsqueeze(1).to_broadcast([128, 4, 128]), ALU.mult, r=[PE_, MASK, MASKH], w=[PT_[g][bi]])
                        po, psm = psf(), psf()
                        for bi, kb in enumerate((prv, cur)):
                            S.mm(po[0:64, :], VV[kb][:, gs_], PT_[g][bi][:, :], start=(bi == 0), stop=(bi == 1), r=[VV[kb], PT_[g][bi]], w=[po])
                        for bi in range(2):
                            S.mm(psm[0:64, :], ONESB[:, 0:64], PT_[g][bi][:, :], start=(bi == 0), stop=(bi == 1), r=[ONESB, PT_[g][bi]], w=[psm])
                        S.tt("dve", RD[0:64, :].rearrange("p (a b) -> p a b", b=128), psm[0:64, :].rearrange("p (a b) -> p a b", b=128),
                             SNK[:, g * 4:(g + 1) * 4, :], ALU.add, r=[psm, SNK], w=[RD])
                        S.op("dve", lambda e: e.reciprocal(RD[0:64, :], RD[0:64, :]), [RD], [RD])
                        S.tt("dve", OBT[:, g * 4:(g + 1) * 4, :].rearrange("p a b -> p (a b)"), po[0:64, :], RD[0:64, :], ALU.mult,
                             r=[po, RD], w=[OBT])
                    for mt in range(2):
                        p = psf()
                        for h in range(4):
                            S.mm(p[:, h * 128:(h + 1) * 128], MKT[:, h, mt * 128:(mt + 1) * 128], XQT[:, h, :], r=[MKT, XQT], w=[p])
                        S.act(PM[mt][:, :], p[:, :], AF.Exp, scale=float(128 ** -0.5), r=[p], w=[PM[mt]])
                    po, psm = psf(), psf()
                    for h in range(4):
                        for mt in range(2):
                            S.mm(po[:, h * 128:(h + 1) * 128], MVB[:, mt, h * 128:(h + 1) * 128], PM[mt][:, h * 128:(h + 1) * 128],
                                 start=(mt == 0), stop=(mt == 1), r=[MVB, PM[mt]], w=[po])
                    for mt in range(2):
                        S.mm(psm[:, :], ONESB[:, :], PM[mt][:, :], start=(mt == 0), stop=(mt == 1), r=[ONESB, PM[mt]], w=[psm])
                    S.op("dve", lambda e, p_=psm: e.reciprocal(RD[:, :], p_[:, :]), [psm], [RD])
                    S.tt("dve", OCT[:, :, :].rearrange("p a b -> p (a b)"), po[:, :], RD[:, :], ALU.mult, r=[po, RD], w=[OCT])
                    rwkv_out(jj, True)
                    merge_out(xt, sc_xm[jj])
            S.barrier()
            es2s = ExitStack()
            with es2s:
                SELT = sb(es2s, "selt", [16, 128])
                SEL2 = sb(es2s, "sel2", [128, 16])
                S.dma("sp", SELT[:, :], selt, w=[SELT])
                S.dma("sp", SEL2[:, :], sel2, w=[SEL2])
                KC = sb(es2s, "kc", [128, 16, 128])
                VC = sb(es2s, "vc", [128, 16, 128])
                QR = MG
                SC_ = sb(es2s, "sc_", [128, 8, 16])
                OP_ = sb(es2s, "op_", [128, 520])
                PN = sb(es2s, "pn", [128, 8])
                SN = sb(es2s, "sn", [128, 8])
                DEN = sb(es2s, "den", [128, 8])
                TPn = sb(es2s, "tpn", [128, 8, 64])
                OBS = YPL
                OBSb = OAB
                SCm = sb(es2s, "scm", [128, 4, 32])
                OPs = sb(es2s, "ops", [128, 4])
                OPr = MT_
                OPm = GOL
                TPk3 = XMo[:, :].rearrange("p (a b) -> p a b", b=64)
                TPk = XMo
                TPm3 = BVL[:, :].rearrange("p (a b) -> p a b", b=128)
                TPm = BVL
                KM3 = KC[:, :, :].rearrange("p a b -> p (a b)").rearrange("p (a b) -> p a b", b=512)
                VM3 = VC[:, :, :].rearrange("p a b -> p (a b)").rearrange("p (a b) -> p a b", b=512)
                xt = XT2[0]
                S.dma("sp", KC[:, :, :].rearrange("p a b -> p (a b)"), s_swak.rearrange("b (g i) c -> (b g) (i c)", i=16), w=[KC])
                S.dma("sp", VC[:, :, :].rearrange("p a b -> p (a b)"), s_swav.rearrange("b (g i) c -> (b g) (i c)", i=16), w=[VC])
                S.dma("sp", swaks_o[:, 0:127, :], s_swak[:, 1:128, :])
                S.dma("sp", swavs_o[:, 0:127, :], s_swav[:, 1:128, :])
                inproj_rest(xs, xt, NT + 1)
                S.dma("sp", swaks_o[:, 127, :], QKV[0:16, 512:640], r=[QKV])
                S.dma("sp", swavs_o[:, 127, :], QKV[0:16, 640:768], r=[QKV])
                xq_transposes()
                S.cp("dve", SCR2[0:16, 0:512], QKV[0:16, 0:512], r=[QKV], w=[SCR2])
                S.cp("dve", SCR2[0:16, 512:1024], XQB[0:16, :], r=[XQB], w=[SCR2])
                for hf in range(2):
                    p = psf()
                    S.mm(p[:, :], SELT[:, :], SCR2[0:16, hf * 512:(hf + 1) * 512], r=[SELT, SCR2], w=[p])
                    S.cp("act", QR[:, hf * 512:(hf + 1) * 512], p[:, :], r=[p], w=[QR])
                for h in range(8):
                    kv = h // 4
                    S.tt("dve", TPk3, KC[:, :, kv * 64:(kv + 1) * 64],
                         QR[:, h * 64:(h + 1) * 64].unsqueeze(1).to_broadcast([128, 16, 64]), ALU.mult, r=[KC, QR], w=[TPk])
                    S.red(SC_[:, h, :], TPk3, r=[TPk], w=[SC_])
                S.act(SC_[:, :, :], SC_[:, :, :], AF.Exp, scale=0.125, r=[SC_], w=[SC_])
                S.red(OP_[:, 512:520], SC_[:, :, :], r=[SC_], w=[OP_.sub("s")])
                for h in range(8):
                    kv = h // 4
                    S.tt("dve", TPk3, VC[:, :, kv * 64:(kv + 1) * 64],
                         SC_[:, h, :].unsqueeze(2).to_broadcast([128, 16, 64]), ALU.mult, r=[VC, SC_], w=[TPk])
                    S.red(OP_[:, h * 64:(h + 1) * 64], TPk3.rearrange("p i d -> p d i"), r=[TPk], w=[OP_.sub(h)])
                po, psm = psf(), psf()
                S.mm(po[0:16, :], SEL2[:, :], OP_[:, 0:512], r=[SEL2] + [OP_.sub(h) for h in range(8)], w=[po])
                S.mm(psm[0:16, 0:8], SEL2[:, :], OP_[:, 512:520], r=[SEL2, OP_.sub("s")], w=[psm])
                q3 = QKV[:, 0:512].rearrange("p (a b) -> p a b", b=64)
                for g in range(2):
                    S.tt("dve", TPn[:, g * 4:(g + 1) * 4, :], q3[:, g * 4:(g + 1) * 4, :],
                         QKV[:, 512 + g * 64:512 + (g + 1) * 64].unsqueeze(1).to_broadcast([128, 4, 64]), ALU.mult, r=[QKV], w=[TPn])
                S.red(SN[:, :], TPn[:, :, :], r=[TPn], w=[SN])
                S.act(PN[:, :], SN[:, :], AF.Exp, scale=0.125, r=[SN], w=[PN])
                S.tt("dve", DEN[:, :], PN[:, :], ESK[:, :], ALU.add, r=[PN, ESK], w=[DEN])
                S.tt("dve", DEN[0:16, :], DEN[0:16, :], psm[0:16, 0:8], ALU.add, r=[DEN, psm], w=[DEN])
                S.op("dve", lambda e: e.reciprocal(DEN[:, :], DEN[:, :]), [DEN], [DEN])
                for g in range(2):
                    S.tt("dve", TPn[:, g * 4:(g + 1) * 4, :], PN[:, g * 4:(g + 1) * 4].unsqueeze(2).to_broadcast([128, 4, 64]),
                         QKV[:, 640 + g * 64:640 + (g + 1) * 64].unsqueeze(1).to_broadcast([128, 4, 64]), ALU.mult, r=[PN, QKV], w=[TPn])
                S.cp("dve", OBS[:, :], TPn[:, :, :].rearrange("p a b -> p (a b)"), r=[TPn], w=[OBS])
                S.tt("dve", OBS[0:16, :], OBS[0:16, :], po[0:16, :], ALU.add, r=[OBS, po], w=[OBS])
                S.tt("dve", OBSb[:, :].rearrange("p (a b) -> p a b", b=64), OBS[:, :].rearrange("p (a b) -> p a b", b=64),
                     DEN[:, :].unsqueeze(2).to_broadcast([128, 8, 64]), ALU.mult, r=[OBS, DEN], w=[OBSb])
                pb = psb()
                for h in range(8):
                    S.tr(pb[0:64, h * 128:(h + 1) * 128], OBSb[:, h * 64:(h + 1) * 64], IDB[:], r=[OBSb, IDB], w=[pb])
                S.cp("act", OBT[:, :, :].rearrange("p a b -> p (a b)"), pb[0:64, :], r=[pb], w=[OBT])
                mk4 = s_memk.rearrange("b (g r i) c -> (b g) r (i c)", g=8, r=8, i=4)
                mv4 = s_memv.rearrange("b (g r i) c -> (b g) r (i c)", g=8, r=8, i=4)
                for r_ in range(8):
                    S.dma("sp", KC[:, :, :].rearrange("p a b -> p (a b)"), mk4[:, r_, :], w=[KC])
                    for h in range(4):
                        S.tt("dve", TPm3, KM3[:, :, h * 128:(h + 1) * 128],
                             QR[:, 512 + h * 128:512 + (h + 1) * 128].unsqueeze(1).to_broadcast([128, 4, 128]), ALU.mult, r=[KC, QR], w=[TPm])
                        S.red(SCm[:, h, r_ * 4:(r_ + 1) * 4], TPm3, r=[TPm], w=[SCm])
                S.act(SCm[:, :, :], SCm[:, :, :], AF.Exp, scale=float(128 ** -0.5), r=[SCm], w=[SCm])
                S.red(OPs[:, :], SCm[:, :, :], r=[SCm], w=[OPs])
                for r_ in range(8):
                    S.dma("sp", VC[:, :, :].rearrange("p a b -> p (a b)"), mv4[:, r_, :], w=[VC])
                    for h in range(4):
                        S.tt("dve", TPm3, VM3[:, :, h * 128:(h + 1) * 128],
                             SCm[:, h, r_ * 4:(r_ + 1) * 4].unsqueeze(2).to_broadcast([128, 4, 128]), ALU.mult, r=[VC, SCm], w=[TPm])
                        dst = OPm if r_ == 0 else OPr
                        S.red(dst[:, h * 128:(h + 1) * 128], TPm3.rearrange("p i d -> p d i"), r=[TPm], w=[dst])
                    if r_ > 0:
                        S.tt("dve", OPm[:, 0:512], OPm[:, 0:512], OPr[:, :], ALU.add, r=[OPm, OPr], w=[OPm])
                po, psm = psf(), psf()
                S.mm(po[0:16, :], SEL2[:, :], OPm[:, 0:512], r=[SEL2, OPm], w=[po])
                S.mm(psm[0:16, 0:4], SEL2[:, :], OPs[:, :], r=[SEL2, OPs], w=[psm])
                S.memset("dve", DEN[:, :], 1.0, w=[DEN])
                S.cp("dve", DEN[0:16, 0:4], psm[0:16, 0:4], r=[psm], w=[DEN])
                S.op("dve", lambda e: e.reciprocal(DEN[:, 0:4], DEN[:, 0:4]), [DEN], [DEN])
                S.memset("dve", OBS[:, :], 0.0, w=[OBS])
                S.cp("dve", OBS[0:16, :], po[0:16, :], r=[po], w=[OBS])
                S.tt("dve", OBSb[:, :].rearrange("p (a b) -> p a b", b=128), OBS[:, :].rearrange("p (a b) -> p a b", b=128),
                     DEN[:, 0:4].unsqueeze(2).to_broadcast([128, 4, 128]), ALU.mult, r=[OBS, DEN], w=[OBSb])
                pb = psb()
                for h in range(4):
                    S.tr(pb[:, h * 128:(h + 1) * 128], OBSb[:, h * 128:(h + 1) * 128], IDB[:], r=[OBSb, IDB], w=[pb])
                S.cp("act", OCT[:, :, :].rearrange("p a b -> p (a b)"), pb[:, 0:512], r=[pb], w=[OCT])
                rwkv_out(NT, False)
                merge_out(xt, sc_xm[NT])
        S.barrier()

        es3 = ExitStack()
        with es3:
            WUP = sb(es3, "wup", [128, 8, 4096], BF16)
            WDN = sb(es3, "wdn", [128, 32, 1024], BF16)
            es3w = ExitStack()
            es3w.__enter__()
            STG3 = [sb(es3w, "stg3_%d" % i, [128, 2048]) for i in range(4)]
            s3i = [0]

            def ld3(dst_ap, src_ap, ncol, wtok):
                st = STG3[s3i[0] % 4]
                ce = ("pool", "act", "dve")[s3i[0] % 3]
                s3i[0] += 1
                S.dma("sp", st[:, 0:ncol], src_ap, w=[st])
                S.cp(ce, dst_ap, st[:, 0:ncol], r=[st], w=[wtok])

            for kt in range(8):
                for hf in range(2):
                    ld3(WUP[:, kt, hf * 2048:(hf + 1) * 2048], w_up[kt * 128:(kt + 1) * 128, hf * 2048:(hf + 1) * 2048], 2048, WUP)
            for fc in range(32):
                ld3(WDN[:, fc, :], w_down[fc * 128:(fc + 1) * 128, :], 1024, WDN)
            S.barrier()
            es3w.__exit__(None, None, None)
            XM3 = [sb(es3, "xm3_%d" % i, [128, D]) for i in range(2)]
            SCR3 = sb(es3, "scr3", [128, D])
            HB3 = sb(es3, "hb3", [128, D], BF16)
            SS3 = sb(es3, "ss3", [128, 8])
            RS3 = sb(es3, "rs3", [128, 8])
            H2T = sb(es3, "h2t", [128, 8, 128], BF16)
            RL = sb(es3, "rl", [128, 512])
            HID = sb(es3, "hid", [128, 32, 128], BF16)
            YO = [sb(es3, "yo%d" % i, [128, D]) for i in range(2)]
            for t in range(NT + 1):
                xm = XM3[t % 2]
                yo = YO[t % 2]
                S.dma("sp", xm[:, :], sc_xm[t], w=[xm])
                norm_T(xm, 1, H2T, SCR3, HB3, SS3, RS3)
                for f4 in range(8):
                    p = psf()
                    for fi in range(4):
                        fc = f4 * 4 + fi
                        for kt in range(8):
                            S.mm(p[:, fi * 128:(fi + 1) * 128], WUP[:, kt, fc * 128:(fc + 1) * 128], H2T[:, kt, :],
                                 start=(kt == 0), stop=(kt == 7), r=[WUP, H2T], w=[p])
                    S.act(RL[:, :], p[:, :], AF.Relu, r=[p], w=[RL])
                    S.tt("dve" if f4 % 2 == 0 else "pool", HID[:, f4 * 4:(f4 + 1) * 4, :].rearrange("p a b -> p (a b)"), RL[:, :], RL[:, :],
                         ALU.mult, r=[RL], w=[HID.sub(f4)])
                for cc in range(2):
                    cs = slice(cc * 512, (cc + 1) * 512)
                    p = psf()
                    for fc in range(32):
                        S.mm(p[:, :], HID[:, fc, :], WDN[:, fc, cs], start=(fc == 0), stop=(fc == 31), r=[HID.sub(fc // 4), WDN], w=[p])
                    S.tt("dve", yo[:, cs], p[:, :], xm[:, cs], ALU.add, r=[p, xm], w=[yo])
                if t < NT:
                    S.dma("sp", y_o[t * 128:(t + 1) * 128, :], yo[:, :], r=[yo])
                else:
                    S.dma("sp", ys_o, yo[0:16, :], r=[yo])
        S.barrier()
        print("total ops", S.gseq, {e: len(S.streams[e]) for e in S.ENG}, flush=True)
        import os
        if os.environ.get("KLOG"):
            for g, e, ln in S.oplog[:int(os.environ["KLOG"])]:
                print(g, e, "line", ln)
        S.emit()
    return nc


def build_rest(nc, S, es, L):
    pass


def _host_inputs(NT, c, I):
    f32 = np.float32
    SEQ = I["x_prompt"].shape[1]
    seq, pos = c // 4, c % 4
    t0 = pos * NT * 128
    xp = np.zeros(((NT + 1) * 128, D), f32)
    if pos > 0:
        xp[:] = I["x_prompt"][seq, t0 - 128:t0 + NT * 128]
    else:
        xp[128:] = I["x_prompt"][seq, 0:NT * 128]
    xs = np.zeros((128, D), f32)
    xs[:16] = I["x_sample"][16 * c:16 * c + 16, 0]
    p = np.arange(128)
    ups = (p[:, None] < p[None, :]).astype(f32)
    upi = (p[:, None] <= p[None, :]).astype(f32)
    cmask = np.stack([ups, upi, ups.T.copy(), upi.T.copy(), np.eye(128, dtype=f32)], axis=1)
    ebias = np.stack([-C0H * (p + 1), -C0H * p, C0H * (p + 1), -C0H * (127 - p)], axis=1).astype(f32)
    half = 8
    inv_freq = np.power(np.float32(500000.0), -np.arange(half, dtype=f32) * np.float32(2.0 / 16)).astype(f32)
    rope = np.zeros((128, NT + 2, 16), f32)
    for j in range(NT + 2):
        if j <= NT:
            posj = (t0 - 128 + j * 128 + p).astype(f32)
        else:
            posj = np.full(128, 8192, f32)
        ang = posj[:, None] * inv_freq[None, :]
        rope[:, j, 0:8] = np.cos(ang)
        rope[:, j, 8:16] = np.sin(ang)
    flags = np.zeros((128, 4), f32)
    flags[:, 0] = 1.0 if pos > 0 else 0.0
    for q in range(3):
        flags[:, 1 + q] = 1.0 if q < pos else 0.0
    gains = np.stack([I["norm_mix"][0].reshape(8, 128).T, I["norm_ffn"][0].reshape(8, 128).T,
                      I["mem_norm"][0].reshape(8, 128).T], axis=1).astype(f32)
    vecA = np.concatenate([I["rw_mu"][0], I["rw_w0"][0], I["rw_a0"][0], I["rw_k_k"][0], I["rw_k_a"][0],
                           I["rw_r_k"][0].reshape(-1)])[None, :].astype(f32)
    vecB = np.concatenate([I["rw_ln_w"][0], I["rw_ln_b"][0], I["q_norm"][0], I["k_norm"][0], I["xq_norm"][0],
                           I["xk_norm"][0], I["swa_sinks"][0]])[None, :].astype(f32)
    b0 = 16 * c
    m = {
        "xp": xp, "xs": xs, "cmask": np.ascontiguousarray(cmask), "ebias": ebias, "rope": rope, "flags": flags,
        "gains": np.ascontiguousarray(gains), "vecA": vecA, "vecB": vecB,
        "s_state": I["state_rwkv"][0, b0:b0 + 16].reshape(128, 4096),
        "s_shift": I["state_rwkv_shift"][0, b0:b0 + 16],
        "s_swak": I["cache_swa_k"][0, b0:b0 + 16].reshape(16, 128, 128),
        "s_swav": I["cache_swa_v"][0, b0:b0 + 16].reshape(16, 128, 128),
        "selt": (np.arange(128)[None, :] // 8 == np.arange(16)[:, None]).astype(f32),
        "sel2": (np.arange(128)[:, None] // 8 == np.arange(16)[None, :]).astype(f32),
        "s_memk": I["cache_mem_k"][0, b0:b0 + 16].reshape(16, 256, 512),
        "s_memv": I["cache_mem_v"][0, b0:b0 + 16].reshape(16, 256, 512),
        "memp": I["mem_prompt"][seq],
        "w_in": I["w_in"][0], "w2a": np.concatenate([I["rw_w2"][0], I["rw_a2"][0]], axis=0), "g2": I["rw_g2"][0],
        "w_mkv": I["w_mem_kv"][0],
        "w_br": np.concatenate([I["w_br_a"][0], I["w_br_b"][0], I["w_br_c"][0]], axis=0),
        "w_out": I["w_out"][0], "w_up": I["w_up"][0], "w_down": I["w_down"][0],
    }
    return {k: np.ascontiguousarray(v, dtype=f32) for k, v in m.items()}


_NC_CACHE = {}


def kernel(**inputs):
    I = {k: np.asarray(v) for k, v in inputs.items()}
    B, SEQ, _ = I["x_prompt"].shape
    NT = SEQ // (4 * 128)
    if NT not in _NC_CACHE:
        _NC_CACHE[NT] = build_nc(NT)
    nc = _NC_CACHE[NT]
    in_maps = [_host_inputs(NT, c, I) for c in range(NCORES)]
    res = run_bass_kernel_spmd(nc, in_maps, core_ids=list(range(NCORES)))
    return assemble(res.results, NT)


def assemble(R, NT):
    f32 = np.float32
    SEQ = 4 * NT * 128
    y_prompt = np.zeros((2, SEQ, D), f32)
    for c in range(8):
        y_prompt[c // 4, (c % 4) * NT * 128:(c % 4 + 1) * NT * 128] = R[c]["y"].reshape(NT * 128, D)
    y_sample = np.concatenate([R[c]["ys"].reshape(16, D) for c in range(8)], axis=0).reshape(128, 1, D)
    st_p = np.stack([R[3]["stp"].reshape(8, 64, 64), R[7]["stp"].reshape(8, 64, 64)])[None]
    shift_p = np.stack([R[3]["zlast"].reshape(-1), R[7]["zlast"].reshape(-1)])[None]
    swak_p = np.stack([R[3]["swakp"], R[7]["swakp"]]).reshape(1, 2, 128, 2, 64)
    swav_p = np.stack([R[3]["swavp"], R[7]["swavp"]]).reshape(1, 2, 128, 2, 64)
    memk_p = np.stack([R[0]["memk"], R[4]["memk"]]).reshape(1, 2, 256, 4, 128)
    memv_p = np.stack([R[0]["memv"], R[4]["memv"]]).reshape(1, 2, 256, 4, 128)
    st_s = np.concatenate([R[c]["sts"].reshape(16, 8, 64, 64) for c in range(8)], axis=0)[None]
    shift_s = np.concatenate([R[c]["shifts"].reshape(16, 1792) for c in range(8)], axis=0)[None]
    swak_s = np.concatenate([R[c]["swaks"].reshape(16, 128, 2, 64) for c in range(8)], axis=0)[None]
    swav_s = np.concatenate([R[c]["swavs"].reshape(16, 128, 2, 64) for c in range(8)], axis=0)[None]
    outs = (y_prompt, y_sample, st_p, shift_p, swak_p, swav_p, memk_p, memv_p, st_s, shift_s, swak_s, swav_s)
    return tuple(np.ascontiguousarray(o, dtype=f32) for o in outs)
```

```python
import numpy as np
from contextlib import ExitStack
import concourse.bass as bass
import concourse.mybir as mybir
from concourse.bass_utils import run_bass_kernel_spmd

F32 = mybir.dt.float32
BF16 = mybir.dt.bfloat16
ALU = mybir.AluOpType
AF = mybir.ActivationFunctionType
AX = mybir.AxisListType

D = 1024
NCORES = 8
C0H = float(np.exp(-0.5) / 2.0)
EPS = 1e-5
GN_EPS = 64e-5
SAFE_OPS = 10 ** 9


class Tok:
    __slots__ = ("w", "r")

    def __init__(self):
        self.w = None
        self.r = {}


class Tile:
    def __init__(self, h):
        self.h = h
        self.tok = Tok()
        self.subs = {}

    def sub(self, key):
        if key not in self.subs:
            self.subs[key] = Tok()
        return self.subs[key]

    def __getitem__(self, k):
        return self.h[k]


def _tok(x):
    return x.tok if isinstance(x, Tile) else x


class Sched:
    ENG = ("pe", "act", "dve", "pool", "sp")

    def __init__(self, nc, es, n_dsem=12):
        self.nc = nc
        self.h = {"pe": nc.tensor, "act": nc.scalar, "dve": nc.vector, "pool": nc.gpsimd, "sp": nc.sync}
        self.streams = {e: [] for e in self.ENG}
        self.cnt = {e: 0 for e in self.ENG}
        self.waited = {e: {} for e in self.ENG}
        self.esem = {e: es.enter_context(nc.semaphore("es_" + e)) for e in self.ENG}
        self.dsem = {}
        self.dcnt = {}
        self.dnext = {}
        for q in ("sp", "pool", "act"):
            self.dsem[q] = [es.enter_context(nc.semaphore("ds_%s%d" % (q, i))) for i in range(n_dsem)]
            self.dcnt[q] = [0] * n_dsem
            self.dnext[q] = 0
        self.ccsem = es.enter_context(nc.semaphore("ccsem"))
        self.gseq = 0
        self.oplog = []
        self.gidx = {e: [] for e in self.ENG}

    def _semobj(self, key):
        if key[0] == "e":
            return self.esem[key[1]]
        if key[0] == "d":
            return self.dsem[key[1]][key[2]]
        return self.ccsem

    def _resolve(self, eng, deps):
        waits = []
        best = {}
        for (key, val) in deps:
            if key == ("e", eng) and eng == "pe":
                continue
            if best.get(key, 0) < val:
                best[key] = val
        for key, val in best.items():
            if self.waited[eng].get(key, 0) < val:
                self.waited[eng][key] = val
                waits.append((self._semobj(key), val))
        return waits

    def _deps(self, reads, writes):
        deps = []
        for t in reads:
            t = _tok(t)
            if t.w is not None:
                deps.append(t.w)
        for t in writes:
            t = _tok(t)
            if t.w is not None:
                deps.append(t.w)
            deps.extend(t.r.items())
        return deps

    def _mark(self, me, reads, writes):
        key, val = me
        for t in reads:
            t = _tok(t)
            if t.r.get(key, 0) < val:
                t.r[key] = val
        for t in writes:
            t = _tok(t)
            t.w = me
            t.r = {}

    def op(self, eng, fn, r=(), w=()):
        waits = self._resolve(eng, self._deps(r, w))
        self.cnt[eng] += 1
        me = (("e", eng), self.cnt[eng])
        self._mark(me, r, w)
        self.streams[eng].append((waits, fn, self.esem[eng], 1))
        self.gseq += 1
        self.gidx[eng].append(self.gseq)
        self._log(eng)

    def _log(self, eng):
        import sys
        f = sys._getframe(2)
        while f.f_code.co_name in ("mm", "tr", "act", "tt", "ts", "stt", "red", "cp", "memset", "op", "dma"):
            f = f.f_back
        self.oplog.append((self.gseq, eng, f.f_lineno))

    def dma(self, q, out, in_, r=(), w=()):
        import os
        if q == "pool" and os.environ.get("KNOPOOL"):
            return
        i = self.dnext[q]
        self.dnext[q] = (i + 1) % len(self.dsem[q])
        deps = self._deps(r, w)
        if self.dcnt[q][i] > 0:
            deps.append((("d", q, i), self.dcnt[q][i]))
        waits = self._resolve(q, deps)
        self.dcnt[q][i] += 16
        me = (("d", q, i), self.dcnt[q][i])
        self._mark(me, r, w)
        self.streams[q].append((waits, lambda e, o=out, s=in_: e.dma_start(out=o, in_=s), self.dsem[q][i], 16))
        self.gseq += 1
        self.gidx[q].append(self.gseq)
        self._log("dma-" + q)

    def barrier(self, toks=()):
        deps = []
        for e in self.ENG:
            if self.cnt[e] > 0:
                deps.append((("e", e), self.cnt[e]))
        for q in self.dsem:
            for i, c in enumerate(self.dcnt[q]):
                if c > 0:
                    deps.append((("d", q, i), c))
        for e in self.ENG:
            waits = self._resolve(e, deps)
            if waits:
                self.streams[e].append((waits, None, None, 0))
                self.gidx[e].append(self.gseq)

    def emit(self):
        import os
        nc = self.nc
        lim = int(os.environ.get("KSTOP", "0")) or SAFE_OPS
        totals = {}
        for name in self.ENG:
            for (waits, fn, sem, inc), gi in zip(self.streams[name], self.gidx[name]):
                if gi > lim or fn is None:
                    continue
                k = id(sem)
                totals[k] = (sem, totals.get(k, (sem, 0))[1] + inc)
        with nc.Block() as block:
            def runner(name):
                def run(eng):
                    for (waits, fn, sem, inc), gi in zip(self.streams[name], self.gidx[name]):
                        if gi > lim:
                            break
                        for s, v in waits:
                            eng.wait_ge(s, v)
                        if fn is not None:
                            fn(eng).then_inc(sem, inc)
                    for s, v in totals.values():
                        eng.wait_ge(s, v)
                return run
            block.tensor(runner("pe"))
            block.scalar(runner("act"))
            block.vector(runner("dve"))
            block.gpsimd(runner("pool"))
            block.sync(runner("sp"))

    def mm(self, out, lhsT, rhs, start=True, stop=True, r=(), w=()):
        self.op("pe", lambda e: e.matmul(out, lhsT, rhs, start=start, stop=stop), r, w)

    def tr(self, out, in_, ident, r=(), w=()):
        self.op("pe", lambda e: e.transpose(out, in_, ident), r, w)

    def act(self, out, in_, func, bias=None, scale=None, r=(), w=()):
        kw = {}
        if bias is not None:
            kw["bias"] = bias
        if scale is not None:
            kw["scale"] = scale
        self.op("act", lambda e: e.activation(out, in_, func, **kw), r, w)

    def tt(self, eng, out, in0, in1, op, r=(), w=()):
        self.op(eng, lambda e: e.tensor_tensor(out, in0, in1, op), r, w)

    def ts(self, eng, out, in0, s1, op0, s2=None, op1=None, r=(), w=()):
        if op1 is None:
            self.op(eng, lambda e: e.tensor_scalar(out, in0, s1, None, op0), r, w)
        else:
            self.op(eng, lambda e: e.tensor_scalar(out, in0, s1, s2, op0, op1), r, w)

    def stt(self, out, in0, scalar, in1, op0, op1, r=(), w=()):
        self.op("dve", lambda e: e.scalar_tensor_tensor(out, in0, scalar, in1, op0, op1), r, w)

    def red(self, out, in_, r=(), w=(), op=ALU.add):
        self.op("dve", lambda e: e.tensor_reduce(out, in_, AX.X, op), r, w)

    def cp(self, eng, out, in_, r=(), w=()):
        if eng == "act":
            self.op("act", lambda e: e.activation(out, in_, AF.Copy), r, w)
        else:
            self.op(eng, lambda e: e.tensor_copy(out, in_), r, w)

    def memset(self, eng, ap, val, w=()):
        self.op(eng, lambda e: e.memset(ap, val), (), w)


def build_nc(NT):
    nc = bass.Bass("TRN2", target_bir_lowering=False)
    NP = NT + 1
    NR = NT + 2

    def din(name, shape):
        return nc.dram_tensor(name, list(shape), F32, kind="ExternalInput").ap()

    def dout(name, shape):
        return nc.dram_tensor(name, list(shape), F32, kind="ExternalOutput").ap()

    xp = din("xp", [NP * 128, D])
    xs = din("xs", [128, D])
    cmask = din("cmask", [128, 5, 128])
    ebias = din("ebias", [128, 4])
    rope = din("rope", [128, NR, 16])
    flags = din("flags", [128, 4])
    gains = din("gains", [128, 3, 8])
    vecA = din("vecA", [1, 4352])
    vecB = din("vecB", [1, 1416])
    s_state = din("s_state", [128, 4096])
    s_shift = din("s_shift", [16, 1792])
    s_swak = din("s_swak", [16, 128, 128])
    s_swav = din("s_swav", [16, 128, 128])
    selt = din("selt", [16, 128])
    sel2 = din("sel2", [128, 16])
    s_memk = din("s_memk", [16, 256, 512])
    s_memv = din("s_memv", [16, 256, 512])
    memp = din("memp", [256, D])
    w_in = din("w_in", [D, 6144])
    w2a = din("w2a", [128, 512])
    g2 = din("g2", [128, 512])
    w_mkv = din("w_mkv", [D, 1024])
    w_br = din("w_br", [1536, D])
    w_out = din("w_out", [D, D])
    w_up = din("w_up", [D, 4096])
    w_down = din("w_down", [4096, D])

    y_o = dout("y", [NT * 128, D])
    ys_o = dout("ys", [16, D])
    stp_o = dout("stp", [8, 64, 64])
    zlast_o = dout("zlast", [1, 1792])
    swakp_o = dout("swakp", [128, 128])
    swavp_o = dout("swavp", [128, 128])
    memk_o = dout("memk", [256, 512])
    memv_o = dout("memv", [256, 512])
    sts_o = dout("sts", [128, 4096])
    shifts_o = dout("shifts", [16, 1792])
    swaks_o = dout("swaks", [16, 128, 128])
    swavs_o = dout("swavs", [16, 128, 128])

    sc_yp = nc.dram_tensor("sc_yp", [NT + 1, 128, 512], F32).ap()
    sc_bv = nc.dram_tensor("sc_bv", [NT + 1, 128, 512], F32).ap()
    sc_g = nc.dram_tensor("sc_g", [NT + 1, 128, 512], F32).ap()
    sc_gt = nc.dram_tensor("sc_gt", [NT, 64, 1024], BF16).ap()
    sc_xm = nc.dram_tensor("sc_xm", [NT + 1, 128, D], F32).ap()
    sc_s1 = nc.dram_tensor("sc_s1", [6, 16, 512], F32).ap()
    sc_s2 = nc.dram_tensor("sc_s2", [16, 512], F32).ap()
    sc_q = nc.dram_tensor("sc_q", [16, 1024], F32).ap()
    cc_src = nc.dram_tensor("cc_src", [64, 1024], F32)
    cc_dst = nc.dram_tensor("cc_dst", [4 * 64, 1024], F32)

    es = ExitStack()
    with es:
        S = Sched(nc, es)

        uid = [0]

        def sb(es_, name, shape, dt=F32):
            uid[0] += 1
            return Tile(es_.enter_context(nc.sbuf_tensor("t%d_%s" % (uid[0], name), list(shape), dt)))

        def pst(es_, name, shape, dt=F32):
            return Tile(es_.enter_context(nc.psum_tensor("p_" + name, list(shape), dt)))

        PF = [pst(es, "pf%d" % i, [128, 512], F32) for i in range(6)]
        PB = [pst(es, "pb%d" % i, [128, 1024], BF16) for i in range(2)]
        pfi = [0]
        pbi = [0]

        def psf():
            t = PF[pfi[0] % 5]
            pfi[0] += 1
            return t

        def psb():
            t = PB[pbi[0] % 2]
            pbi[0] += 1
            return t

        MASK = sb(es, "mask", [128, 5, 128])
        EB = sb(es, "eb", [128, 4])
        ROPE = sb(es, "rope", [128, NR, 16])
        FLG = sb(es, "flg", [128, 4])
        GAIN = sb(es, "gain", [128, 3, 8])
        IDB = sb(es, "idb", [128, 128], BF16)
        NEGH = sb(es, "negh", [128, 16])
        ONES2 = sb(es, "ones2", [128, 2])
        ONESB = sb(es, "onesb", [128, 128], BF16)
        S.dma("sp", MASK[:], cmask, w=[MASK])
        S.dma("sp", EB[:], ebias, w=[EB])
        S.dma("sp", ROPE[:], rope, w=[ROPE])
        S.dma("sp", FLG[:], flags, w=[FLG])
        S.dma("sp", GAIN[:], gains, w=[GAIN])
        S.cp("dve", IDB[:], MASK[:, 4, :], r=[MASK], w=[IDB])
        S.memset("dve", NEGH[:], -0.5, w=[NEGH])
        S.memset("dve", ONES2[:], 1.0, w=[ONES2])
        ONESF = sb(es, "onesf", [128, 64])
        CB128 = sb(es, "cb128", [128, 1])
        CBH = sb(es, "cbh", [128, 1])
        S.memset("dve", CBH[:], -C0H, w=[CBH])
        S.memset("dve", CB128[:], -C0H * 128.0, w=[CB128])
        S.memset("dve", ONESF[:], 1.0, w=[ONESF])
        S.memset("dve", ONESB[:], 1.0, w=[ONESB])
        UPS, UPI, LOS, LOI, IDF = (MASK[:, i, :] for i in range(5))
        SINB = sb(es, "sinb", [64, 8, 64], BF16)

        def rstd_of(x_ap, n, gs, eps, scr, ss, rs, r_toks):
            S.act(scr[:, 0:n * gs], x_ap, AF.Square, r=r_toks, w=[scr])
            S.red(ss[:, 0:n], scr[:, 0:n * gs].rearrange("p (a b) -> p a b", b=gs), r=[scr], w=[ss])
            S.ts("dve", ss[:, 0:n], ss[:, 0:n], 1.0 / gs, ALU.mult, eps, ALU.add, r=[ss], w=[ss])
            S.tt("pool", rs[:, 0:n], ss[:, 0:n], NEGH[:, 0:n], ALU.pow, r=[ss, NEGH], w=[rs])

        def norm_T(xt, gi, hT, scr, hb, ss, rs, dst=None):
            rstd_of(xt[:, :], 1, D, EPS, scr, ss, rs, [xt])
            S.act(hb[:, :], xt[:, :], AF.Copy, scale=rs[:, 0:1], r=[xt, rs], w=[hb])
            pb = psb()
            for kt in range(8):
                S.tr(pb[:, kt * 128:(kt + 1) * 128], hb[:, kt * 128:(kt + 1) * 128], IDB[:], r=[hb, IDB], w=[pb])
            S.tt("dve", (hT[:, :, :] if dst is None else dst), pb[:, :].rearrange("p (a b) -> p a b", b=128),
                 GAIN[:, gi, :].unsqueeze(2).to_broadcast([128, 8, 128]), ALU.mult, r=[pb, GAIN], w=[hT])

        es1 = ExitStack()
        with es1:
            WZ = sb(es1, "wz", [128, 8, 1792], BF16)
            W2A = sb(es1, "w2a", [128, 512], BF16)
            G2 = sb(es1, "g2", [128, 512], BF16)
            VA = sb(es1, "va", [128, 4352])
            es1w = ExitStack()
            es1w.__enter__()
            STG = [sb(es1w, "stg%d" % i, [128, 1792]) for i in range(4)]
            stg_i = [0]

            def load_cast(dst_ap, src_ap, ncol, wtok):
                st = STG[stg_i[0] % 4]
                ce = ("pool", "act", "dve")[stg_i[0] % 3]
                stg_i[0] += 1
                S.dma("sp", st[:, 0:ncol], src_ap, w=[st])
                S.cp(ce, dst_ap, st[:, 0:ncol], r=[st], w=[wtok])

            for kt in range(8):
                load_cast(WZ[:, kt, :], w_in[kt * 128:(kt + 1) * 128, 0:1792], 1792, WZ)
            load_cast(W2A[:, :], w2a, 512, W2A)
            load_cast(G2[:, :], g2, 512, G2)
            S.barrier()
            es1w.__exit__(None, None, None)
            S.dma("sp", VA[:], vecA.partition_broadcast(128).rearrange("p a n -> p (a n)"), w=[VA])
            MU = VA[:, 0:1792]
            W0 = VA[:, 1792:2304]
            A0 = VA[:, 2304:2816]
            K_K = VA[:, 2816:3328]
            K_A = VA[:, 3328:3840]
            R_K = VA[:, 3840:4352]

            XT = [sb(es1, "xt%d" % i, [128, D]) for i in range(2)]
            SCR = sb(es1, "scr", [128, D])
            HB = sb(es1, "hb", [128, D], BF16)
            SS = sb(es1, "ss", [128, 8])
            RS = sb(es1, "rs", [128, 8])
            HT = [sb(es1, "ht%d" % i, [128, 8, 128], BF16) for i in range(2)]
            Z = [sb(es1, "z%d" % i, [128, 1792]) for i in range(2)]
            ZP = sb(es1, "zp", [128, 1792])
            ZS = sb(es1, "zs", [128, 1792])
            LC = sb(es1, "lc", [128, 256], BF16)
            LCT = sb(es1, "lct", [128, 256], BF16)
            TW = sb(es1, "tw", [128, 512])
            AA = sb(es1, "aa", [128, 512])
            GO = sb(es1, "go", [128, 512])
            KK = sb(es1, "kk", [128, 512])
            KH = sb(es1, "kh", [128, 512])
            BB = sb(es1, "bb", [128, 512])
            T1 = sb(es1, "t1", [128, 512])
            T2 = sb(es1, "t2", [128, 512])
            BV = sb(es1, "bv", [128, 512])
            SS8 = sb(es1, "ss8", [128, 8])
            RN8 = sb(es1, "rn8", [128, 8])
            BS8 = sb(es1, "bs8", [128, 8])

            def rwkv_pre(zc, zp_ready_toks):
                S.tt("pool", ZS[:, :], ZP[:, :], zc[:, :], ALU.subtract, r=[ZP, zc], w=[ZS])
                S.tt("dve", ZS[:, :], ZS[:, :], MU, ALU.mult, r=[ZS, VA], w=[ZS])
                S.tt("pool", ZS[:, :], ZS[:, :], zc[:, :], ALU.add, r=[ZS, zc], w=[ZS])
                r_ = ZS[:, 0:512]
                k_ = ZS[:, 512:1024]
                S.act(LC[:, 0:64], ZS[:, 1536:1600], AF.Tanh, r=[ZS], w=[LC])
                S.act(LC[:, 64:128], ZS[:, 1600:1664], AF.Copy, r=[ZS], w=[LC])
                S.act(LC[:, 128:256], ZS[:, 1664:1792], AF.Tanh, scale=0.5, r=[ZS], w=[LC])
                S.ts("dve", LC[:, 128:256], LC[:, 128:256], 0.5, ALU.mult, 0.5, ALU.add, r=[LC], w=[LC])
                pb = psb()
                S.tr(pb[:, 0:128], LC[:, 0:128], IDB[:], r=[LC, IDB], w=[pb])
                S.tr(pb[:, 128:256], LC[:, 128:256], IDB[:], r=[LC, IDB], w=[pb])
                S.cp("act", LCT[:, :], pb[:, 0:256], r=[pb], w=[LCT])
                yield
                pw, pa, pg = psf(), psf(), psf()
                S.mm(pw[:, :], LCT[0:64, 0:128], W2A[0:64, :], r=[LCT, W2A], w=[pw])
                S.mm(pa[:, :], LCT[64:128, 0:128], W2A[64:128, :], r=[LCT, W2A], w=[pa])
                S.mm(pg[:, :], LCT[:, 128:256], G2[:, :], r=[LCT, G2], w=[pg])
                S.tt("dve", TW[:, :], pw[:, :], W0, ALU.add, r=[pw, VA], w=[TW])
                S.act(TW[:, :], TW[:, :], AF.Tanh, scale=0.5, r=[TW], w=[TW])
                S.tt("dve", AA[:, :], pa[:, :], A0, ALU.add, r=[pa, VA], w=[AA])
                S.act(AA[:, :], AA[:, :], AF.Tanh, scale=0.5, r=[AA], w=[AA])
                S.ts("dve", AA[:, :], AA[:, :], 0.5, ALU.mult, 0.5, ALU.add, r=[AA], w=[AA])
                S.cp("act", GO[:, :], pg[:, :], r=[pg], w=[GO])
                yield
                S.tt("dve", KK[:, :], k_, K_K, ALU.mult, r=[ZS, VA], w=[KK])
                S.act(T1[:, :], KK[:, :], AF.Square, r=[KK], w=[T1])
                S.red(SS8[:, :], T1[:, :].rearrange("p (a b) -> p a b", b=64), r=[T1], w=[SS8])
                S.ts("dve", SS8[:, :], SS8[:, :], 1e-24, ALU.max, r=[SS8], w=[SS8])
                S.tt("pool", RN8[:, :], SS8[:, :], NEGH[:, 0:8], ALU.pow, r=[SS8, NEGH], w=[RN8])
                S.tt("dve", KK[:, :].rearrange("p (a b) -> p a b", b=64), KK[:, :].rearrange("p (a b) -> p a b", b=64),
                     RN8[:, :].unsqueeze(2).to_broadcast([128, 8, 64]), ALU.mult, r=[KK, RN8], w=[KK])
                S.stt(T1[:, :], AA[:, :], -1.0, K_A, ALU.add, ALU.mult, r=[AA, VA], w=[T1])
                S.stt(KH[:, :], T1[:, :], 1.0, k_, ALU.add, ALU.mult, r=[T1, ZS], w=[KH])
                yield
                S.tt("pool", BB[:, :], KK[:, :], AA[:, :], ALU.mult, r=[KK, AA], w=[BB])
                S.tt("pool", T2[:, :], r_, KH[:, :], ALU.mult, r=[ZS, KH], w=[T2])
                S.tt("pool", T2[:, :], T2[:, :], R_K, ALU.mult, r=[T2, VA], w=[T2])
                S.red(BS8[:, :], T2[:, :].rearrange("p (a b) -> p a b", b=64), r=[T2], w=[BS8])
                S.tt("dve", BV[:, :].rearrange("p (a b) -> p a b", b=64), ZS[:, 1024:1536].rearrange("p (a b) -> p a b", b=64),
                     BS8[:, :].unsqueeze(2).to_broadcast([128, 8, 64]), ALU.mult, r=[ZS, BS8], w=[BV])

            def zproj(xsrc_ap, zc, ht, xt):
                S.dma("sp", xt[:, :], xsrc_ap, w=[xt])
                norm_T(xt, 0, ht, SCR, HB, SS, RS)
                for c, (lo, hi) in enumerate(((0, 512), (512, 1024), (1024, 1536), (1536, 1792))):
                    p = psf()
                    for kt in range(8):
                        S.mm(p[:, 0:hi - lo], ht[:, kt, :], WZ[:, kt, lo:hi], start=(kt == 0), stop=(kt == 7),
                             r=[ht, WZ], w=[p])
                    S.cp("act", zc[:, lo:hi], p[:, 0:hi - lo], r=[p], w=[zc])
                    yield

            es1p = ExitStack()
            with es1p:
                SP = sb(es1p, "sp", [64, 8, 128])
                SPB = sb(es1p, "spb", [64, 8, 128], BF16)
                es1q = ExitStack()
                es1q.__enter__()
                M4 = sb(es1q, "m4", [128, 4, 128])
                S.cp("dve", M4[:, 0:2, :], MASK[:, 0:1, :].to_broadcast([128, 2, 128]), r=[MASK], w=[M4])
                S.cp("dve", M4[:, 2:4, :], MASK[:, 1:2, :].to_broadcast([128, 2, 128]), r=[MASK], w=[M4])
                GI = sb(es1q, "gi", [128, 512])
                GV = sb(es1q, "gv", [128, 512])
                GX = sb(es1q, "gx", [128, 512])
                GE = sb(es1q, "ge", [128, 512])
                TM = sb(es1q, "tm", [128, 7, 512], BF16)
                FT = sb(es1q, "ft", [128, 4, 4, 128], BF16)
                GC = sb(es1q, "gc", [64, 512])
                HW = [sb(es1q, "hw%d" % i, [128, 5, 128], BF16) for i in range(8)]
                IW = [[sb(es1q, "iw%d_%d" % (i, j), [128, 3, 128], BF16) for j in range(2)] for i in range(8)]
                XF = [sb(es1q, "xf%d" % i, [128, 128], BF16) for i in range(8)]
                RH = [sb(es1q, "rh%d" % i, [128, 128], BF16) for i in range(8)]
                WU = [sb(es1q, "wu%d" % i, [128, 128], BF16) for i in range(8)]
                PTs = [sb(es1q, "pt%d" % i, [64, 64]) for i in range(8)]
                HS = [sb(es1q, "hs%d" % i, [64, 64]) for i in range(8)]
                QT = [sb(es1q, "qt%d" % i, [64, 128], BF16) for i in range(8)]
                GTS = [sb(es1q, "gts%d" % i, [64, 8, 128], BF16) for i in range(2)]
                YP = [sb(es1q, "yp%d" % i, [128, 512]) for i in range(2)]

                S.memset("dve", SP[:, :, 0:64], 0.0, w=[SP])
                for h in range(8):
                    S.cp("dve", SP[:, h, 64:128], MASK[0:64, 4, 0:64], r=[MASK], w=[SP])
                S.cp("act", SPB[:, :, :], SP[:, :, :], r=[SP], w=[SPB])

                TMs = [TM, sb(es1q, "tm_b", [128, 7, 512], BF16)]
                FTs = [FT, sb(es1q, "ft_b", [128, 4, 4, 128], BF16)]
                GCs = [GC, sb(es1q, "gc_b", [64, 512])]

                def pre_gen(j):
                    TM, FT, GC = TMs[j % 2], FTs[j % 2], GCs[j % 2]
                    zc = Z[j % 2]
                    yield from zproj(xp[j * 128:(j + 1) * 128, :], zc, HT[j % 2], XT[j % 2])
                    if j == 0:
                        return
                    zprev = Z[(j - 1) % 2]
                    S.dma("sp", ZP[1:128, :], zc[0:127, :], r=[zc], w=[ZP])
                    S.dma("sp", ZP[0:1, :], zprev[127:128, :], r=[zprev], w=[ZP])
                    if j == NT:
                        S.dma("sp", zlast_o, zc[127:128, :], r=[zc])
                    yield from rwkv_pre(zc, None)
                    jj = j - 1
                    S.dma("sp", sc_bv[jj], BV[:, :], r=[BV])
                    S.dma("sp", sc_g[jj], GO[:, :], r=[GO])
                    p_i, p_s, p_r = psf(), psf(), psf()
                    S.mm(p_i[:, :], UPI, TW[:, :], r=[MASK, TW], w=[p_i])
                    S.mm(p_s[:, :], UPS, TW[:, :], r=[MASK, TW], w=[p_s])
                    S.mm(p_r[:, :], LOS, TW[:, :], r=[MASK, TW], w=[p_r])
                    S.act(GI[:, :], p_i[:, :], AF.Exp, bias=EB[:, 0:1], scale=-C0H, r=[p_i, EB], w=[GI])
                    S.act(GV[:, :], p_i[:, :], AF.Exp, bias=EB[:, 2:3], scale=C0H, r=[p_i, EB], w=[GV])
                    S.act(GX[:, :], p_s[:, :], AF.Exp, bias=EB[:, 1:2], scale=-C0H, r=[p_s, EB], w=[GX])
                    S.act(GE[:, :], p_r[:, :], AF.Exp, bias=EB[:, 3:4], scale=-C0H, r=[p_r, EB], w=[GE])
                    yield
                    pc = psf()
                    for h in range(8):
                        S.mm(pc[0:64, h * 64:(h + 1) * 64], TW[:, h * 64:(h + 1) * 64], ONESF[:, :], r=[TW, ONESF], w=[pc])
                    S.act(GC[:, :], pc[0:64, :], AF.Exp, bias=CB128[0:64, 0:1], scale=-C0H, r=[pc, CB128], w=[GC])
                    S.tt("dve", TM[:, 0, :], ZS[:, 0:512], GI[:, :], ALU.mult, r=[ZS, GI], w=[TM.sub(0)])
                    S.stt(TM[:, 1, :], KK[:, :], -1.0, GX[:, :], ALU.mult, ALU.mult, r=[KK, GX], w=[TM.sub(1)])
                    S.tt("dve", TM[:, 2, :], BB[:, :], GV[:, :], ALU.mult, r=[BB, GV], w=[TM.sub(2)])
                    S.tt("pool", TM[:, 3, :], KH[:, :], GV[:, :], ALU.mult, r=[KH, GV], w=[TM.sub(3)])
                    S.tt("pool", TM[:, 4, :], BB[:, :], GE[:, :], ALU.mult, r=[BB, GE], w=[TM.sub(4)])
                    S.tt("pool", TM[:, 5, :], KH[:, :], GE[:, :], ALU.mult, r=[KH, GE], w=[TM.sub(5)])
                    S.cp("act", TM[:, 6, :], ZS[:, 1024:1536], r=[ZS], w=[TM.sub(6)])
                    yield
                    srcslot = (1, 0, 2, 3)
                    for half in range(2):
                        pb = psb()
                        for hpp in range(2):
                            hp = half * 2 + hpp
                            for sl in range(4):
                                o = (hpp * 4 + sl) * 128
                                S.tr(pb[:, o:o + 128], TM[:, srcslot[sl], hp * 128:(hp + 1) * 128], IDB[:],
                                     r=[TM.sub(srcslot[sl]), IDB], w=[pb])
                        S.cp("act" if half == 0 else "dve",
                             FT[:, half * 2:half * 2 + 2, :, :].rearrange("p a b c -> p (a b c)"), pb[:, :],
                             r=[pb], w=[FT.sub(half)])
                        yield

                def stages(j, step):
                    TM, FT, GC = TMs[j % 2], FTs[j % 2], GCs[j % 2]
                    jj = j - 1
                    ypar = YP[jj % 2]
                    gts = GTS[jj % 2]
                    p_y = PF[5]
                    H8 = range(8)

                    def hv(h):
                        hp, base = h // 2, 64 * (h % 2)
                        return hp, FT.sub(hp // 2), slice(h * 64, (h + 1) * 64), slice(base, base + 64)

                    for h in H8:
                        hp, fs, hsl, ps_ = hv(h)
                        hw = HW[h]
                        aT, rT, bT, kT = (FT[ps_, hp, i, :] for i in range(4))
                        pA = psf()
                        S.mm(pA[:, 0:128], bT, aT, r=[fs], w=[pA])
                        S.mm(pA[:, 128:256], kT, aT, r=[fs], w=[pA])
                        S.mm(pA[:, 256:384], bT, rT, r=[fs], w=[pA])
                        S.mm(pA[:, 384:512], kT, rT, r=[fs], w=[pA])
                        pN = psf()
                        S.mm(pN[:, 0:128], aT, bT, r=[fs], w=[pN])
                        S.tt("dve", hw[:, 1:5, :], pA[:, :].rearrange("p (a b) -> p a b", b=128), M4[:, :, :], ALU.mult,
                             r=[pA, M4], w=[hw])
                        S.tt("dve", hw[:, 0, :], pN[:, 0:128], LOS, ALU.mult, r=[pN, MASK], w=[hw])
                    step()
                    curs = {}
                    for h in H8:
                        hw = HW[h]
                        pI = psf()
                        S.mm(pI[:, 0:128], hw[:, 1, :], hw[:, 0, :], r=[hw], w=[pI])
                        S.mm(pI[:, 128:256], hw[:, 0, :], hw[:, 1, :], r=[hw], w=[pI])
                        cur = IW[h][0]
                        S.cp("act", cur[:, 0:2, :].rearrange("p a b -> p (a b)"), pI[:, 0:256], r=[pI], w=[cur])
                        S.tt("pool", cur[:, 2, :], hw[:, 1, :], IDF, ALU.add, r=[hw, MASK], w=[cur])
                        curs[h] = cur
                    step()
                    for lev in range(1, 6):
                        step()
                        for h in H8:
                            cur = curs[h]
                            nxt = IW[h][lev % 2]
                            pI = psf()
                            S.mm(pI[:, 0:128], cur[:, 1, :], cur[:, 0, :], r=[cur], w=[pI])
                            S.mm(pI[:, 128:384], cur[:, 0, :], cur[:, 1:3, :].rearrange("p a b -> p (a b)"), r=[cur], w=[pI])
                            S.cp("act", nxt[:, 0:2, :].rearrange("p a b -> p (a b)"), pI[:, 0:256], r=[pI], w=[nxt])
                            S.tt("dve", nxt[:, 2, :], cur[:, 2, :], pI[:, 256:384], ALU.add, r=[cur, pI], w=[nxt])
                            curs[h] = nxt
                    step()
                    for h in H8:
                        cur = curs[h]
                        pI = psf()
                        S.mm(pI[:, 0:128], cur[:, 0, :], cur[:, 2, :], r=[cur], w=[pI])
                        S.tt("dve", XF[h][:, :], cur[:, 2, :], pI[:, 0:128], ALU.add, r=[cur, pI], w=[XF[h]])
                    step()
                    for h in H8:
                        hp, fs, hsl, ps_ = hv(h)
                        hw, rh = HW[h], RH[h]
                        pV = psf()
                        S.mm(pV[:, 0:64], hw[:, 2, :], TM[:, 6, hsl], r=[hw, TM.sub(6)], w=[pV])
                        S.cp("pool", rh[:, 0:64], TM[:, 1, hsl], r=[TM.sub(1)], w=[rh])
                        S.cp("act", rh[:, 64:128], pV[:, 0:64], r=[pV], w=[rh])
                    step()
                    for h in H8:
                        pW = psf()
                        S.mm(pW[:, 0:128], XF[h][:, :], RH[h][:, :], r=[XF[h], RH[h]], w=[pW])
                        S.cp("act", WU[h][:, :], pW[:, 0:128], r=[pW], w=[WU[h]])
                    step()
                    for h in H8:
                        hp, fs, hsl, ps_ = hv(h)
                        hw, wu, pts, hs, qt = HW[h], WU[h], PTs[h], HS[h], QT[h]
                        pP = psf()
                        S.mm(pP[0:64, 0:64], wu[:, 0:64], TM[:, 4, hsl], r=[wu, TM.sub(4)], w=[pP])
                        S.mm(pP[0:64, 64:128], TM[:, 4, hsl], wu[:, 64:128], start=True, stop=False, r=[wu, TM.sub(4)], w=[pP])
                        S.mm(pP[0:64, 64:128], TM[:, 5, hsl], TM[:, 6, hsl], start=False, stop=True,
                             r=[TM.sub(5), TM.sub(6)], w=[pP])
                        S.mm(pP[0:64, 128:256], wu[:, 0:64], hw[:, 3, :], start=True, stop=False, r=[wu, hw], w=[pP])
                        S.mm(pP[0:64, 128:256], TM[:, 0, hsl], IDB[:, :], start=False, stop=True, r=[TM.sub(0), IDB], w=[pP])
                        S.stt(pts[:, :], MASK[0:64, 4, 0:64], GC[:, h * 64:h * 64 + 1], pP[0:64, 0:64], ALU.mult, ALU.add,
                              r=[MASK, GC, pP], w=[pts])
                        S.cp("dve", hs[:, :], pP[0:64, 64:128], r=[pP], w=[hs])
                        S.cp("dve", qt[:, :], pP[0:64, 128:256], r=[pP], w=[qt])
                    step()
                    for h in H8:
                        hp, fs, hsl, ps_ = hv(h)
                        hw, wu, qt = HW[h], WU[h], QT[h]
                        S.mm(p_y[:, hsl], hw[:, 3, :], wu[:, 64:128], start=True, stop=False, r=[hw, wu], w=[p_y])
                        S.mm(p_y[:, hsl], hw[:, 4, :], TM[:, 6, hsl], start=False, stop=False, r=[hw, TM.sub(6)], w=[p_y])
                        S.mm(p_y[:, hsl], qt[:, :], SPB[:, h, 0:64], start=False, stop=True, r=[qt, SPB.sub(h)], w=[p_y])
                    step()
                    for h in H8:
                        qt = QT[h]
                        pG = psf()
                        S.mm(pG[0:64, 0:128], SPB[:, h, 64:128], qt[:, :], r=[SPB.sub(h), qt], w=[pG])
                        S.cp("act", gts[:, h, :], pG[0:64, 0:128], r=[pG], w=[gts])
                    step()
                    for h in H8:
                        pts, hs = PTs[h], HS[h]
                        pS = psf()
                        S.mm(pS[0:64, 0:128], pts[:, :], SP[:, h, :], r=[pts, SP.sub(h)], w=[pS])
                        S.tt("dve", SP[:, h, 0:64], pS[0:64, 0:64], hs[:, :], ALU.add, r=[pS, hs], w=[SP.sub(h)])
                        S.cp("act", SP[:, h, 64:128], pS[0:64, 64:128], r=[pS], w=[SP.sub(h)])
                        S.cp("pool", SPB[:, h, :], SP[:, h, :], r=[SP.sub(h)], w=[SPB.sub(h)])
                    S.cp("act", ypar[:, :], p_y[:, :], r=[p_y], w=[ypar])
                    S.dma("sp", sc_yp[jj], ypar[:, :], r=[ypar])
                    S.dma("sp", sc_gt[jj], gts[:, :, :].rearrange("p a b -> p (a b)"), r=[gts])


                for _ in pre_gen(0):
                    pass
                for _ in pre_gen(1):
                    pass
                for j in range(1, NP):
                    g = pre_gen(j + 1) if j + 1 < NP else iter(())
                    stages(j, lambda g=g: next(g, None))
                    for _ in g:
                        pass
                S.barrier()
                es1q.__exit__(None, None, None)
                EX = sb(es1p, "ex", [64, 8, 128])
                EXA = sb(es1p, "exa", [64, 4, 8, 128])
                SIN = sb(es1p, "sin", [64, 8, 64])
                SC1 = sb(es1p, "sc1", [64, 8, 64])
                SFT = sb(es1p, "sft", [64, 8, 64])
                hall = [SP.sub(h) for h in range(8)]
                S.cp("dve", EX[:, :, 0:64], SP[:, :, 0:64], r=hall, w=[EX])
                pT = psf()
                for h in range(8):
                    S.tr(pT[0:64, h * 64:(h + 1) * 64], SP[:, h, 64:128], MASK[0:64, 4, 0:64], r=hall + [MASK], w=[pT])
                S.cp("dve", EX[:, :, 64:128], pT[0:64, :].rearrange("p (a b) -> p a b", b=64), r=[pT], w=[EX])
                S.dma("pool", cc_src.ap(), EX[:, :, :].rearrange("p a b -> p (a b)"), r=[EX], w=[EXA.sub("src")])
                deps = S._deps([EXA.sub("src")], [EXA.sub("dst")])
                waits = S._resolve("pool", deps)
                S.cnt["pool"] += 1
                me = (("e", "pool"), S.cnt["pool"])
                S._mark(me, [EXA.sub("src")], [EXA.sub("dst")])
                S.streams["pool"].append((waits, lambda e: e.collective_compute(
                    "AllGather", ALU.bypass, replica_groups=[[0, 1, 2, 3], [4, 5, 6, 7]],
                    ins=[cc_src.ap()], outs=[cc_dst.ap()]), S.esem["pool"], 1))
                S.gseq += 1
                S.gidx["pool"].append(S.gseq)
                S.dma("pool", EXA[:, :, :, :].rearrange("p r a b -> p r (a b)"),
                      cc_dst.ap().rearrange("(r p) n -> p r n", p=64), r=[EXA.sub("dst")], w=[EXA])
                S.memset("dve", SIN[:, :, :], 0.0, w=[SIN])
                for q in range(3):
                    pc_ = psf()
                    for h in range(8):
                        S.mm(pc_[0:64, h * 64:(h + 1) * 64], EXA[:, q, h, 64:128], SIN[:, h, :], r=[EXA, SIN], w=[pc_])
                    S.tt("dve", SC1[:, :, :], pc_[0:64, :].rearrange("p (a b) -> p a b", b=64), EXA[:, q, :, 0:64], ALU.add,
                         r=[pc_, EXA], w=[SC1])
                    S.tt("dve", SC1[:, :, :], SC1[:, :, :], SIN[:, :, :], ALU.subtract, r=[SC1, SIN], w=[SC1])
                    S.stt(SIN[:, :, :].rearrange("p a b -> p (a b)"), SC1[:, :, :].rearrange("p a b -> p (a b)"),
                          FLG[0:64, 1 + q:2 + q], SIN[:, :, :].rearrange("p a b -> p (a b)"), ALU.mult, ALU.add,
                          r=[SC1, SIN, FLG], w=[SIN])
                pf_ = psf()
                for h in range(8):
                    S.mm(pf_[0:64, h * 64:(h + 1) * 64], EX[:, h, 64:128], SIN[:, h, :], r=[EX, SIN], w=[pf_])
                S.tt("dve", SFT[:, :, :], pf_[0:64, :].rearrange("p (a b) -> p a b", b=64), EX[:, :, 0:64], ALU.add,
                     r=[pf_, EX], w=[SFT])
                pf2 = psf()
                for h in range(8):
                    S.tr(pf2[0:64, h * 64:(h + 1) * 64], SFT[:, h, :], MASK[0:64, 4, 0:64], r=[SFT, MASK], w=[pf2])
                S.cp("dve", SC1[:, :, :], pf2[0:64, :].rearrange("p (a b) -> p a b", b=64), r=[pf2], w=[SC1])
                S.dma("sp", stp_o.rearrange("h v k -> v h k"), SC1[:, :, :], r=[SC1])
                S.cp("act", SINB[:, :, :], SIN[:, :, :], r=[SIN], w=[SINB])
            S.barrier()
            es1s = ExitStack()
            with es1s:
                SX = sb(es1s, "sx", [128, 6, 512])
                R6 = sb(es1s, "r6", [128, 6, 64])
                ST = sb(es1s, "st", [128, 64, 64])
                TP = sb(es1s, "tp", [128, 64, 64])
                SA = sb(es1s, "sa", [128, 64])
                YV = sb(es1s, "yv", [128, 64])
                YS = sb(es1s, "ysr", [128, 512])
                WD = sb(es1s, "wd", [128, 512])
                zc = Z[0]
                S.dma("sp", ST[:, :, :].rearrange("p a b -> p (a b)"), s_state, w=[ST])
                for _ in zproj(xs, zc, HT[0], XT[0]):
                    pass
                S.dma("sp", shifts_o, zc[0:16, :], r=[zc])
                S.memset("dve", ZP[:, :], 0.0, w=[ZP])
                S.dma("sp", ZP[0:16, :], s_shift, w=[ZP])
                for _ in rwkv_pre(zc, None):
                    pass
                S.dma("sp", sc_bv[NT], BV[:, :], r=[BV])
                S.dma("sp", sc_g[NT], GO[:, :], r=[GO])
                S.act(WD[:, :], TW[:, :], AF.Exp, bias=CBH[:, 0:1], scale=-C0H, r=[TW, CBH], w=[WD])
                for i, src in enumerate((ZS[:, 0:512], WD[:, :], KH[:, :], ZS[:, 1024:1536], KK[:, :], BB[:, :])):
                    S.cp("dve" if i % 2 == 0 else "pool", SX[:, i, :], src, r=[ZS, WD, KH, KK, BB], w=[SX])
                S.dma("sp", sc_s1.rearrange("i b n -> b i n"), SX[0:16, :, :], r=[SX], w=[R6.sub("d")])
                S.dma("sp", R6[:, :, :], sc_s1.rearrange("i b (h d) -> (b h) i d", d=64), r=[R6.sub("d")], w=[R6])
                r_, w_, k_, v_, kk_, b_ = (R6[:, i, :] for i in range(6))

                def bv(ap):
                    return ap.unsqueeze(1).to_broadcast([128, 64, 64])

                def bk(ap):
                    return ap.unsqueeze(2).to_broadcast([128, 64, 64])

                S.tt("dve", TP[:, :, :], ST[:, :, :], bv(kk_), ALU.mult, r=[ST, R6], w=[TP])
                S.red(SA[:, :], TP[:, :, :], r=[TP], w=[SA])
                S.tt("pool", ST[:, :, :], ST[:, :, :], bv(w_), ALU.mult, r=[ST, R6, TP], w=[ST])
                S.tt("dve", TP[:, :, :], bk(SA[:, :]), bv(b_), ALU.mult, r=[SA, R6], w=[TP])
                S.tt("dve", ST[:, :, :], ST[:, :, :], TP[:, :, :], ALU.subtract, r=[ST, TP], w=[ST])
                S.tt("pool", TP[:, :, :], bk(v_), bv(k_), ALU.mult, r=[R6, ST], w=[TP])
                S.tt("dve", ST[:, :, :], ST[:, :, :], TP[:, :, :], ALU.add, r=[ST, TP], w=[ST])
                S.dma("sp", sts_o, ST[:, :, :].rearrange("p a b -> p (a b)"), r=[ST])
                S.tt("dve", TP[:, :, :], ST[:, :, :], bv(r_), ALU.mult, r=[ST, R6], w=[TP])
                S.red(YV[:, :], TP[:, :, :], r=[TP], w=[YV])
                S.dma("sp", sc_s2.rearrange("b (h d) -> (b h) d", d=64), YV[:, :], r=[YV], w=[YS.sub("d")])
                S.memset("dve", YS[:, :], 0.0, w=[YS])
                S.dma("sp", YS[0:16, :], sc_s2, r=[YS.sub("d")], w=[YS])
                S.dma("sp", sc_yp[NT], YS[:, :], r=[YS])
        S.barrier()

        MKT = sb(es, "mkt", [128, 4, 256], BF16)
        MVB = sb(es, "mvb", [128, 2, 512], BF16)
        VB = sb(es, "vb", [128, 1416])
        S.dma("sp", VB[:], vecB.partition_broadcast(128).rearrange("p a n -> p (a n)"), w=[VB])
        LN_W, LN_B = VB[:, 0:512], VB[:, 512:1024]
        XQG, XKG = VB[:, 1152:1280], VB[:, 1280:1408]
        es0 = ExitStack()
        with es0:
            WM = sb(es0, "wm", [128, 8, 1024], BF16)
            STG0 = [sb(es0, "stg0_%d" % i, [128, 1024]) for i in range(2)]
            for kt in range(8):
                st = STG0[kt % 2]
                S.dma("sp", st[:, :], w_mkv[kt * 128:(kt + 1) * 128, :], w=[st])
                S.cp(("pool", "act", "dve")[kt % 3], WM[:, kt, :], st[:, :], r=[st], w=[WM])
            XM0 = sb(es0, "xm0", [128, D])
            SCR0 = sb(es0, "scr0", [128, D])
            HB0 = sb(es0, "hb0", [128, D], BF16)
            SS0 = sb(es0, "ss0", [128, 8])
            RS0 = sb(es0, "rs0", [128, 8])
            HT0 = sb(es0, "ht0", [128, 8, 128], BF16)
            MKF = sb(es0, "mkf", [128, 512])
            MVF = sb(es0, "mvf", [128, 512])
            MKB = sb(es0, "mkb", [128, 512], BF16)
            for mt in range(2):
                S.dma("sp", XM0[:, :], memp[mt * 128:(mt + 1) * 128, :], w=[XM0])
                norm_T(XM0, 2, HT0, SCR0, HB0, SS0, RS0)
                pk, pv_ = psf(), psf()
                for kt in range(8):
                    S.mm(pk[:, :], HT0[:, kt, :], WM[:, kt, 0:512], start=(kt == 0), stop=(kt == 7), r=[HT0, WM], w=[pk])
                for kt in range(8):
                    S.mm(pv_[:, :], HT0[:, kt, :], WM[:, kt, 512:1024], start=(kt == 0), stop=(kt == 7), r=[HT0, WM], w=[pv_])
                S.cp("act", MVF[:, :], pv_[:, :], r=[pv_], w=[MVF])
                S.dma("sp", memv_o[mt * 128:(mt + 1) * 128, :], MVF[:, :], r=[MVF])
                S.cp("pool", MVB[:, mt, :], MVF[:, :], r=[MVF], w=[MVB])
                rstd_of(pk[:, :], 4, 128, EPS, SCR0, SS0, RS0, [pk])
                S.tt("dve", MKF[:, :].rearrange("p (a b) -> p a b", b=128), pk[:, :].rearrange("p (a b) -> p a b", b=128),
                     RS0[:, 0:4].unsqueeze(2).to_broadcast([128, 4, 128]), ALU.mult, r=[pk, RS0], w=[MKF])
                S.tt("dve", MKF[:, :].rearrange("p (a b) -> p a b", b=128), MKF[:, :].rearrange("p (a b) -> p a b", b=128),
                     XKG.unsqueeze(1).to_broadcast([128, 4, 128]), ALU.mult, r=[MKF, VB], w=[MKF])
                S.dma("sp", memk_o[mt * 128:(mt + 1) * 128, :], MKF[:, :], r=[MKF])
                S.cp("act", MKB[:, :], MKF[:, :], r=[MKF], w=[MKB])
                pb = psb()
                for h in range(4):
                    S.tr(pb[:, h * 128:(h + 1) * 128], MKB[:, h * 128:(h + 1) * 128], IDB[:], r=[MKB, IDB], w=[pb])
                S.cp("act", MKT[:, :, mt * 128:(mt + 1) * 128], pb[:, 0:512].rearrange("p (a b) -> p a b", b=128), r=[pb], w=[MKT])
        S.barrier()

        es2 = ExitStack()
        with es2:
            WR = sb(es2, "wr", [128, 8, 4352], BF16)
            WAC = sb(es2, "wac", [128, 8, 1024], BF16)
            WBB = sb(es2, "wbb", [64, 8, 1024], BF16)
            WO = sb(es2, "wo", [128, 8, 1024], BF16)
            es2w = ExitStack()
            es2w.__enter__()
            STG2 = [sb(es2w, "stg2_%d" % i, [128, 1088]) for i in range(6)]
            sgi = [0]

            def ldc(dst_ap, src_ap, np_, ncol, wtok):
                st = STG2[sgi[0] % 6]
                ce = ("pool", "act", "dve")[sgi[0] % 3]
                sgi[0] += 1
                S.dma("sp", st[0:np_, 0:ncol], src_ap, w=[st])
                S.cp(ce, dst_ap, st[0:np_, 0:ncol], r=[st], w=[wtok])

            for kt in range(8):
                for hf in range(4):
                    ldc(WR[:, kt, hf * 1088:(hf + 1) * 1088], w_in[kt * 128:(kt + 1) * 128, 1792 + hf * 1088:1792 + (hf + 1) * 1088], 128, 1088, WR)
            for i in range(4):
                ldc(WAC[:, i, :], w_br[i * 128:(i + 1) * 128, :], 128, 1024, WAC)
                ldc(WAC[:, 4 + i, :], w_br[1024 + i * 128:1024 + (i + 1) * 128, :], 128, 1024, WAC)
            for h in range(8):
                ldc(WBB[:, h, :], w_br[512 + h * 64:512 + (h + 1) * 64, :], 64, 1024, WBB)
            for kt in range(8):
                ldc(WO[:, kt, :], w_out[kt * 128:(kt + 1) * 128, :], 128, 1024, WO)
            S.barrier()
            es2w.__exit__(None, None, None)

            XT2 = [sb(es2, "xt2", [128, D])] * 2
            HB2 = sb(es2, "hb2", [128, D], BF16)
            SSx = sb(es2, "ssx", [128, 8])
            RSx = sb(es2, "rsx", [128, 8])
            HT2 = sb(es2, "ht2", [128, 8, 128], BF16)
            QKV = sb(es2, "qkv", [128, 1280])
            GTH = sb(es2, "gth", [128, 3072], BF16)
            QKG = sb(es2, "qkg", [128, 10, 64])
            SS10 = sb(es2, "ss10", [128, 16])
            RS10 = sb(es2, "rs10", [128, 16])
            RT = sb(es2, "rt", [128, 4, 10, 8])
            QKB = sb(es2, "qkb", [128, 640], BF16)
            XQB = sb(es2, "xqb", [128, 512], BF16)
            XQT = sb(es2, "xqt", [128, 4, 128], BF16)
            ESK = sb(es2, "esk", [128, 8])
            OBT = sb(es2, "obt", [64, 8, 128], BF16)
            OCT = sb(es2, "oct", [128, 4, 128], BF16)
            OAT = sb(es2, "oat", [128, 4, 128], BF16)
            YPL = sb(es2, "ypl", [128, 512])
            BVL = sb(es2, "bvl", [128, 512])
            GOL = sb(es2, "gol", [128, 512])
            OAB = sb(es2, "oab", [128, 512], BF16)
            ST8 = [sb(es2, "st8_%d" % i, [128, 8]) for i in range(4)]
            MG = sb(es2, "mg", [128, 1024])
            SCR2 = MG
            MT_ = sb(es2, "mt_", [128, 512])
            MGB = sb(es2, "mgb", [128, 1024], BF16)
            MGT = sb(es2, "mgt", [128, 8, 128], BF16)
            XMo = sb(es2, "xmo", [128, D])
            GNE = sb(es2, "gne", [128, 1])
            S.cp("dve", QKG[:, 0:8, :], VB[:, 1024:1088].unsqueeze(1).to_broadcast([128, 8, 64]), r=[VB], w=[QKG])
            S.cp("dve", QKG[:, 8:10, :], VB[:, 1088:1152].unsqueeze(1).to_broadcast([128, 2, 64]), r=[VB], w=[QKG])
            S.act(ESK[:, :], VB[:, 1408:1416], AF.Exp, r=[VB], w=[ESK])

            def inproj_rest(x_src, xt, ropej):
                S.dma("sp", xt[:, :], x_src, w=[xt])
                norm_T(xt, 0, HT2, SCR2, HB2, SSx, RSx)
                for (lo, hi, dlo) in ((0, 512, 0), (512, 768, 512), (768, 1280, 768)):
                    p = psf()
                    for kt in range(8):
                        S.mm(p[:, 0:hi - lo], HT2[:, kt, :], WR[:, kt, lo:hi], start=(kt == 0), stop=(kt == 7), r=[HT2, WR], w=[p])
                    S.cp("act", QKV[:, dlo:dlo + hi - lo], p[:, 0:hi - lo], r=[p], w=[QKV])
                for gc in range(6):
                    p = psf()
                    for kt in range(8):
                        S.mm(p[:, :], HT2[:, kt, :], WR[:, kt, 1280 + gc * 512:1280 + (gc + 1) * 512], start=(kt == 0), stop=(kt == 7),
                             r=[HT2, WR], w=[p])
                    S.act(GTH[:, gc * 512:(gc + 1) * 512], p[:, :], AF.Tanh, scale=0.5, r=[p], w=[GTH])
                qk3 = QKV[:, 0:640].rearrange("p (a b) -> p a b", b=64)
                rstd_of(QKV[:, 0:640], 10, 64, EPS, SCR2, SS10, RS10, [QKV])
                S.tt("dve", qk3, qk3, RS10[:, 0:10].unsqueeze(2).to_broadcast([128, 10, 64]), ALU.mult, r=[QKV, RS10], w=[QKV])
                S.tt("dve", qk3, qk3, QKG[:, :, :], ALU.mult, r=[QKV, QKG], w=[QKV])
                cosb = ROPE[:, ropej, 0:8].unsqueeze(1).to_broadcast([128, 10, 8])
                sinb = ROPE[:, ropej, 8:16].unsqueeze(1).to_broadcast([128, 10, 8])
                x1, x2 = qk3[:, :, 0:8], qk3[:, :, 8:16]
                S.tt("dve", RT[:, 0, :, :], x1, cosb, ALU.mult, r=[QKV, ROPE], w=[RT])
                S.tt("dve", RT[:, 1, :, :], x2, sinb, ALU.mult, r=[QKV, ROPE], w=[RT])
                S.tt("dve", RT[:, 2, :, :], x2, cosb, ALU.mult, r=[QKV, ROPE], w=[RT])
                S.tt("dve", RT[:, 3, :, :], x1, sinb, ALU.mult, r=[QKV, ROPE], w=[RT])
                S.tt("dve", x1, RT[:, 0, :, :], RT[:, 1, :, :], ALU.subtract, r=[RT], w=[QKV])
                S.tt("dve", x2, RT[:, 2, :, :], RT[:, 3, :, :], ALU.add, r=[RT], w=[QKV])
                xq3 = QKV[:, 768:1280].rearrange("p (a b) -> p a b", b=128)
                rstd_of(QKV[:, 768:1280], 4, 128, EPS, SCR2, SS10, RS10, [QKV])
                S.tt("dve", xq3, xq3, RS10[:, 0:4].unsqueeze(2).to_broadcast([128, 4, 128]), ALU.mult, r=[QKV, RS10], w=[QKV])
                S.tt("dve", XQB[:, :].rearrange("p (a b) -> p a b", b=128), xq3, XQG.unsqueeze(1).to_broadcast([128, 4, 128]),
                     ALU.mult, r=[QKV, VB], w=[XQB])

            def rwkv_out(jj, with_state):
                S.dma("sp", YPL[:, :], sc_yp[jj], w=[YPL])
                S.dma("sp", BVL[:, :], sc_bv[jj], w=[BVL])
                S.dma("sp", GOL[:, :], sc_g[jj], w=[GOL])
                if with_state:
                    S.dma("sp", GTL[:, :, :].rearrange("p a b -> p (a b)"), sc_gt[jj], w=[GTL])
                    p = psf()
                    for h in range(8):
                        S.mm(p[:, h * 64:(h + 1) * 64], GTL[:, h, :], SINB[:, h, :], r=[GTL, SINB], w=[p])
                    S.tt("dve", XMo[:, 0:512], p[:, :], YPL[:, :], ALU.add, r=[p, YPL], w=[XMo])
                    ysrc = XMo
                else:
                    ysrc = YPL
                y3 = ysrc[:, 0:512].rearrange("p (a b) -> p a b", b=64)
                sm, sq, mn, vr = ST8
                S.red(sm[:, :], y3, r=[ysrc], w=[sm])
                S.act(XMo[:, 512:1024], ysrc[:, 0:512], AF.Square, r=[ysrc], w=[XMo])
                S.red(sq[:, :], XMo[:, 512:1024].rearrange("p (a b) -> p a b", b=64), r=[XMo], w=[sq])
                S.ts("dve", mn[:, :], sm[:, :], 1.0 / 64, ALU.mult, r=[sm], w=[mn])
                S.tt("dve", vr[:, :], mn[:, :], mn[:, :], ALU.mult, r=[mn], w=[vr])
                S.stt(vr[:, :], sq[:, :], 1.0 / 64, vr[:, :], ALU.mult, ALU.subtract, r=[sq, vr], w=[vr])
                S.ts("dve", vr[:, :], vr[:, :], GN_EPS, ALU.add, r=[vr], w=[vr])
                S.tt("pool", sq[:, :], vr[:, :], NEGH[:, 0:8], ALU.pow, r=[vr, NEGH], w=[sq])
                yq3 = XMo[:, 512:1024].rearrange("p (a b) -> p a b", b=64)
                S.tt("dve", yq3, y3, mn[:, :].unsqueeze(2).to_broadcast([128, 8, 64]), ALU.subtract, r=[ysrc, mn], w=[XMo])
                S.tt("dve", yq3, yq3, sq[:, :].unsqueeze(2).to_broadcast([128, 8, 64]), ALU.mult, r=[XMo, sq], w=[XMo])
                S.tt("pool", XMo[:, 512:1024], XMo[:, 512:1024], LN_W, ALU.mult, r=[XMo, VB], w=[XMo])
                S.tt("pool", XMo[:, 512:1024], XMo[:, 512:1024], LN_B, ALU.add, r=[XMo, VB], w=[XMo])
                S.tt("dve", XMo[:, 512:1024], XMo[:, 512:1024], BVL[:, :], ALU.add, r=[XMo, BVL], w=[XMo])
                S.tt("dve", OAB[:, :], XMo[:, 512:1024], GOL[:, :], ALU.mult, r=[XMo, GOL], w=[OAB])
                pb = psb()
                for i in range(4):
                    S.tr(pb[:, i * 128:(i + 1) * 128], OAB[:, i * 128:(i + 1) * 128], IDB[:], r=[OAB, IDB], w=[pb])
                S.cp("act", OAT[:, :, :].rearrange("p a b -> p (a b)"), pb[:, 0:512], r=[pb], w=[OAT])

            def xq_transposes():
                pb = psb()
                for h in range(4):
                    S.tr(pb[:, h * 128:(h + 1) * 128], XQB[:, h * 128:(h + 1) * 128], IDB[:], r=[XQB, IDB], w=[pb])
                S.cp("act", XQT[:, :, :].rearrange("p a b -> p (a b)"), pb[:, 0:512], r=[pb], w=[XQT])

            def merge_out(xt, xm_dst_ap, extra_r=()):
                for cc in range(2):
                    cs = slice(cc * 512, (cc + 1) * 512)
                    pa_, pb_, pc_ = psf(), psf(), psf()
                    for k in range(4):
                        S.mm(pa_[:, :], OAT[:, k, :], WAC[:, k, cs], start=(k == 0), stop=(k == 3), r=[OAT, WAC], w=[pa_])
                    for h in range(8):
                        S.mm(pb_[:, :], OBT[:, h, :], WBB[:, h, cs], start=(h == 0), stop=(h == 7), r=[OBT, WBB], w=[pb_])
                    for k in range(4):
                        S.mm(pc_[:, :], OCT[:, k, :], WAC[:, 4 + k, cs], start=(k == 0), stop=(k == 3), r=[OCT, WAC], w=[pc_])
                    S.stt(MG[:, cs], GTH[:, cc * 512:(cc + 1) * 512], 1.0, pa_[:, :], ALU.add, ALU.mult, r=[GTH, pa_], w=[MG.sub(cc)])
                    S.stt(MT_[:, :], GTH[:, 1024 + cc * 512:1024 + (cc + 1) * 512], 1.0, pb_[:, :], ALU.add, ALU.mult, r=[GTH, pb_], w=[MT_])
                    S.tt("pool", MG[:, cs], MG[:, cs], MT_[:, :], ALU.add, r=[MG.sub(cc), MT_], w=[MG.sub(cc)])
                    S.stt(MT_[:, :], GTH[:, 2048 + cc * 512:2048 + (cc + 1) * 512], 1.0, pc_[:, :], ALU.add, ALU.mult, r=[GTH, pc_], w=[MT_])
                    S.tt("pool", MGB[:, cs], MG[:, cs], MT_[:, :], ALU.add, r=[MG.sub(cc), MT_], w=[MGB])
                pb = psb()
                for kt in range(8):
                    S.tr(pb[:, kt * 128:(kt + 1) * 128], MGB[:, kt * 128:(kt + 1) * 128], IDB[:], r=[MGB, IDB], w=[pb])
                S.cp("act", MGT[:, :, :].rearrange("p a b -> p (a b)"), pb[:, :], r=[pb], w=[MGT])
                for cc in range(2):
                    cs = slice(cc * 512, (cc + 1) * 512)
                    p = psf()
                    for kt in range(8):
                        S.mm(p[:, :], MGT[:, kt, :], WO[:, kt, cs], start=(kt == 0), stop=(kt == 7), r=[MGT, WO], w=[p])
                    S.stt(XMo[:, cs], p[:, :], 0.5, xt[:, cs], ALU.mult, ALU.add, r=[p, xt], w=[XMo])
                S.dma("sp", xm_dst_ap, XMo[:, :], r=[XMo])

            GTL = sb(es2, "gtl", [64, 8, 128], BF16)

            es2p = ExitStack()
            with es2p:
                QT2 = sb(es2p, "qt2", [128, 4, 128], BF16)
                KT = [sb(es2p, "kt%d" % i, [128, 128], BF16) for i in range(2)]
                VV = [sb(es2p, "vv%d" % i, [128, 128], BF16) for i in range(2)]
                PE_ = sb(es2p, "pe_", [128, 512])
                PT_ = [[sb(es2p, "pt_%d_%d" % (g, b), [128, 512], BF16) for b in range(2)] for g in range(2)]
                PM = [sb(es2p, "pm%d" % i, [128, 512], BF16) for i in range(2)]
                RD = sb(es2p, "rd", [128, 512])
                MASKH = sb(es2p, "maskh", [128, 128])
                SNK = sb(es2p, "snk", [64, 8, 128])
                S.cp("dve", SNK[:, :, :], ESK[0:64, :].unsqueeze(2).to_broadcast([64, 8, 128]), r=[ESK], w=[SNK])
                S.ts("dve", MASKH[:, :], LOI, FLG[:, 0:1], ALU.mult, r=[MASK, FLG], w=[MASKH])

                def swa_kv_prep(cur):
                    S.cp("dve", QKB[:, 0:512].rearrange("p (h g d) -> p h g d", h=4, g=2),
                         QKV[:, 0:512].rearrange("p (g h d) -> p h g d", g=2, h=4), r=[QKV], w=[QKB])
                    S.cp("act", QKB[:, 512:640], QKV[:, 512:640], r=[QKV], w=[QKB])
                    S.cp("act", VV[cur][:, :], QKV[:, 640:768], r=[QKV], w=[VV[cur]])
                    pb = psb()
                    for i in range(5):
                        S.tr(pb[:, i * 128:(i + 1) * 128], QKB[:, i * 128:(i + 1) * 128], IDB[:], r=[QKB, IDB], w=[pb])
                    S.cp("act", QT2[:, :, :].rearrange("p a b -> p (a b)"), pb[:, 0:512], r=[pb], w=[QT2])
                    S.cp("act", KT[cur][:, :], pb[:, 512:640], r=[pb], w=[KT[cur]])

                for j in range(NP):
                    cur, prv = j % 2, (j - 1) % 2
                    xt = XT2[j % 2]
                    if j == 0:
                        S.dma("sp", xt[:, :], xp[0:128, :], w=[xt])
                        norm_T(xt, 0, HT2, SCR2, HB2, SSx, RSx)
                        p = psf()
                        for kt in range(8):
                            S.mm(p[:, 0:256], HT2[:, kt, :], WR[:, kt, 512:768], start=(kt == 0), stop=(kt == 7), r=[HT2, WR], w=[p])
                        S.cp("act", QKV[:, 512:768], p[:, 0:256], r=[p], w=[QKV])
                        S.memset("dve", QKV[:, 0:512], 0.0, w=[QKV])
                        S.memset("dve", QKV[:, 768:1280], 0.0, w=[QKV])
                        qk3 = QKV[:, 0:640].rearrange("p (a b) -> p a b", b=64)
                        rstd_of(QKV[:, 0:640], 10, 64, EPS, SCR2, SS10, RS10, [QKV])
                        S.tt("dve", qk3, qk3, RS10[:, 0:10].unsqueeze(2).to_broadcast([128, 10, 64]), ALU.mult, r=[QKV, RS10], w=[QKV])
                        S.tt("dve", qk3, qk3, QKG[:, :, :], ALU.mult, r=[QKV, QKG], w=[QKV])
                        cosb = ROPE[:, 0, 0:8].unsqueeze(1).to_broadcast([128, 10, 8])
                        sinb = ROPE[:, 0, 8:16].unsqueeze(1).to_broadcast([128, 10, 8])
                        x1, x2 = qk3[:, :, 0:8], qk3[:, :, 8:16]
                        S.tt("dve", RT[:, 0, :, :], x1, cosb, ALU.mult, r=[QKV, ROPE], w=[RT])
                        S.tt("dve", RT[:, 1, :, :], x2, sinb, ALU.mult, r=[QKV, ROPE], w=[RT])
                        S.tt("dve", RT[:, 2, :, :], x2, cosb, ALU.mult, r=[QKV, ROPE], w=[RT])
                        S.tt("dve", RT[:, 3, :, :], x1, sinb, ALU.mult, r=[QKV, ROPE], w=[RT])
                        S.tt("dve", x1, RT[:, 0, :, :], RT[:, 1, :, :], ALU.subtract, r=[RT], w=[QKV])
                        S.tt("dve", x2, RT[:, 2, :, :], RT[:, 3, :, :], ALU.add, r=[RT], w=[QKV])
                        swa_kv_prep(cur)
                        continue
                    jj = j - 1
                    inproj_rest(xp[j * 128:(j + 1) * 128, :], xt, j)
                    if j == NT:
                        S.dma("sp", swakp_o, QKV[:, 512:640], r=[QKV])
                        S.dma("sp", swavp_o, QKV[:, 640:768], r=[QKV])
                    swa_kv_prep(cur)
                    xq_transposes()
                    for g in range(2):
                        gs_ = slice(g * 64, (g + 1) * 64)
                        for bi, kb in enumerate((prv, cur)):
                            p = psf()
                            S.mm(p[:, :], KT[kb][gs_, :], QT2[gs_, :, :].rearrange("p a b -> p (a b)"), r=[KT[kb], QT2], w=[p])
                            S.act(PE_[:, :], p[:, :], AF.Exp, scale=0.125, r=[p], w=[PE_])
                            if bi == 0:
                                m_ap = (MASKH[:, :] if j == 1 else LOI)
                            else:
                                m_ap = UPI
                            S.tt("dve", PT_[g][bi][:, :].rearrange("p (a b) -> p a b", b=128), PE_[:, :].rearrange("p (a b) -> p a b", b=128),
                                 m_ap.unsqueeze(1).to_broadcast([128, 4, 128]), ALU.mult, r=[PE_, MASK, MASKH], w=[PT_[g][bi]])
                        po, psm = psf(), psf()
                        for bi, kb in enumerate((prv, cur)):
                            S.mm(po[0:64, :], VV[kb][:, gs_], PT_[g][bi][:, :], start=(bi == 0), stop=(bi == 1), r=[VV[kb], PT_[g][bi]], w=[po])
                        for bi in range(2):
                            S.mm(psm[0:64, :], ONESB[:, 0:64], PT_[g][bi][:, :], start=(bi == 0), stop=(bi == 1), r=[ONESB, PT_[g][bi]], w=[psm])
                        S.tt("dve", RD[0:64, :].rearrange("p (a b) -> p a b", b=128), psm[0:64, :].rearrange("p (a b) -> p a b", b=128),
                             SNK[:, g * 4:(g + 1) * 4, :], ALU.add, r=[psm, SNK], w=[RD])
                        S.op("dve", lambda e: e.reciprocal(RD[0:64, :], RD[0:64, :]), [RD], [RD])
                        S.tt("dve", OBT[:, g * 4:(g + 1) * 4, :].rearrange("p a b -> p (a b)"), po[0:64, :], RD[0:64, :], ALU.mult,
                             r=[po, RD], w=[OBT])
                    for mt in range(2):
                        p = psf()
                        for h in range(4):
                            S.mm(p[:, h * 128:(h + 1) * 128], MKT[:, h, mt * 128:(mt + 1) * 128], XQT[:, h, :], r=[MKT, XQT], w=[p])
                        S.act(PM[mt][:, :], p[:, :], AF.Exp, scale=float(128 ** -0.5), r=[p], w=[PM[mt]])
                    po, psm = psf(), psf()
                    for h in range(4):
                        for mt in range(2):
                            S.mm(po[:, h * 128:(h + 1) * 128], MVB[:, mt, h * 128:(h + 1) * 128], PM[mt][:, h * 128:(h + 1) * 128],
                                 start=(mt == 0), stop=(mt == 1), r=[MVB, PM[mt]], w=[po])
                    for mt in range(2):
                        S.mm(psm[:, :], ONESB[:, :], PM[mt][:, :], start=(mt == 0), stop=(mt == 1), r=[ONESB, PM[mt]], w=[psm])
                    S.op("dve", lambda e, p_=psm: e.reciprocal(RD[:, :], p_[:, :]), [psm], [RD])
                    S.tt("dve", OCT[:, :, :].rearrange("p a b -> p (a b)"), po[:, :], RD[:, :], ALU.mult, r=[po, RD], w=[OCT])
                    rwkv_out(jj, True)
                    merge_out(xt, sc_xm[jj])
            S.barrier()
            es2s = ExitStack()
            with es2s:
                SELT = sb(es2s, "selt", [16, 128])
                SEL2 = sb(es2s, "sel2", [128, 16])
                S.dma("sp", SELT[:, :], selt, w=[SELT])
                S.dma("sp", SEL2[:, :], sel2, w=[SEL2])
                KC = sb(es2s, "kc", [128, 16, 128])
                VC = sb(es2s, "vc", [128, 16, 128])
                QR = MG
                SC_ = sb(es2s, "sc_", [128, 8, 16])
                OP_ = sb(es2s, "op_", [128, 520])
                PN = sb(es2s, "pn", [128, 8])
                SN = sb(es2s, "sn", [128, 8])
                DEN = sb(es2s, "den", [128, 8])
                TPn = sb(es2s, "tpn", [128, 8, 64])
                OBS = YPL
                OBSb = OAB
                SCm = sb(es2s, "scm", [128, 4, 32])
                OPs = sb(es2s, "ops", [128, 4])
                OPr = MT_
                OPm = GOL
                TPk3 = XMo[:, :].rearrange("p (a b) -> p a b", b=64)
                TPk = XMo
                TPm3 = BVL[:, :].rearrange("p (a b) -> p a b", b=128)
                TPm = BVL
                KM3 = KC[:, :, :].rearrange("p a b -> p (a b)").rearrange("p (a b) -> p a b", b=512)
                VM3 = VC[:, :, :].rearrange("p a b -> p (a b)").rearrange("p (a b) -> p a b", b=512)
                xt = XT2[0]
                S.dma("sp", KC[:, :, :].rearrange("p a b -> p (a b)"), s_swak.rearrange("b (g i) c -> (b g) (i c)", i=16), w=[KC])
                S.dma("sp", VC[:, :, :].rearrange("p a b -> p (a b)"), s_swav.rearrange("b (g i) c -> (b g) (i c)", i=16), w=[VC])
                S.dma("sp", swaks_o[:, 0:127, :], s_swak[:, 1:128, :])
                S.dma("sp", swavs_o[:, 0:127, :], s_swav[:, 1:128, :])
                inproj_rest(xs, xt, NT + 1)
                S.dma("sp", swaks_o[:, 127, :], QKV[0:16, 512:640], r=[QKV])
                S.dma("sp", swavs_o[:, 127, :], QKV[0:16, 640:768], r=[QKV])
                xq_transposes()
                S.cp("dve", SCR2[0:16, 0:512], QKV[0:16, 0:512], r=[QKV], w=[SCR2])
                S.cp("dve", SCR2[0:16, 512:1024], XQB[0:16, :], r=[XQB], w=[SCR2])
                for hf in range(2):
                    p = psf()
                    S.mm(p[:, :], SELT[:, :], SCR2[0:16, hf * 512:(hf + 1) * 512], r=[SELT, SCR2], w=[p])
                    S.cp("act", QR[:, hf * 512:(hf + 1) * 512], p[:, :], r=[p], w=[QR])
                for h in range(8):
                    kv = h // 4
                    S.tt("dve", TPk3, KC[:, :, kv * 64:(kv + 1) * 64],
                         QR[:, h * 64:(h + 1) * 64].unsqueeze(1).to_broadcast([128, 16, 64]), ALU.mult, r=[KC, QR], w=[TPk])
                    S.red(SC_[:, h, :], TPk3, r=[TPk], w=[SC_])
                S.act(SC_[:, :, :], SC_[:, :, :], AF.Exp, scale=0.125, r=[SC_], w=[SC_])
                S.red(OP_[:, 512:520], SC_[:, :, :], r=[SC_], w=[OP_.sub("s")])
                for h in range(8):
                    kv = h // 4
                    S.tt("dve", TPk3, VC[:, :, kv * 64:(kv + 1) * 64],
                         SC_[:, h, :].unsqueeze(2).to_broadcast([128, 16, 64]), ALU.mult, r=[VC, SC_], w=[TPk])
                    S.red(OP_[:, h * 64:(h + 1) * 64], TPk3.rearrange("p i d -> p d i"), r=[TPk], w=[OP_.sub(h)])
                po, psm = psf(), psf()
                S.mm(po[0:16, :], SEL2[:, :], OP_[:, 0:512], r=[SEL2] + [OP_.sub(h) for h in range(8)], w=[po])
                S.mm(psm[0:16, 0:8], SEL2[:, :], OP_[:, 512:520], r=[SEL2, OP_.sub("s")], w=[psm])
                q3 = QKV[:, 0:512].rearrange("p (a b) -> p a b", b=64)
                for g in range(2):
                    S.tt("dve", TPn[:, g * 4:(g + 1) * 4, :], q3[:, g * 4:(g + 1) * 4, :],
                         QKV[:, 512 + g * 64:512 + (g + 1) * 64].unsqueeze(1).to_broadcast([128, 4, 64]), ALU.mult, r=[QKV], w=[TPn])
                S.red(SN[:, :], TPn[:, :, :], r=[TPn], w=[SN])
                S.act(PN[:, :], SN[:, :], AF.Exp, scale=0.125, r=[SN], w=[PN])
                S.tt("dve", DEN[:, :], PN[:, :], ESK[:, :], ALU.add, r=[PN, ESK], w=[DEN])
                S.tt("dve", DEN[0:16, :], DEN[0:16, :], psm[0:16, 0:8], ALU.add, r=[DEN, psm], w=[DEN])
                S.op("dve", lambda e: e.reciprocal(DEN[:, :], DEN[:, :]), [DEN], [DEN])
                for g in range(2):
                    S.tt("dve", TPn[:, g * 4:(g + 1) * 4, :], PN[:, g * 4:(g + 1) * 4].unsqueeze(2).to_broadcast([128, 4, 64]),
                         QKV[:, 640 + g * 64:640 + (g + 1) * 64].unsqueeze(1).to_broadcast([128, 4, 64]), ALU.mult, r=[PN, QKV], w=[TPn])
                S.cp("dve", OBS[:, :], TPn[:, :, :].rearrange("p a b -> p (a b)"), r=[TPn], w=[OBS])
                S.tt("dve", OBS[0:16, :], OBS[0:16, :], po[0:16, :], ALU.add, r=[OBS, po], w=[OBS])
                S.tt("dve", OBSb[:, :].rearrange("p (a b) -> p a b", b=64), OBS[:, :].rearrange("p (a b) -> p a b", b=64),
                     DEN[:, :].unsqueeze(2).to_broadcast([128, 8, 64]), ALU.mult, r=[OBS, DEN], w=[OBSb])
                pb = psb()
                for h in range(8):
                    S.tr(pb[0:64, h * 128:(h + 1) * 128], OBSb[:, h * 64:(h + 1) * 64], IDB[:], r=[OBSb, IDB], w=[pb])
                S.cp("act", OBT[:, :, :].rearrange("p a b -> p (a b)"), pb[0:64, :], r=[pb], w=[OBT])
                mk4 = s_memk.rearrange("b (g r i) c -> (b g) r (i c)", g=8, r=8, i=4)
                mv4 = s_memv.rearrange("b (g r i) c -> (b g) r (i c)", g=8, r=8, i=4)
                for r_ in range(8):
                    S.dma("sp", KC[:, :, :].rearrange("p a b -> p (a b)"), mk4[:, r_, :], w=[KC])
                    for h in range(4):
                        S.tt("dve", TPm3, KM3[:, :, h * 128:(h + 1) * 128],
                             QR[:, 512 + h * 128:512 + (h + 1) * 128].unsqueeze(1).to_broadcast([128, 4, 128]), ALU.mult, r=[KC, QR], w=[TPm])
                        S.red(SCm[:, h, r_ * 4:(r_ + 1) * 4], TPm3, r=[TPm], w=[SCm])
                S.act(SCm[:, :, :], SCm[:, :, :], AF.Exp, scale=float(128 ** -0.5), r=[SCm], w=[SCm])
                S.red(OPs[:, :], SCm[:, :, :], r=[SCm], w=[OPs])
                for r_ in range(8):
                    S.dma("sp", VC[:, :, :].rearrange("p a b -> p (a b)"), mv4[:, r_, :], w=[VC])
                    for h in range(4):
                        S.tt("dve", TPm3, VM3[:, :, h * 128:(h + 1) * 128],
                             SCm[:, h, r_ * 4:(r_ + 1) * 4].unsqueeze(2).to_broadcast([128, 4, 128]), ALU.mult, r=[VC, SCm], w=[TPm])
                        dst = OPm if r_ == 0 else OPr
                        S.red(dst[:, h * 128:(h + 1) * 128], TPm3.rearrange("p i d -> p d i"), r=[TPm], w=[dst])
                    if r_ > 0:
                        S.tt("dve", OPm[:, 0:512], OPm[:, 0:512], OPr[:, :], ALU.add, r=[OPm, OPr], w=[OPm])
                po, psm = psf(), psf()
                S.mm(po[0:16, :], SEL2[:, :], OPm[:, 0:512], r=[SEL2, OPm], w=[po])
                S.mm(psm[0:16, 0:4], SEL2[:, :], OPs[:, :], r=[SEL2, OPs], w=[psm])
                S.memset("dve", DEN[:, :], 1.0, w=[DEN])
                S.cp("dve", DEN[0:16, 0:4], psm[0:16, 0:4], r=[psm], w=[DEN])
                S.op("dve", lambda e: e.reciprocal(DEN[:, 0:4], DEN[:, 0:4]), [DEN], [DEN])
                S.memset("dve", OBS[:, :], 0.0, w=[OBS])
                S.cp("dve", OBS[0:16, :], po[0:16, :], r=[po], w=[OBS])
                S.tt("dve", OBSb[:, :].rearrange("p (a b) -> p a b", b=128), OBS[:, :].rearrange("p (a b) -> p a b", b=128),
                     DEN[:, 0:4].unsqueeze(2).to_broadcast([128, 4, 128]), ALU.mult, r=[OBS, DEN], w=[OBSb])
                pb = psb()
                for h in range(4):
                    S.tr(pb[:, h * 128:(h + 1) * 128], OBSb[:, h * 128:(h + 1) * 128], IDB[:], r=[OBSb, IDB], w=[pb])
                S.cp("act", OCT[:, :, :].rearrange("p a b -> p (a b)"), pb[:, 0:512], r=[pb], w=[OCT])
                rwkv_out(NT, False)
                merge_out(xt, sc_xm[NT])
        S.barrier()

        es3 = ExitStack()
        with es3:
            WUP = sb(es3, "wup", [128, 8, 4096], BF16)
            WDN = sb(es3, "wdn", [128, 32, 1024], BF16)
            es3w = ExitStack()
            es3w.__enter__()
            STG3 = [sb(es3w, "stg3_%d" % i, [128, 2048]) for i in range(4)]
            s3i = [0]

            def ld3(dst_ap, src_ap, ncol, wtok):
                st = STG3[s3i[0] % 4]
                ce = ("pool", "act", "dve")[s3i[0] % 3]
                s3i[0] += 1
                S.dma("sp", st[:, 0:ncol], src_ap, w=[st])
                S.cp(ce, dst_ap, st[:, 0:ncol], r=[st], w=[wtok])

            for kt in range(8):
                for hf in range(2):
                    ld3(WUP[:, kt, hf * 2048:(hf + 1) * 2048], w_up[kt * 128:(kt + 1) * 128, hf * 2048:(hf + 1) * 2048], 2048, WUP)
            for fc in range(32):
                ld3(WDN[:, fc, :], w_down[fc * 128:(fc + 1) * 128, :], 1024, WDN)
            S.barrier()
            es3w.__exit__(None, None, None)
            XM3 = [sb(es3, "xm3_%d" % i, [128, D]) for i in range(2)]
            SCR3 = sb(es3, "scr3", [128, D])
            HB3 = sb(es3, "hb3", [128, D], BF16)
            SS3 = sb(es3, "ss3", [128, 8])
            RS3 = sb(es3, "rs3", [128, 8])
            H2T = sb(es3, "h2t", [128, 8, 256], BF16)
            S.memset("dve", H2T[:, :, :], 0.0, w=[H2T])
            RL = sb(es3, "rl", [128, 512])
            HID = sb(es3, "hid", [128, 32, 256], BF16)
            YO = [sb(es3, "yo%d" % i, [128, D]) for i in range(2)]
            for t0 in range(0, NT + 1, 2):
                tiles = [t for t in (t0, t0 + 1) if t <= NT]
                for ti, t in enumerate(tiles):
                    xm = XM3[t % 2]
                    S.dma("sp", xm[:, :], sc_xm[t], w=[xm])
                    norm_T(xm, 1, H2T, SCR3, HB3, SS3, RS3, dst=H2T[:, :, ti * 128:(ti + 1) * 128])
                for f2 in range(16):
                    p = psf()
                    for fi in range(2):
                        fc = f2 * 2 + fi
                        for kt in range(8):
                            S.mm(p[:, fi * 256:(fi + 1) * 256], WUP[:, kt, fc * 128:(fc + 1) * 128], H2T[:, kt, :],
                                 start=(kt == 0), stop=(kt == 7), r=[WUP, H2T], w=[p])
                    S.act(RL[:, :], p[:, :], AF.Relu, r=[p], w=[RL])
                    S.tt("dve" if f2 % 2 == 0 else "pool", HID[:, f2 * 2:(f2 + 1) * 2, :].rearrange("p a b -> p (a b)"), RL[:, :], RL[:, :],
                         ALU.mult, r=[RL], w=[HID.sub(f2)])
                for ti, t in enumerate(tiles):
                    xm = XM3[t % 2]
                    yo = YO[t % 2]
                    for cc in range(2):
                        cs = slice(cc * 512, (cc + 1) * 512)
                        p = psf()
                        for fc in range(32):
                            S.mm(p[:, :], HID[:, fc, ti * 128:(ti + 1) * 128], WDN[:, fc, cs], start=(fc == 0), stop=(fc == 31),
                                 r=[HID.sub(fc // 2), WDN], w=[p])
                        S.tt("dve", yo[:, cs], p[:, :], xm[:, cs], ALU.add, r=[p, xm], w=[yo])
                    if t < NT:
                        S.dma("sp", y_o[t * 128:(t + 1) * 128, :], yo[:, :], r=[yo])
                    else:
                        S.dma("sp", ys_o, yo[0:16, :], r=[yo])
        S.barrier()
        print("total ops", S.gseq, {e: len(S.streams[e]) for e in S.ENG}, flush=True)
        import os
        if os.environ.get("KLOG"):
            for g, e, ln in S.oplog[:int(os.environ["KLOG"])]:
                print(g, e, "line", ln)
        S.emit()
    return nc


def build_rest(nc, S, es, L):
    pass


def _host_inputs(NT, c, I):
    f32 = np.float32
    SEQ = I["x_prompt"].shape[1]
    seq, pos = c // 4, c % 4
    t0 = pos * NT * 128
    xp = np.zeros(((NT + 1) * 128, D), f32)
    if pos > 0:
        xp[:] = I["x_prompt"][seq, t0 - 128:t0 + NT * 128]
    else:
        xp[128:] = I["x_prompt"][seq, 0:NT * 128]
    xs = np.zeros((128, D), f32)
    xs[:16] = I["x_sample"][16 * c:16 * c + 16, 0]
    p = np.arange(128)
    ups = (p[:, None] < p[None, :]).astype(f32)
    upi = (p[:, None] <= p[None, :]).astype(f32)
    cmask = np.stack([ups, upi, ups.T.copy(), upi.T.copy(), np.eye(128, dtype=f32)], axis=1)
    ebias = np.stack([-C0H * (p + 1), -C0H * p, C0H * (p + 1), -C0H * (127 - p)], axis=1).astype(f32)
    half = 8
    inv_freq = np.power(np.float32(500000.0), -np.arange(half, dtype=f32) * np.float32(2.0 / 16)).astype(f32)
    rope = np.zeros((128, NT + 2, 16), f32)
    for j in range(NT + 2):
        if j <= NT:
            posj = (t0 - 128 + j * 128 + p).astype(f32)
        else:
            posj = np.full(128, 8192, f32)
        ang = posj[:, None] * inv_freq[None, :]
        rope[:, j, 0:8] = np.cos(ang)
        rope[:, j, 8:16] = np.sin(ang)
    flags = np.zeros((128, 4), f32)
    flags[:, 0] = 1.0 if pos > 0 else 0.0
    for q in range(3):
        flags[:, 1 + q] = 1.0 if q < pos else 0.0
    gains = np.stack([I["norm_mix"][0].reshape(8, 128).T, I["norm_ffn"][0].reshape(8, 128).T,
                      I["mem_norm"][0].reshape(8, 128).T], axis=1).astype(f32)
    vecA = np.concatenate([I["rw_mu"][0], I["rw_w0"][0], I["rw_a0"][0], I["rw_k_k"][0], I["rw_k_a"][0],
                           I["rw_r_k"][0].reshape(-1)])[None, :].astype(f32)
    vecB = np.concatenate([I["rw_ln_w"][0], I["rw_ln_b"][0], I["q_norm"][0], I["k_norm"][0], I["xq_norm"][0],
                           I["xk_norm"][0], I["swa_sinks"][0]])[None, :].astype(f32)
    b0 = 16 * c
    m = {
        "xp": xp, "xs": xs, "cmask": np.ascontiguousarray(cmask), "ebias": ebias, "rope": rope, "flags": flags,
        "gains": np.ascontiguousarray(gains), "vecA": vecA, "vecB": vecB,
        "s_state": I["state_rwkv"][0, b0:b0 + 16].reshape(128, 4096),
        "s_shift": I["state_rwkv_shift"][0, b0:b0 + 16],
        "s_swak": I["cache_swa_k"][0, b0:b0 + 16].reshape(16, 128, 128),
        "s_swav": I["cache_swa_v"][0, b0:b0 + 16].reshape(16, 128, 128),
        "selt": (np.arange(128)[None, :] // 8 == np.arange(16)[:, None]).astype(f32),
        "sel2": (np.arange(128)[:, None] // 8 == np.arange(16)[None, :]).astype(f32),
        "s_memk": I["cache_mem_k"][0, b0:b0 + 16].reshape(16, 256, 512),
        "s_memv": I["cache_mem_v"][0, b0:b0 + 16].reshape(16, 256, 512),
        "memp": I["mem_prompt"][seq],
        "w_in": I["w_in"][0], "w2a": np.concatenate([I["rw_w2"][0], I["rw_a2"][0]], axis=0), "g2": I["rw_g2"][0],
        "w_mkv": I["w_mem_kv"][0],
        "w_br": np.concatenate([I["w_br_a"][0], I["w_br_b"][0], I["w_br_c"][0]], axis=0),
        "w_out": I["w_out"][0], "w_up": I["w_up"][0], "w_down": I["w_down"][0],
    }
    return {k: np.ascontiguousarray(v, dtype=f32) for k, v in m.items()}


_NC_CACHE = {}


def kernel(**inputs):
    I = {k: np.asarray(v) for k, v in inputs.items()}
    B, SEQ, _ = I["x_prompt"].shape
    NT = SEQ // (4 * 128)
    if NT not in _NC_CACHE:
        _NC_CACHE[NT] = build_nc(NT)
    nc = _NC_CACHE[NT]
    in_maps = [_host_inputs(NT, c, I) for c in range(NCORES)]
    res = run_bass_kernel_spmd(nc, in_maps, core_ids=list(range(NCORES)))
    return assemble(res.results, NT)


def assemble(R, NT):
    f32 = np.float32
    SEQ = 4 * NT * 128
    y_prompt = np.zeros((2, SEQ, D), f32)
    for c in range(8):
        y_prompt[c // 4, (c % 4) * NT * 128:(c % 4 + 1) * NT * 128] = R[c]["y"].reshape(NT * 128, D)
    y_sample = np.concatenate([R[c]["ys"].reshape(16, D) for c in range(8)], axis=0).reshape(128, 1, D)
    st_p = np.stack([R[3]["stp"].reshape(8, 64, 64), R[7]["stp"].reshape(8, 64, 64)])[None]
    shift_p = np.stack([R[3]["zlast"].reshape(-1), R[7]["zlast"].reshape(-1)])[None]
    swak_p = np.stack([R[3]["swakp"], R[7]["swakp"]]).reshape(1, 2, 128, 2, 64)
    swav_p = np.stack([R[3]["swavp"], R[7]["swavp"]]).reshape(1, 2, 128, 2, 64)
    memk_p = np.stack([R[0]["memk"], R[4]["memk"]]).reshape(1, 2, 256, 4, 128)
    memv_p = np.stack([R[0]["memv"], R[4]["memv"]]).reshape(1, 2, 256, 4, 128)
    st_s = np.concatenate([R[c]["sts"].reshape(16, 8, 64, 64) for c in range(8)], axis=0)[None]
    shift_s = np.concatenate([R[c]["shifts"].reshape(16, 1792) for c in range(8)], axis=0)[None]
    swak_s = np.concatenate([R[c]["swaks"].reshape(16, 128, 2, 64) for c in range(8)], axis=0)[None]
    swav_s = np.concatenate([R[c]["swavs"].reshape(16, 128, 2, 64) for c in range(8)], axis=0)[None]
    outs = (y_prompt, y_sample, st_p, shift_p, swak_p, swav_p, memk_p, memv_p, st_s, shift_s, swak_s, swav_s)
    return tuple(np.ascontiguousarray(o, dtype=f32) for o in outs)
```

```python
import numpy as np
from contextlib import ExitStack
import concourse.bass as bass
import concourse.mybir as mybir
from concourse.bass_utils import run_bass_kernel_spmd

F32 = mybir.dt.float32
BF16 = mybir.dt.bfloat16
ALU = mybir.AluOpType
AF = mybir.ActivationFunctionType
AX = mybir.AxisListType

D = 1024
NCORES = 8
C0H = float(np.exp(-0.5) / 2.0)
EPS = 1e-5
GN_EPS = 64e-5
SAFE_OPS = 10 ** 9


class Tok:
    __slots__ = ("w", "r")

    def __init__(self):
        self.w = None
        self.r = {}


class Tile:
    def __init__(self, h):
        self.h = h
        self.tok = Tok()
        self.subs = {}

    def sub(self, key):
        if key not in self.subs:
            self.subs[key] = Tok()
        return self.subs[key]

    def __getitem__(self, k):
        return self.h[k]


def _tok(x):
    return x.tok if isinstance(x, Tile) else x


class Sched:
    ENG = ("pe", "act", "dve", "pool", "sp")

    def __init__(self, nc, es, n_dsem=12):
        self.nc = nc
        self.h = {"pe": nc.tensor, "act": nc.scalar, "dve": nc.vector, "pool": nc.gpsimd, "sp": nc.sync}
        self.streams = {e: [] for e in self.ENG}
        self.cnt = {e: 0 for e in self.ENG}
        self.waited = {e: {} for e in self.ENG}
        self.esem = {e: es.enter_context(nc.semaphore("es_" + e)) for e in self.ENG}
        self.dsem = {}
        self.dcnt = {}
        self.dnext = {}
        for q in ("sp", "pool", "act"):
            self.dsem[q] = [es.enter_context(nc.semaphore("ds_%s%d" % (q, i))) for i in range(n_dsem)]
            self.dcnt[q] = [0] * n_dsem
            self.dnext[q] = 0
        self.ccsem = es.enter_context(nc.semaphore("ccsem"))
        self.gseq = 0
        self.oplog = []
        self.gidx = {e: [] for e in self.ENG}

    def _semobj(self, key):
        if key[0] == "e":
            return self.esem[key[1]]
        if key[0] == "d":
            return self.dsem[key[1]][key[2]]
        return self.ccsem

    def _resolve(self, eng, deps):
        waits = []
        best = {}
        for (key, val) in deps:
            if key == ("e", eng) and eng == "pe":
                continue
            if best.get(key, 0) < val:
                best[key] = val
        for key, val in best.items():
            if self.waited[eng].get(key, 0) < val:
                self.waited[eng][key] = val
                waits.append((self._semobj(key), val))
        return waits

    def _deps(self, reads, writes):
        deps = []
        for t in reads:
            t = _tok(t)
            if t.w is not None:
                deps.append(t.w)
        for t in writes:
            t = _tok(t)
            if t.w is not None:
                deps.append(t.w)
            deps.extend(t.r.items())
        return deps

    def _mark(self, me, reads, writes):
        key, val = me
        for t in reads:
            t = _tok(t)
            if t.r.get(key, 0) < val:
                t.r[key] = val
        for t in writes:
            t = _tok(t)
            t.w = me
            t.r = {}

    def op(self, eng, fn, r=(), w=()):
        waits = self._resolve(eng, self._deps(r, w))
        self.cnt[eng] += 1
        me = (("e", eng), self.cnt[eng])
        self._mark(me, r, w)
        self.streams[eng].append((waits, fn, self.esem[eng], 1))
        self.gseq += 1
        self.gidx[eng].append(self.gseq)
        self._log(eng)

    def _log(self, eng):
        import sys
        f = sys._getframe(2)
        while f.f_code.co_name in ("mm", "tr", "act", "tt", "ts", "stt", "red", "cp", "memset", "op", "dma"):
            f = f.f_back
        self.oplog.append((self.gseq, eng, f.f_lineno))

    def dma(self, q, out, in_, r=(), w=()):
        import os
        if q == "pool" and os.environ.get("KNOPOOL"):
            return
        i = self.dnext[q]
        self.dnext[q] = (i + 1) % len(self.dsem[q])
        deps = self._deps(r, w)
        if self.dcnt[q][i] > 0:
            deps.append((("d", q, i), self.dcnt[q][i]))
        waits = self._resolve(q, deps)
        self.dcnt[q][i] += 16
        me = (("d", q, i), self.dcnt[q][i])
        self._mark(me, r, w)
        self.streams[q].append((waits, lambda e, o=out, s=in_: e.dma_start(out=o, in_=s), self.dsem[q][i], 16))
        self.gseq += 1
        self.gidx[q].append(self.gseq)
        self._log("dma-" + q)

    def barrier(self, toks=()):
        deps = []
        for e in self.ENG:
            if self.cnt[e] > 0:
                deps.append((("e", e), self.cnt[e]))
        for q in self.dsem:
            for i, c in enumerate(self.dcnt[q]):
                if c > 0:
                    deps.append((("d", q, i), c))
        for e in self.ENG:
            waits = self._resolve(e, deps)
            if waits:
                self.streams[e].append((waits, None, None, 0))
                self.gidx[e].append(self.gseq)

    def emit(self):
        import os
        nc = self.nc
        lim = int(os.environ.get("KSTOP", "0")) or SAFE_OPS
        totals = {}
        for name in self.ENG:
            for (waits, fn, sem, inc), gi in zip(self.streams[name], self.gidx[name]):
                if gi > lim or fn is None:
                    continue
                k = id(sem)
                totals[k] = (sem, totals.get(k, (sem, 0))[1] + inc)
        with nc.Block() as block:
            def runner(name):
                def run(eng):
                    for (waits, fn, sem, inc), gi in zip(self.streams[name], self.gidx[name]):
                        if gi > lim:
                            break
                        for s, v in waits:
                            eng.wait_ge(s, v)
                        if fn is not None:
                            fn(eng).then_inc(sem, inc)
                    for s, v in totals.values():
                        eng.wait_ge(s, v)
                return run
            block.tensor(runner("pe"))
            block.scalar(runner("act"))
            block.vector(runner("dve"))
            block.gpsimd(runner("pool"))
            block.sync(runner("sp"))

    def mm(self, out, lhsT, rhs, start=True, stop=True, r=(), w=()):
        self.op("pe", lambda e: e.matmul(out, lhsT, rhs, start=start, stop=stop), r, w)

    def tr(self, out, in_, ident, r=(), w=()):
        self.op("pe", lambda e: e.transpose(out, in_, ident), r, w)

    def act(self, out, in_, func, bias=None, scale=None, r=(), w=()):
        kw = {}
        if bias is not None:
            kw["bias"] = bias
        if scale is not None:
            kw["scale"] = scale
        self.op("act", lambda e: e.activation(out, in_, func, **kw), r, w)

    def tt(self, eng, out, in0, in1, op, r=(), w=()):
        self.op(eng, lambda e: e.tensor_tensor(out, in0, in1, op), r, w)

    def ts(self, eng, out, in0, s1, op0, s2=None, op1=None, r=(), w=()):
        if op1 is None:
            self.op(eng, lambda e: e.tensor_scalar(out, in0, s1, None, op0), r, w)
        else:
            self.op(eng, lambda e: e.tensor_scalar(out, in0, s1, s2, op0, op1), r, w)

    def stt(self, out, in0, scalar, in1, op0, op1, r=(), w=()):
        self.op("dve", lambda e: e.scalar_tensor_tensor(out, in0, scalar, in1, op0, op1), r, w)

    def red(self, out, in_, r=(), w=(), op=ALU.add):
        self.op("dve", lambda e: e.tensor_reduce(out, in_, AX.X, op), r, w)

    def cp(self, eng, out, in_, r=(), w=()):
        if eng == "act":
            self.op("act", lambda e: e.activation(out, in_, AF.Copy), r, w)
        else:
            self.op(eng, lambda e: e.tensor_copy(out, in_), r, w)

    def memset(self, eng, ap, val, w=()):
        self.op(eng, lambda e: e.memset(ap, val), (), w)


def build_nc(NT):
    nc = bass.Bass("TRN2", target_bir_lowering=False)
    NP = NT + 1
    NR = NT + 2

    def din(name, shape):
        return nc.dram_tensor(name, list(shape), F32, kind="ExternalInput").ap()

    def dout(name, shape):
        return nc.dram_tensor(name, list(shape), F32, kind="ExternalOutput").ap()

    xp = din("xp", [NP * 128, D])
    xs = din("xs", [128, D])
    cmask = din("cmask", [128, 5, 128])
    ebias = din("ebias", [128, 4])
    rope = din("rope", [128, NR, 16])
    flags = din("flags", [128, 4])
    gains = din("gains", [128, 3, 8])
    vecA = din("vecA", [1, 4352])
    vecB = din("vecB", [1, 1416])
    s_state = din("s_state", [128, 4096])
    s_shift = din("s_shift", [16, 1792])
    s_swak = din("s_swak", [16, 128, 128])
    s_swav = din("s_swav", [16, 128, 128])
    selt = din("selt", [16, 128])
    sel2 = din("sel2", [128, 16])
    s_memk = din("s_memk", [16, 256, 512])
    s_memv = din("s_memv", [16, 256, 512])
    memp = din("memp", [256, D])
    w_in = din("w_in", [D, 6144])
    w2a = din("w2a", [128, 512])
    g2 = din("g2", [128, 512])
    w_mkv = din("w_mkv", [D, 1024])
    w_br = din("w_br", [1536, D])
    w_out = din("w_out", [D, D])
    w_up = din("w_up", [D, 4096])
    w_down = din("w_down", [4096, D])

    y_o = dout("y", [NT * 128, D])
    ys_o = dout("ys", [16, D])
    stp_o = dout("stp", [8, 64, 64])
    zlast_o = dout("zlast", [1, 1792])
    swakp_o = dout("swakp", [128, 128])
    swavp_o = dout("swavp", [128, 128])
    memk_o = dout("memk", [256, 512])
    memv_o = dout("memv", [256, 512])
    sts_o = dout("sts", [128, 4096])
    shifts_o = dout("shifts", [16, 1792])
    swaks_o = dout("swaks", [16, 128, 128])
    swavs_o = dout("swavs", [16, 128, 128])

    sc_yp = nc.dram_tensor("sc_yp", [NT + 1, 128, 512], F32).ap()
    sc_bv = nc.dram_tensor("sc_bv", [NT + 1, 128, 512], F32).ap()
    sc_g = nc.dram_tensor("sc_g", [NT + 1, 128, 512], F32).ap()
    sc_gt = nc.dram_tensor("sc_gt", [NT, 64, 1024], BF16).ap()
    sc_xm = nc.dram_tensor("sc_xm", [NT + 1, 128, D], F32).ap()
    sc_s1 = nc.dram_tensor("sc_s1", [6, 16, 512], F32).ap()
    sc_s2 = nc.dram_tensor("sc_s2", [16, 512], F32).ap()
    sc_q = nc.dram_tensor("sc_q", [16, 1024], F32).ap()
    cc_src = nc.dram_tensor("cc_src", [64, 1024], F32)
    cc_dst = nc.dram_tensor("cc_dst", [4 * 64, 1024], F32)

    es = ExitStack()
    with es:
        S = Sched(nc, es)

        uid = [0]

        def sb(es_, name, shape, dt=F32):
            uid[0] += 1
            return Tile(es_.enter_context(nc.sbuf_tensor("t%d_%s" % (uid[0], name), list(shape), dt)))

        def pst(es_, name, shape, dt=F32):
            return Tile(es_.enter_context(nc.psum_tensor("p_" + name, list(shape), dt)))

        PF = [pst(es, "pf%d" % i, [128, 512], F32) for i in range(6)]
        PB = [pst(es, "pb%d" % i, [128, 1024], BF16) for i in range(2)]
        pfi = [0]
        pbi = [0]

        def psf():
            t = PF[pfi[0] % 5]
            pfi[0] += 1
            return t

        def psb():
            t = PB[pbi[0] % 2]
            pbi[0] += 1
            return t

        MASK = sb(es, "mask", [128, 5, 128])
        EB = sb(es, "eb", [128, 4])
        ROPE = sb(es, "rope", [128, NR, 16])
        FLG = sb(es, "flg", [128, 4])
        GAIN = sb(es, "gain", [128, 3, 8])
        IDB = sb(es, "idb", [128, 128], BF16)
        NEGH = sb(es, "negh", [128, 16])
        ONES2 = sb(es, "ones2", [128, 2])
        ONESB = sb(es, "onesb", [128, 128], BF16)
        S.dma("sp", MASK[:], cmask, w=[MASK])
        S.dma("sp", EB[:], ebias, w=[EB])
        S.dma("sp", ROPE[:], rope, w=[ROPE])
        S.dma("sp", FLG[:], flags, w=[FLG])
        S.dma("sp", GAIN[:], gains, w=[GAIN])
        S.cp("dve", IDB[:], MASK[:, 4, :], r=[MASK], w=[IDB])
        S.memset("dve", NEGH[:], -0.5, w=[NEGH])
        S.memset("dve", ONES2[:], 1.0, w=[ONES2])
        ONESF = sb(es, "onesf", [128, 64])
        CB128 = sb(es, "cb128", [128, 1])
        CBH = sb(es, "cbh", [128, 1])
        S.memset("dve", CBH[:], -C0H, w=[CBH])
        S.memset("dve", CB128[:], -C0H * 128.0, w=[CB128])
        S.memset("dve", ONESF[:], 1.0, w=[ONESF])
        S.memset("dve", ONESB[:], 1.0, w=[ONESB])
        UPS, UPI, LOS, LOI, IDF = (MASK[:, i, :] for i in range(5))
        SINB = sb(es, "sinb", [64, 8, 64], BF16)

        def rstd_of(x_ap, n, gs, eps, scr, ss, rs, r_toks):
            S.act(scr[:, 0:n * gs], x_ap, AF.Square, r=r_toks, w=[scr])
            S.red(ss[:, 0:n], scr[:, 0:n * gs].rearrange("p (a b) -> p a b", b=gs), r=[scr], w=[ss])
            S.ts("dve", ss[:, 0:n], ss[:, 0:n], 1.0 / gs, ALU.mult, eps, ALU.add, r=[ss], w=[ss])
            S.tt("pool", rs[:, 0:n], ss[:, 0:n], NEGH[:, 0:n], ALU.pow, r=[ss, NEGH], w=[rs])

        def norm_T(xt, gi, hT, scr, hb, ss, rs, dst=None):
            rstd_of(xt[:, :], 1, D, EPS, scr, ss, rs, [xt])
            S.act(hb[:, :], xt[:, :], AF.Copy, scale=rs[:, 0:1], r=[xt, rs], w=[hb])
            pb = psb()
            for kt in range(8):
                S.tr(pb[:, kt * 128:(kt + 1) * 128], hb[:, kt * 128:(kt + 1) * 128], IDB[:], r=[hb, IDB], w=[pb])
            S.tt("dve", (hT[:, :, :] if dst is None else dst), pb[:, :].rearrange("p (a b) -> p a b", b=128),
                 GAIN[:, gi, :].unsqueeze(2).to_broadcast([128, 8, 128]), ALU.mult, r=[pb, GAIN], w=[hT])

        es1 = ExitStack()
        with es1:
            WZ = sb(es1, "wz", [128, 8, 1792], BF16)
            W2A = sb(es1, "w2a", [128, 512], BF16)
            G2 = sb(es1, "g2", [128, 512], BF16)
            VA = sb(es1, "va", [128, 4352])
            es1w = ExitStack()
            es1w.__enter__()
            STG = [sb(es1w, "stg%d" % i, [128, 1792]) for i in range(4)]
            stg_i = [0]

            def load_cast(dst_ap, src_ap, ncol, wtok):
                st = STG[stg_i[0] % 4]
                ce = ("pool", "act", "dve")[stg_i[0] % 3]
                stg_i[0] += 1
                S.dma("sp", st[:, 0:ncol], src_ap, w=[st])
                S.cp(ce, dst_ap, st[:, 0:ncol], r=[st], w=[wtok])

            for kt in range(8):
                load_cast(WZ[:, kt, :], w_in[kt * 128:(kt + 1) * 128, 0:1792], 1792, WZ)
            load_cast(W2A[:, :], w2a, 512, W2A)
            load_cast(G2[:, :], g2, 512, G2)
            S.barrier()
            es1w.__exit__(None, None, None)
            S.dma("sp", VA[:], vecA.partition_broadcast(128).rearrange("p a n -> p (a n)"), w=[VA])
            MU = VA[:, 0:1792]
            W0 = VA[:, 1792:2304]
            A0 = VA[:, 2304:2816]
            K_K = VA[:, 2816:3328]
            K_A = VA[:, 3328:3840]
            R_K = VA[:, 3840:4352]

            XT = [sb(es1, "xt%d" % i, [128, D]) for i in range(2)]
            SCR = sb(es1, "scr", [128, D])
            HB = sb(es1, "hb", [128, D], BF16)
            SS = sb(es1, "ss", [128, 8])
            RS = sb(es1, "rs", [128, 8])
            HT = [sb(es1, "ht%d" % i, [128, 8, 128], BF16) for i in range(2)]
            Z = [sb(es1, "z%d" % i, [128, 1792]) for i in range(2)]
            ZP = sb(es1, "zp", [128, 1792])
            ZS = sb(es1, "zs", [128, 1792])
            LC = sb(es1, "lc", [128, 256], BF16)
            LCT = sb(es1, "lct", [128, 256], BF16)
            TW = sb(es1, "tw", [128, 512])
            AA = sb(es1, "aa", [128, 512])
            GO = sb(es1, "go", [128, 512])
            KK = sb(es1, "kk", [128, 512])
            KH = sb(es1, "kh", [128, 512])
            BB = sb(es1, "bb", [128, 512])
            T1 = sb(es1, "t1", [128, 512])
            T2 = sb(es1, "t2", [128, 512])
            BV = sb(es1, "bv", [128, 512])
            SS8 = sb(es1, "ss8", [128, 8])
            RN8 = sb(es1, "rn8", [128, 8])
            BS8 = sb(es1, "bs8", [128, 8])

            def rwkv_pre(zc, zp_ready_toks):
                S.tt("dve", ZS[:, :], ZP[:, :], zc[:, :], ALU.subtract, r=[ZP, zc], w=[ZS])
                S.tt("dve", ZS[:, :], ZS[:, :], MU, ALU.mult, r=[ZS, VA], w=[ZS])
                S.tt("dve", ZS[:, :], ZS[:, :], zc[:, :], ALU.add, r=[ZS, zc], w=[ZS])
                r_ = ZS[:, 0:512]
                k_ = ZS[:, 512:1024]
                S.act(LC[:, 0:64], ZS[:, 1536:1600], AF.Tanh, r=[ZS], w=[LC])
                S.act(LC[:, 64:128], ZS[:, 1600:1664], AF.Copy, r=[ZS], w=[LC])
                S.act(LC[:, 128:256], ZS[:, 1664:1792], AF.Tanh, scale=0.5, r=[ZS], w=[LC])
                S.ts("dve", LC[:, 128:256], LC[:, 128:256], 0.5, ALU.mult, 0.5, ALU.add, r=[LC], w=[LC])
                pb = psb()
                S.tr(pb[:, 0:128], LC[:, 0:128], IDB[:], r=[LC, IDB], w=[pb])
                S.tr(pb[:, 128:256], LC[:, 128:256], IDB[:], r=[LC, IDB], w=[pb])
                S.cp("act", LCT[:, :], pb[:, 0:256], r=[pb], w=[LCT])
                yield
                pw, pa, pg = psf(), psf(), psf()
                S.mm(pw[:, :], LCT[0:64, 0:128], W2A[0:64, :], r=[LCT, W2A], w=[pw])
                S.mm(pa[:, :], LCT[64:128, 0:128], W2A[64:128, :], r=[LCT, W2A], w=[pa])
                S.mm(pg[:, :], LCT[:, 128:256], G2[:, :], r=[LCT, G2], w=[pg])
                S.tt("dve", TW[:, :], pw[:, :], W0, ALU.add, r=[pw, VA], w=[TW])
                S.act(TW[:, :], TW[:, :], AF.Tanh, scale=0.5, r=[TW], w=[TW])
                S.tt("dve", AA[:, :], pa[:, :], A0, ALU.add, r=[pa, VA], w=[AA])
                S.act(AA[:, :], AA[:, :], AF.Tanh, scale=0.5, r=[AA], w=[AA])
                S.ts("dve", AA[:, :], AA[:, :], 0.5, ALU.mult, 0.5, ALU.add, r=[AA], w=[AA])
                S.cp("act", GO[:, :], pg[:, :], r=[pg], w=[GO])
                yield
                S.tt("dve", KK[:, :], k_, K_K, ALU.mult, r=[ZS, VA], w=[KK])
                S.act(T1[:, :], KK[:, :], AF.Square, r=[KK], w=[T1])
                S.red(SS8[:, :], T1[:, :].rearrange("p (a b) -> p a b", b=64), r=[T1], w=[SS8])
                S.ts("dve", SS8[:, :], SS8[:, :], 1e-24, ALU.max, r=[SS8], w=[SS8])
                S.tt("pool", RN8[:, :], SS8[:, :], NEGH[:, 0:8], ALU.pow, r=[SS8, NEGH], w=[RN8])
                S.tt("dve", KK[:, :].rearrange("p (a b) -> p a b", b=64), KK[:, :].rearrange("p (a b) -> p a b", b=64),
                     RN8[:, :].unsqueeze(2).to_broadcast([128, 8, 64]), ALU.mult, r=[KK, RN8], w=[KK])
                S.stt(T1[:, :], AA[:, :], -1.0, K_A, ALU.add, ALU.mult, r=[AA, VA], w=[T1])
                S.stt(KH[:, :], T1[:, :], 1.0, k_, ALU.add, ALU.mult, r=[T1, ZS], w=[KH])
                yield
                S.tt("pool", BB[:, :], KK[:, :], AA[:, :], ALU.mult, r=[KK, AA], w=[BB])
                S.tt("pool", T2[:, :], r_, KH[:, :], ALU.mult, r=[ZS, KH], w=[T2])
                S.tt("pool", T2[:, :], T2[:, :], R_K, ALU.mult, r=[T2, VA], w=[T2])
                S.red(BS8[:, :], T2[:, :].rearrange("p (a b) -> p a b", b=64), r=[T2], w=[BS8])
                S.tt("dve", BV[:, :].rearrange("p (a b) -> p a b", b=64), ZS[:, 1024:1536].rearrange("p (a b) -> p a b", b=64),
                     BS8[:, :].unsqueeze(2).to_broadcast([128, 8, 64]), ALU.mult, r=[ZS, BS8], w=[BV])

            def zproj(xsrc_ap, zc, ht, xt):
                S.dma("sp", xt[:, :], xsrc_ap, w=[xt])
                norm_T(xt, 0, ht, SCR, HB, SS, RS)
                for c, (lo, hi) in enumerate(((0, 512), (512, 1024), (1024, 1536), (1536, 1792))):
                    p = psf()
                    for kt in range(8):
                        S.mm(p[:, 0:hi - lo], ht[:, kt, :], WZ[:, kt, lo:hi], start=(kt == 0), stop=(kt == 7),
                             r=[ht, WZ], w=[p])
                    S.cp("act", zc[:, lo:hi], p[:, 0:hi - lo], r=[p], w=[zc])
                    yield

            es1p = ExitStack()
            with es1p:
                SP = sb(es1p, "sp", [64, 8, 128])
                SPB = sb(es1p, "spb", [64, 8, 128], BF16)
                es1q = ExitStack()
                es1q.__enter__()
                M4 = sb(es1q, "m4", [128, 4, 128])
                S.cp("dve", M4[:, 0:2, :], MASK[:, 0:1, :].to_broadcast([128, 2, 128]), r=[MASK], w=[M4])
                S.cp("dve", M4[:, 2:4, :], MASK[:, 1:2, :].to_broadcast([128, 2, 128]), r=[MASK], w=[M4])
                GI = sb(es1q, "gi", [128, 512])
                GV = sb(es1q, "gv", [128, 512])
                GX = sb(es1q, "gx", [128, 512])
                GE = sb(es1q, "ge", [128, 512])
                TM = sb(es1q, "tm", [128, 7, 512], BF16)
                FT = sb(es1q, "ft", [128, 4, 4, 128], BF16)
                GC = sb(es1q, "gc", [64, 512])
                HW = [sb(es1q, "hw%d" % i, [128, 5, 128], BF16) for i in range(8)]
                IW = [[sb(es1q, "iw%d_%d" % (i, j), [128, 3, 128], BF16) for j in range(2)] for i in range(8)]
                XF = [sb(es1q, "xf%d" % i, [128, 128], BF16) for i in range(8)]
                RH = [sb(es1q, "rh%d" % i, [128, 128], BF16) for i in range(8)]
                WU = [sb(es1q, "wu%d" % i, [128, 128], BF16) for i in range(8)]
                PTs = [sb(es1q, "pt%d" % i, [64, 64]) for i in range(8)]
                HS = [sb(es1q, "hs%d" % i, [64, 64]) for i in range(8)]
                QT = [sb(es1q, "qt%d" % i, [64, 128], BF16) for i in range(8)]
                GTS = [sb(es1q, "gts%d" % i, [64, 8, 128], BF16) for i in range(2)]
                YP = [sb(es1q, "yp%d" % i, [128, 512]) for i in range(2)]

                S.memset("dve", SP[:, :, 0:64], 0.0, w=[SP])
                for h in range(8):
                    S.cp("dve", SP[:, h, 64:128], MASK[0:64, 4, 0:64], r=[MASK], w=[SP])
                S.cp("act", SPB[:, :, :], SP[:, :, :], r=[SP], w=[SPB])

                TMs = [TM, sb(es1q, "tm_b", [128, 7, 512], BF16)]
                FTs = [FT, sb(es1q, "ft_b", [128, 4, 4, 128], BF16)]
                GCs = [GC, sb(es1q, "gc_b", [64, 512])]

                def pre_gen(j):
                    TM, FT, GC = TMs[j % 2], FTs[j % 2], GCs[j % 2]
                    zc = Z[j % 2]
                    yield from zproj(xp[j * 128:(j + 1) * 128, :], zc, HT[j % 2], XT[j % 2])
                    if j == 0:
                        return
                    zprev = Z[(j - 1) % 2]
                    S.dma("sp", ZP[1:128, :], zc[0:127, :], r=[zc], w=[ZP])
                    S.dma("sp", ZP[0:1, :], zprev[127:128, :], r=[zprev], w=[ZP])
                    if j == NT:
                        S.dma("sp", zlast_o, zc[127:128, :], r=[zc])
                    yield from rwkv_pre(zc, None)
                    jj = j - 1
                    S.dma("sp", sc_bv[jj], BV[:, :], r=[BV])
                    S.dma("sp", sc_g[jj], GO[:, :], r=[GO])
                    p_i, p_s, p_r = psf(), psf(), psf()
                    S.mm(p_i[:, :], UPI, TW[:, :], r=[MASK, TW], w=[p_i])
                    S.mm(p_s[:, :], UPS, TW[:, :], r=[MASK, TW], w=[p_s])
                    S.mm(p_r[:, :], LOS, TW[:, :], r=[MASK, TW], w=[p_r])
                    S.act(GI[:, :], p_i[:, :], AF.Exp, bias=EB[:, 0:1], scale=-C0H, r=[p_i, EB], w=[GI])
                    S.act(GV[:, :], p_i[:, :], AF.Exp, bias=EB[:, 2:3], scale=C0H, r=[p_i, EB], w=[GV])
                    S.act(GX[:, :], p_s[:, :], AF.Exp, bias=EB[:, 1:2], scale=-C0H, r=[p_s, EB], w=[GX])
                    S.act(GE[:, :], p_r[:, :], AF.Exp, bias=EB[:, 3:4], scale=-C0H, r=[p_r, EB], w=[GE])
                    yield
                    pc = psf()
                    for h in range(8):
                        S.mm(pc[0:64, h * 64:(h + 1) * 64], TW[:, h * 64:(h + 1) * 64], ONESF[:, :], r=[TW, ONESF], w=[pc])
                    S.act(GC[:, :], pc[0:64, :], AF.Exp, bias=CB128[0:64, 0:1], scale=-C0H, r=[pc, CB128], w=[GC])
                    S.tt("dve", TM[:, 0, :], ZS[:, 0:512], GI[:, :], ALU.mult, r=[ZS, GI], w=[TM.sub(0)])
                    S.stt(TM[:, 1, :], KK[:, :], -1.0, GX[:, :], ALU.mult, ALU.mult, r=[KK, GX], w=[TM.sub(1)])
                    S.tt("dve", TM[:, 2, :], BB[:, :], GV[:, :], ALU.mult, r=[BB, GV], w=[TM.sub(2)])
                    S.tt("pool", TM[:, 3, :], KH[:, :], GV[:, :], ALU.mult, r=[KH, GV], w=[TM.sub(3)])
                    S.tt("pool", TM[:, 4, :], BB[:, :], GE[:, :], ALU.mult, r=[BB, GE], w=[TM.sub(4)])
                    S.tt("pool", TM[:, 5, :], KH[:, :], GE[:, :], ALU.mult, r=[KH, GE], w=[TM.sub(5)])
                    S.cp("act", TM[:, 6, :], ZS[:, 1024:1536], r=[ZS], w=[TM.sub(6)])
                    yield
                    srcslot = (1, 0, 2, 3)
                    for half in range(2):
                        pb = psb()
                        for hpp in range(2):
                            hp = half * 2 + hpp
                            for sl in range(4):
                                o = (hpp * 4 + sl) * 128
                                S.tr(pb[:, o:o + 128], TM[:, srcslot[sl], hp * 128:(hp + 1) * 128], IDB[:],
                                     r=[TM.sub(srcslot[sl]), IDB], w=[pb])
                        S.cp("act" if half == 0 else "dve",
                             FT[:, half * 2:half * 2 + 2, :, :].rearrange("p a b c -> p (a b c)"), pb[:, :],
                             r=[pb], w=[FT.sub(half)])
                        yield

                def stages(j, step):
                    TM, FT, GC = TMs[j % 2], FTs[j % 2], GCs[j % 2]
                    jj = j - 1
                    ypar = YP[jj % 2]
                    gts = GTS[jj % 2]
                    p_y = PF[5]
                    H8 = range(8)

                    def hv(h):
                        hp, base = h // 2, 64 * (h % 2)
                        return hp, FT.sub(hp // 2), slice(h * 64, (h + 1) * 64), slice(base, base + 64)

                    for h in H8:
                        hp, fs, hsl, ps_ = hv(h)
                        hw = HW[h]
                        aT, rT, bT, kT = (FT[ps_, hp, i, :] for i in range(4))
                        pA = psf()
                        S.mm(pA[:, 0:128], bT, aT, r=[fs], w=[pA])
                        S.mm(pA[:, 128:256], kT, aT, r=[fs], w=[pA])
                        S.mm(pA[:, 256:384], bT, rT, r=[fs], w=[pA])
                        S.mm(pA[:, 384:512], kT, rT, r=[fs], w=[pA])
                        pN = psf()
                        S.mm(pN[:, 0:128], aT, bT, r=[fs], w=[pN])
                        S.tt("dve", hw[:, 1:5, :], pA[:, :].rearrange("p (a b) -> p a b", b=128), M4[:, :, :], ALU.mult,
                             r=[pA, M4], w=[hw])
                        S.tt("dve", hw[:, 0, :], pN[:, 0:128], LOS, ALU.mult, r=[pN, MASK], w=[hw])
                    step()
                    curs = {}
                    for h in H8:
                        hw = HW[h]
                        pI = psf()
                        S.mm(pI[:, 0:128], hw[:, 1, :], hw[:, 0, :], r=[hw], w=[pI])
                        S.mm(pI[:, 128:256], hw[:, 0, :], hw[:, 1, :], r=[hw], w=[pI])
                        cur = IW[h][0]
                        S.cp("act", cur[:, 0:2, :].rearrange("p a b -> p (a b)"), pI[:, 0:256], r=[pI], w=[cur])
                        S.tt("pool", cur[:, 2, :], hw[:, 1, :], IDF, ALU.add, r=[hw, MASK], w=[cur])
                        curs[h] = cur
                    step()
                    for lev in range(1, 6):
                        step()
                        for h in H8:
                            cur = curs[h]
                            nxt = IW[h][lev % 2]
                            pI = psf()
                            S.mm(pI[:, 0:128], cur[:, 1, :], cur[:, 0, :], r=[cur], w=[pI])
                            S.mm(pI[:, 128:384], cur[:, 0, :], cur[:, 1:3, :].rearrange("p a b -> p (a b)"), r=[cur], w=[pI])
                            S.cp("act", nxt[:, 0:2, :].rearrange("p a b -> p (a b)"), pI[:, 0:256], r=[pI], w=[nxt])
                            S.tt("dve", nxt[:, 2, :], cur[:, 2, :], pI[:, 256:384], ALU.add, r=[cur, pI], w=[nxt])
                            curs[h] = nxt
                    step()
                    for h in H8:
                        cur = curs[h]
                        pI = psf()
                        S.mm(pI[:, 0:128], cur[:, 0, :], cur[:, 2, :], r=[cur], w=[pI])
                        S.tt("dve", XF[h][:, :], cur[:, 2, :], pI[:, 0:128], ALU.add, r=[cur, pI], w=[XF[h]])
                    step()
                    for h in H8:
                        hp, fs, hsl, ps_ = hv(h)
                        hw, rh = HW[h], RH[h]
                        pV = psf()
                        S.mm(pV[:, 0:64], hw[:, 2, :], TM[:, 6, hsl], r=[hw, TM.sub(6)], w=[pV])
                        S.cp("pool", rh[:, 0:64], TM[:, 1, hsl], r=[TM.sub(1)], w=[rh])
                        S.cp("act", rh[:, 64:128], pV[:, 0:64], r=[pV], w=[rh])
                    step()
                    for h in H8:
                        pW = psf()
                        S.mm(pW[:, 0:128], XF[h][:, :], RH[h][:, :], r=[XF[h], RH[h]], w=[pW])
                        S.cp("act", WU[h][:, :], pW[:, 0:128], r=[pW], w=[WU[h]])
                    step()
                    for h in H8:
                        hp, fs, hsl, ps_ = hv(h)
                        hw, wu, pts, hs, qt = HW[h], WU[h], PTs[h], HS[h], QT[h]
                        pP = psf()
                        S.mm(pP[0:64, 0:64], wu[:, 0:64], TM[:, 4, hsl], r=[wu, TM.sub(4)], w=[pP])
                        S.mm(pP[0:64, 64:128], TM[:, 4, hsl], wu[:, 64:128], start=True, stop=False, r=[wu, TM.sub(4)], w=[pP])
                        S.mm(pP[0:64, 64:128], TM[:, 5, hsl], TM[:, 6, hsl], start=False, stop=True,
                             r=[TM.sub(5), TM.sub(6)], w=[pP])
                        S.mm(pP[0:64, 128:256], wu[:, 0:64], hw[:, 3, :], start=True, stop=False, r=[wu, hw], w=[pP])
                        S.mm(pP[0:64, 128:256], TM[:, 0, hsl], IDB[:, :], start=False, stop=True, r=[TM.sub(0), IDB], w=[pP])
                        S.stt(pts[:, :], MASK[0:64, 4, 0:64], GC[:, h * 64:h * 64 + 1], pP[0:64, 0:64], ALU.mult, ALU.add,
                              r=[MASK, GC, pP], w=[pts])
                        S.cp("dve", hs[:, :], pP[0:64, 64:128], r=[pP], w=[hs])
                        S.cp("dve", qt[:, :], pP[0:64, 128:256], r=[pP], w=[qt])
                    step()
                    for h in H8:
                        hp, fs, hsl, ps_ = hv(h)
                        hw, wu, qt = HW[h], WU[h], QT[h]
                        S.mm(p_y[:, hsl], hw[:, 3, :], wu[:, 64:128], start=True, stop=False, r=[hw, wu], w=[p_y])
                        S.mm(p_y[:, hsl], hw[:, 4, :], TM[:, 6, hsl], start=False, stop=False, r=[hw, TM.sub(6)], w=[p_y])
                        S.mm(p_y[:, hsl], qt[:, :], SPB[:, h, 0:64], start=False, stop=True, r=[qt, SPB.sub(h)], w=[p_y])
                    step()
                    for h in H8:
                        qt = QT[h]
                        pG = psf()
                        S.mm(pG[0:64, 0:128], SPB[:, h, 64:128], qt[:, :], r=[SPB.sub(h), qt], w=[pG])
                        S.cp("act", gts[:, h, :], pG[0:64, 0:128], r=[pG], w=[gts])
                    step()
                    for h in H8:
                        pts, hs = PTs[h], HS[h]
                        pS = psf()
                        S.mm(pS[0:64, 0:128], pts[:, :], SP[:, h, :], r=[pts, SP.sub(h)], w=[pS])
                        S.tt("dve", SP[:, h, 0:64], pS[0:64, 0:64], hs[:, :], ALU.add, r=[pS, hs], w=[SP.sub(h)])
                        S.cp("act", SP[:, h, 64:128], pS[0:64, 64:128], r=[pS], w=[SP.sub(h)])
                        S.cp("pool", SPB[:, h, :], SP[:, h, :], r=[SP.sub(h)], w=[SPB.sub(h)])
                    S.cp("act", ypar[:, :], p_y[:, :], r=[p_y], w=[ypar])
                    S.dma("sp", sc_yp[jj], ypar[:, :], r=[ypar])
                    S.dma("sp", sc_gt[jj], gts[:, :, :].rearrange("p a b -> p (a b)"), r=[gts])


                for _ in pre_gen(0):
                    pass
                for _ in pre_gen(1):
                    pass
                for j in range(1, NP):
                    g = pre_gen(j + 1) if j + 1 < NP else iter(())
                    stages(j, lambda g=g: next(g, None))
                    for _ in g:
                        pass
                S.barrier()
                es1q.__exit__(None, None, None)
                EX = sb(es1p, "ex", [64, 8, 128])
                EXA = sb(es1p, "exa", [64, 4, 8, 128])
                SIN = sb(es1p, "sin", [64, 8, 64])
                SC1 = sb(es1p, "sc1", [64, 8, 64])
                SFT = sb(es1p, "sft", [64, 8, 64])
                hall = [SP.sub(h) for h in range(8)]
                S.cp("dve", EX[:, :, 0:64], SP[:, :, 0:64], r=hall, w=[EX])
                pT = psf()
                for h in range(8):
                    S.tr(pT[0:64, h * 64:(h + 1) * 64], SP[:, h, 64:128], MASK[0:64, 4, 0:64], r=hall + [MASK], w=[pT])
                S.cp("dve", EX[:, :, 64:128], pT[0:64, :].rearrange("p (a b) -> p a b", b=64), r=[pT], w=[EX])
                S.dma("pool", cc_src.ap(), EX[:, :, :].rearrange("p a b -> p (a b)"), r=[EX], w=[EXA.sub("src")])
                deps = S._deps([EXA.sub("src")], [EXA.sub("dst")])
                waits = S._resolve("pool", deps)
                S.cnt["pool"] += 1
                me = (("e", "pool"), S.cnt["pool"])
                S._mark(me, [EXA.sub("src")], [EXA.sub("dst")])
                S.streams["pool"].append((waits, lambda e: e.collective_compute(
                    "AllGather", ALU.bypass, replica_groups=[[0, 1, 2, 3], [4, 5, 6, 7]],
                    ins=[cc_src.ap()], outs=[cc_dst.ap()]), S.esem["pool"], 1))
                S.gseq += 1
                S.gidx["pool"].append(S.gseq)
                S.dma("pool", EXA[:, :, :, :].rearrange("p r a b -> p r (a b)"),
                      cc_dst.ap().rearrange("(r p) n -> p r n", p=64), r=[EXA.sub("dst")], w=[EXA])
                S.memset("dve", SIN[:, :, :], 0.0, w=[SIN])
                for q in range(3):
                    pc_ = psf()
                    for h in range(8):
                        S.mm(pc_[0:64, h * 64:(h + 1) * 64], EXA[:, q, h, 64:128], SIN[:, h, :], r=[EXA, SIN], w=[pc_])
                    S.tt("dve", SC1[:, :, :], pc_[0:64, :].rearrange("p (a b) -> p a b", b=64), EXA[:, q, :, 0:64], ALU.add,
                         r=[pc_, EXA], w=[SC1])
                    S.tt("dve", SC1[:, :, :], SC1[:, :, :], SIN[:, :, :], ALU.subtract, r=[SC1, SIN], w=[SC1])
                    S.stt(SIN[:, :, :].rearrange("p a b -> p (a b)"), SC1[:, :, :].rearrange("p a b -> p (a b)"),
                          FLG[0:64, 1 + q:2 + q], SIN[:, :, :].rearrange("p a b -> p (a b)"), ALU.mult, ALU.add,
                          r=[SC1, SIN, FLG], w=[SIN])
                pf_ = psf()
                for h in range(8):
                    S.mm(pf_[0:64, h * 64:(h + 1) * 64], EX[:, h, 64:128], SIN[:, h, :], r=[EX, SIN], w=[pf_])
                S.tt("dve", SFT[:, :, :], pf_[0:64, :].rearrange("p (a b) -> p a b", b=64), EX[:, :, 0:64], ALU.add,
                     r=[pf_, EX], w=[SFT])
                pf2 = psf()
                for h in range(8):
                    S.tr(pf2[0:64, h * 64:(h + 1) * 64], SFT[:, h, :], MASK[0:64, 4, 0:64], r=[SFT, MASK], w=[pf2])
                S.cp("dve", SC1[:, :, :], pf2[0:64, :].rearrange("p (a b) -> p a b", b=64), r=[pf2], w=[SC1])
                S.dma("sp", stp_o.rearrange("h v k -> v h k"), SC1[:, :, :], r=[SC1])
                S.cp("act", SINB[:, :, :], SIN[:, :, :], r=[SIN], w=[SINB])
            S.barrier()
            es1s = ExitStack()
            with es1s:
                SX = sb(es1s, "sx", [128, 6, 512])
                R6 = sb(es1s, "r6", [128, 6, 64])
                ST = sb(es1s, "st", [128, 64, 64])
                TP = sb(es1s, "tp", [128, 64, 64])
                SA = sb(es1s, "sa", [128, 64])
                YV = sb(es1s, "yv", [128, 64])
                YS = sb(es1s, "ysr", [128, 512])
                WD = sb(es1s, "wd", [128, 512])
                zc = Z[0]
                S.dma("sp", ST[:, :, :].rearrange("p a b -> p (a b)"), s_state, w=[ST])
                for _ in zproj(xs, zc, HT[0], XT[0]):
                    pass
                S.dma("sp", shifts_o, zc[0:16, :], r=[zc])
                S.memset("dve", ZP[:, :], 0.0, w=[ZP])
                S.dma("sp", ZP[0:16, :], s_shift, w=[ZP])
                for _ in rwkv_pre(zc, None):
                    pass
                S.dma("sp", sc_bv[NT], BV[:, :], r=[BV])
                S.dma("sp", sc_g[NT], GO[:, :], r=[GO])
                S.act(WD[:, :], TW[:, :], AF.Exp, bias=CBH[:, 0:1], scale=-C0H, r=[TW, CBH], w=[WD])
                for i, src in enumerate((ZS[:, 0:512], WD[:, :], KH[:, :], ZS[:, 1024:1536], KK[:, :], BB[:, :])):
                    S.cp("dve" if i % 2 == 0 else "pool", SX[:, i, :], src, r=[ZS, WD, KH, KK, BB], w=[SX])
                S.dma("sp", sc_s1.rearrange("i b n -> b i n"), SX[0:16, :, :], r=[SX], w=[R6.sub("d")])
                S.dma("sp", R6[:, :, :], sc_s1.rearrange("i b (h d) -> (b h) i d", d=64), r=[R6.sub("d")], w=[R6])
                r_, w_, k_, v_, kk_, b_ = (R6[:, i, :] for i in range(6))

                def bv(ap):
                    return ap.unsqueeze(1).to_broadcast([128, 64, 64])

                def bk(ap):
                    return ap.unsqueeze(2).to_broadcast([128, 64, 64])

                S.tt("dve", TP[:, :, :], ST[:, :, :], bv(kk_), ALU.mult, r=[ST, R6], w=[TP])
                S.red(SA[:, :], TP[:, :, :], r=[TP], w=[SA])
                S.tt("pool", ST[:, :, :], ST[:, :, :], bv(w_), ALU.mult, r=[ST, R6, TP], w=[ST])
                S.tt("dve", TP[:, :, :], bk(SA[:, :]), bv(b_), ALU.mult, r=[SA, R6], w=[TP])
                S.tt("dve", ST[:, :, :], ST[:, :, :], TP[:, :, :], ALU.subtract, r=[ST, TP], w=[ST])
                S.tt("pool", TP[:, :, :], bk(v_), bv(k_), ALU.mult, r=[R6, ST], w=[TP])
                S.tt("dve", ST[:, :, :], ST[:, :, :], TP[:, :, :], ALU.add, r=[ST, TP], w=[ST])
                S.dma("sp", sts_o, ST[:, :, :].rearrange("p a b -> p (a b)"), r=[ST])
                S.tt("dve", TP[:, :, :], ST[:, :, :], bv(r_), ALU.mult, r=[ST, R6], w=[TP])
                S.red(YV[:, :], TP[:, :, :], r=[TP], w=[YV])
                S.dma("sp", sc_s2.rearrange("b (h d) -> (b h) d", d=64), YV[:, :], r=[YV], w=[YS.sub("d")])
                S.memset("dve", YS[:, :], 0.0, w=[YS])
                S.dma("sp", YS[0:16, :], sc_s2, r=[YS.sub("d")], w=[YS])
                S.dma("sp", sc_yp[NT], YS[:, :], r=[YS])
        S.barrier()

        MKT = sb(es, "mkt", [128, 4, 256], BF16)
        MVB = sb(es, "mvb", [128, 2, 512], BF16)
        VB = sb(es, "vb", [128, 1416])
        S.dma("sp", VB[:], vecB.partition_broadcast(128).rearrange("p a n -> p (a n)"), w=[VB])
        LN_W, LN_B = VB[:, 0:512], VB[:, 512:1024]
        XQG, XKG = VB[:, 1152:1280], VB[:, 1280:1408]
        es0 = ExitStack()
        with es0:
            WM = sb(es0, "wm", [128, 8, 1024], BF16)
            STG0 = [sb(es0, "stg0_%d" % i, [128, 1024]) for i in range(2)]
            for kt in range(8):
                st = STG0[kt % 2]
                S.dma("sp", st[:, :], w_mkv[kt * 128:(kt + 1) * 128, :], w=[st])
                S.cp(("pool", "act", "dve")[kt % 3], WM[:, kt, :], st[:, :], r=[st], w=[WM])
            XM0 = sb(es0, "xm0", [128, D])
            SCR0 = sb(es0, "scr0", [128, D])
            HB0 = sb(es0, "hb0", [128, D], BF16)
            SS0 = sb(es0, "ss0", [128, 8])
            RS0 = sb(es0, "rs0", [128, 8])
            HT0 = sb(es0, "ht0", [128, 8, 128], BF16)
            MKF = sb(es0, "mkf", [128, 512])
            MVF = sb(es0, "mvf", [128, 512])
            MKB = sb(es0, "mkb", [128, 512], BF16)
            for mt in range(2):
                S.dma("sp", XM0[:, :], memp[mt * 128:(mt + 1) * 128, :], w=[XM0])
                norm_T(XM0, 2, HT0, SCR0, HB0, SS0, RS0)
                pk, pv_ = psf(), psf()
                for kt in range(8):
                    S.mm(pk[:, :], HT0[:, kt, :], WM[:, kt, 0:512], start=(kt == 0), stop=(kt == 7), r=[HT0, WM], w=[pk])
                for kt in range(8):
                    S.mm(pv_[:, :], HT0[:, kt, :], WM[:, kt, 512:1024], start=(kt == 0), stop=(kt == 7), r=[HT0, WM], w=[pv_])
                S.cp("act", MVF[:, :], pv_[:, :], r=[pv_], w=[MVF])
                S.dma("sp", memv_o[mt * 128:(mt + 1) * 128, :], MVF[:, :], r=[MVF])
                S.cp("pool", MVB[:, mt, :], MVF[:, :], r=[MVF], w=[MVB])
                rstd_of(pk[:, :], 4, 128, EPS, SCR0, SS0, RS0, [pk])
                S.tt("dve", MKF[:, :].rearrange("p (a b) -> p a b", b=128), pk[:, :].rearrange("p (a b) -> p a b", b=128),
                     RS0[:, 0:4].unsqueeze(2).to_broadcast([128, 4, 128]), ALU.mult, r=[pk, RS0], w=[MKF])
                S.tt("dve", MKF[:, :].rearrange("p (a b) -> p a b", b=128), MKF[:, :].rearrange("p (a b) -> p a b", b=128),
                     XKG.unsqueeze(1).to_broadcast([128, 4, 128]), ALU.mult, r=[MKF, VB], w=[MKF])
                S.dma("sp", memk_o[mt * 128:(mt + 1) * 128, :], MKF[:, :], r=[MKF])
                S.cp("act", MKB[:, :], MKF[:, :], r=[MKF], w=[MKB])
                pb = psb()
                for h in range(4):
                    S.tr(pb[:, h * 128:(h + 1) * 128], MKB[:, h * 128:(h + 1) * 128], IDB[:], r=[MKB, IDB], w=[pb])
                S.cp("act", MKT[:, :, mt * 128:(mt + 1) * 128], pb[:, 0:512].rearrange("p (a b) -> p a b", b=128), r=[pb], w=[MKT])
        S.barrier()

        es2 = ExitStack()
        with es2:
            WR = sb(es2, "wr", [128, 8, 4352], BF16)
            WAC = sb(es2, "wac", [128, 8, 1024], BF16)
            WBB = sb(es2, "wbb", [64, 8, 1024], BF16)
            WO = sb(es2, "wo", [128, 8, 1024], BF16)
            es2w = ExitStack()
            es2w.__enter__()
            STG2 = [sb(es2w, "stg2_%d" % i, [128, 1088]) for i in range(6)]
            sgi = [0]

            def ldc(dst_ap, src_ap, np_, ncol, wtok):
                st = STG2[sgi[0] % 6]
                ce = ("pool", "act", "dve")[sgi[0] % 3]
                sgi[0] += 1
                S.dma("sp", st[0:np_, 0:ncol], src_ap, w=[st])
                S.cp(ce, dst_ap, st[0:np_, 0:ncol], r=[st], w=[wtok])

            for kt in range(8):
                for hf in range(4):
                    ldc(WR[:, kt, hf * 1088:(hf + 1) * 1088], w_in[kt * 128:(kt + 1) * 128, 1792 + hf * 1088:1792 + (hf + 1) * 1088], 128, 1088, WR)
            for i in range(4):
                ldc(WAC[:, i, :], w_br[i * 128:(i + 1) * 128, :], 128, 1024, WAC)
                ldc(WAC[:, 4 + i, :], w_br[1024 + i * 128:1024 + (i + 1) * 128, :], 128, 1024, WAC)
            for h in range(8):
                ldc(WBB[:, h, :], w_br[512 + h * 64:512 + (h + 1) * 64, :], 64, 1024, WBB)
            for kt in range(8):
                ldc(WO[:, kt, :], w_out[kt * 128:(kt + 1) * 128, :], 128, 1024, WO)
            S.barrier()
            es2w.__exit__(None, None, None)

            XT2 = [sb(es2, "xt2", [128, D])] * 2
            HB2 = sb(es2, "hb2", [128, D], BF16)
            SSx = sb(es2, "ssx", [128, 8])
            RSx = sb(es2, "rsx", [128, 8])
            HT2 = sb(es2, "ht2", [128, 8, 128], BF16)
            QKV = sb(es2, "qkv", [128, 1280])
            GTH = sb(es2, "gth", [128, 3072], BF16)
            QKG = sb(es2, "qkg", [128, 10, 64])
            SS10 = sb(es2, "ss10", [128, 16])
            RS10 = sb(es2, "rs10", [128, 16])
            RT = sb(es2, "rt", [128, 4, 10, 8])
            QKB = sb(es2, "qkb", [128, 640], BF16)
            XQB = sb(es2, "xqb", [128, 512], BF16)
            XQT = sb(es2, "xqt", [128, 4, 128], BF16)
            ESK = sb(es2, "esk", [128, 8])
            OBT = sb(es2, "obt", [64, 8, 128], BF16)
            OCT = sb(es2, "oct", [128, 4, 128], BF16)
            OAT = sb(es2, "oat", [128, 4, 128], BF16)
            YPL = sb(es2, "ypl", [128, 512])
            BVL = sb(es2, "bvl", [128, 512])
            GOL = sb(es2, "gol", [128, 512])
            OAB = sb(es2, "oab", [128, 512], BF16)
            ST8 = [sb(es2, "st8_%d" % i, [128, 8]) for i in range(4)]
            MG = sb(es2, "mg", [128, 1024])
            SCR2 = MG
            MT_ = sb(es2, "mt_", [128, 512])
            MGB = sb(es2, "mgb", [128, 1024], BF16)
            MGT = sb(es2, "mgt", [128, 8, 128], BF16)
            XMo = sb(es2, "xmo", [128, D])
            GNE = sb(es2, "gne", [128, 1])
            S.cp("dve", QKG[:, 0:8, :], VB[:, 1024:1088].unsqueeze(1).to_broadcast([128, 8, 64]), r=[VB], w=[QKG])
            S.cp("dve", QKG[:, 8:10, :], VB[:, 1088:1152].unsqueeze(1).to_broadcast([128, 2, 64]), r=[VB], w=[QKG])
            S.act(ESK[:, :], VB[:, 1408:1416], AF.Exp, r=[VB], w=[ESK])

            def inproj_rest(x_src, xt, ropej):
                S.dma("sp", xt[:, :], x_src, w=[xt])
                norm_T(xt, 0, HT2, SCR2, HB2, SSx, RSx)
                for (lo, hi, dlo) in ((0, 512, 0), (512, 768, 512), (768, 1280, 768)):
                    p = psf()
                    for kt in range(8):
                        S.mm(p[:, 0:hi - lo], HT2[:, kt, :], WR[:, kt, lo:hi], start=(kt == 0), stop=(kt == 7), r=[HT2, WR], w=[p])
                    S.cp("act", QKV[:, dlo:dlo + hi - lo], p[:, 0:hi - lo], r=[p], w=[QKV])
                for gc in range(6):
                    p = psf()
                    for kt in range(8):
                        S.mm(p[:, :], HT2[:, kt, :], WR[:, kt, 1280 + gc * 512:1280 + (gc + 1) * 512], start=(kt == 0), stop=(kt == 7),
                             r=[HT2, WR], w=[p])
                    S.act(GTH[:, gc * 512:(gc + 1) * 512], p[:, :], AF.Tanh, scale=0.5, r=[p], w=[GTH])
                qk3 = QKV[:, 0:640].rearrange("p (a b) -> p a b", b=64)
                rstd_of(QKV[:, 0:640], 10, 64, EPS, SCR2, SS10, RS10, [QKV])
                S.tt("dve", qk3, qk3, RS10[:, 0:10].unsqueeze(2).to_broadcast([128, 10, 64]), ALU.mult, r=[QKV, RS10], w=[QKV])
                S.tt("dve", qk3, qk3, QKG[:, :, :], ALU.mult, r=[QKV, QKG], w=[QKV])
                cosb = ROPE[:, ropej, 0:8].unsqueeze(1).to_broadcast([128, 10, 8])
                sinb = ROPE[:, ropej, 8:16].unsqueeze(1).to_broadcast([128, 10, 8])
                x1, x2 = qk3[:, :, 0:8], qk3[:, :, 8:16]
                S.tt("dve", RT[:, 0, :, :], x1, cosb, ALU.mult, r=[QKV, ROPE], w=[RT])
                S.tt("dve", RT[:, 1, :, :], x2, sinb, ALU.mult, r=[QKV, ROPE], w=[RT])
                S.tt("dve", RT[:, 2, :, :], x2, cosb, ALU.mult, r=[QKV, ROPE], w=[RT])
                S.tt("dve", RT[:, 3, :, :], x1, sinb, ALU.mult, r=[QKV, ROPE], w=[RT])
                S.tt("dve", x1, RT[:, 0, :, :], RT[:, 1, :, :], ALU.subtract, r=[RT], w=[QKV])
                S.tt("dve", x2, RT[:, 2, :, :], RT[:, 3, :, :], ALU.add, r=[RT], w=[QKV])
                xq3 = QKV[:, 768:1280].rearrange("p (a b) -> p a b", b=128)
                rstd_of(QKV[:, 768:1280], 4, 128, EPS, SCR2, SS10, RS10, [QKV])
                S.tt("dve", xq3, xq3, RS10[:, 0:4].unsqueeze(2).to_broadcast([128, 4, 128]), ALU.mult, r=[QKV, RS10], w=[QKV])
                S.tt("dve", XQB[:, :].rearrange("p (a b) -> p a b", b=128), xq3, XQG.unsqueeze(1).to_broadcast([128, 4, 128]),
                     ALU.mult, r=[QKV, VB], w=[XQB])

            def rwkv_out(jj, with_state):
                S.dma("sp", YPL[:, :], sc_yp[jj], w=[YPL])
                S.dma("sp", BVL[:, :], sc_bv[jj], w=[BVL])
                S.dma("sp", GOL[:, :], sc_g[jj], w=[GOL])
                if with_state:
                    S.dma("sp", GTL[:, :, :].rearrange("p a b -> p (a b)"), sc_gt[jj], w=[GTL])
                    p = psf()
                    for h in range(8):
                        S.mm(p[:, h * 64:(h + 1) * 64], GTL[:, h, :], SINB[:, h, :], r=[GTL, SINB], w=[p])
                    S.tt("dve", XMo[:, 0:512], p[:, :], YPL[:, :], ALU.add, r=[p, YPL], w=[XMo])
                    ysrc = XMo
                else:
                    ysrc = YPL
                y3 = ysrc[:, 0:512].rearrange("p (a b) -> p a b", b=64)
                sm, sq, mn, vr = ST8
                S.red(sm[:, :], y3, r=[ysrc], w=[sm])
                S.act(XMo[:, 512:1024], ysrc[:, 0:512], AF.Square, r=[ysrc], w=[XMo])
                S.red(sq[:, :], XMo[:, 512:1024].rearrange("p (a b) -> p a b", b=64), r=[XMo], w=[sq])
                S.ts("dve", mn[:, :], sm[:, :], 1.0 / 64, ALU.mult, r=[sm], w=[mn])
                S.tt("dve", vr[:, :], mn[:, :], mn[:, :], ALU.mult, r=[mn], w=[vr])
                S.stt(vr[:, :], sq[:, :], 1.0 / 64, vr[:, :], ALU.mult, ALU.subtract, r=[sq, vr], w=[vr])
                S.ts("dve", vr[:, :], vr[:, :], GN_EPS, ALU.add, r=[vr], w=[vr])
                S.tt("pool", sq[:, :], vr[:, :], NEGH[:, 0:8], ALU.pow, r=[vr, NEGH], w=[sq])
                yq3 = XMo[:, 512:1024].rearrange("p (a b) -> p a b", b=64)
                S.tt("dve", yq3, y3, mn[:, :].unsqueeze(2).to_broadcast([128, 8, 64]), ALU.subtract, r=[ysrc, mn], w=[XMo])
                S.tt("dve", yq3, yq3, sq[:, :].unsqueeze(2).to_broadcast([128, 8, 64]), ALU.mult, r=[XMo, sq], w=[XMo])
                S.tt("pool", XMo[:, 512:1024], XMo[:, 512:1024], LN_W, ALU.mult, r=[XMo, VB], w=[XMo])
                S.tt("pool", XMo[:, 512:1024], XMo[:, 512:1024], LN_B, ALU.add, r=[XMo, VB], w=[XMo])
                S.tt("dve", XMo[:, 512:1024], XMo[:, 512:1024], BVL[:, :], ALU.add, r=[XMo, BVL], w=[XMo])
                S.tt("dve", OAB[:, :], XMo[:, 512:1024], GOL[:, :], ALU.mult, r=[XMo, GOL], w=[OAB])
                pb = psb()
                for i in range(4):
                    S.tr(pb[:, i * 128:(i + 1) * 128], OAB[:, i * 128:(i + 1) * 128], IDB[:], r=[OAB, IDB], w=[pb])
                S.cp("act", OAT[:, :, :].rearrange("p a b -> p (a b)"), pb[:, 0:512], r=[pb], w=[OAT])

            def xq_transposes():
                pb = psb()
                for h in range(4):
                    S.tr(pb[:, h * 128:(h + 1) * 128], XQB[:, h * 128:(h + 1) * 128], IDB[:], r=[XQB, IDB], w=[pb])
                S.cp("act", XQT[:, :, :].rearrange("p a b -> p (a b)"), pb[:, 0:512], r=[pb], w=[XQT])

            def merge_out(xt, xm_dst_ap, extra_r=()):
                for cc in range(2):
                    cs = slice(cc * 512, (cc + 1) * 512)
                    pa_, pb_, pc_ = psf(), psf(), psf()
                    for k in range(4):
                        S.mm(pa_[:, :], OAT[:, k, :], WAC[:, k, cs], start=(k == 0), stop=(k == 3), r=[OAT, WAC], w=[pa_])
                    for h in range(8):
                        S.mm(pb_[:, :], OBT[:, h, :], WBB[:, h, cs], start=(h == 0), stop=(h == 7), r=[OBT, WBB], w=[pb_])
                    for k in range(4):
                        S.mm(pc_[:, :], OCT[:, k, :], WAC[:, 4 + k, cs], start=(k == 0), stop=(k == 3), r=[OCT, WAC], w=[pc_])
                    S.stt(MG[:, cs], GTH[:, cc * 512:(cc + 1) * 512], 1.0, pa_[:, :], ALU.add, ALU.mult, r=[GTH, pa_], w=[MG.sub(cc)])
                    S.stt(MT_[:, :], GTH[:, 1024 + cc * 512:1024 + (cc + 1) * 512], 1.0, pb_[:, :], ALU.add, ALU.mult, r=[GTH, pb_], w=[MT_])
                    S.tt("pool", MG[:, cs], MG[:, cs], MT_[:, :], ALU.add, r=[MG.sub(cc), MT_], w=[MG.sub(cc)])
                    S.stt(MT_[:, :], GTH[:, 2048 + cc * 512:2048 + (cc + 1) * 512], 1.0, pc_[:, :], ALU.add, ALU.mult, r=[GTH, pc_], w=[MT_])
                    S.tt("pool", MGB[:, cs], MG[:, cs], MT_[:, :], ALU.add, r=[MG.sub(cc), MT_], w=[MGB])
                pb = psb()
                for kt in range(8):
                    S.tr(pb[:, kt * 128:(kt + 1) * 128], MGB[:, kt * 128:(kt + 1) * 128], IDB[:], r=[MGB, IDB], w=[pb])
                S.cp("act", MGT[:, :, :].rearrange("p a b -> p (a b)"), pb[:, :], r=[pb], w=[MGT])
                for cc in range(2):
                    cs = slice(cc * 512, (cc + 1) * 512)
                    p = psf()
                    for kt in range(8):
                        S.mm(p[:, :], MGT[:, kt, :], WO[:, kt, cs], start=(kt == 0), stop=(kt == 7), r=[MGT, WO], w=[p])
                    S.stt(XMo[:, cs], p[:, :], 0.5, xt[:, cs], ALU.mult, ALU.add, r=[p, xt], w=[XMo])
                S.dma("sp", xm_dst_ap, XMo[:, :], r=[XMo])

            GTL = sb(es2, "gtl", [64, 8, 128], BF16)

            es2p = ExitStack()
            with es2p:
                QT2 = sb(es2p, "qt2", [128, 4, 128], BF16)
                KT = [sb(es2p, "kt%d" % i, [128, 128], BF16) for i in range(2)]
                VV = [sb(es2p, "vv%d" % i, [128, 128], BF16) for i in range(2)]
                PE_ = sb(es2p, "pe_", [128, 512])
                PT_ = [[sb(es2p, "pt_%d_%d" % (g, b), [128, 512], BF16) for b in range(2)] for g in range(2)]
                PM = [sb(es2p, "pm%d" % i, [128, 512], BF16) for i in range(2)]
                RD = sb(es2p, "rd", [128, 512])
                MASKH = sb(es2p, "maskh", [128, 128])
                SNK = sb(es2p, "snk", [64, 8, 128])
                S.cp("dve", SNK[:, :, :], ESK[0:64, :].unsqueeze(2).to_broadcast([64, 8, 128]), r=[ESK], w=[SNK])
                S.ts("dve", MASKH[:, :], LOI, FLG[:, 0:1], ALU.mult, r=[MASK, FLG], w=[MASKH])

                def swa_kv_prep(cur):
                    S.cp("dve", QKB[:, 0:512].rearrange("p (h g d) -> p h g d", h=4, g=2),
                         QKV[:, 0:512].rearrange("p (g h d) -> p h g d", g=2, h=4), r=[QKV], w=[QKB])
                    S.cp("act", QKB[:, 512:640], QKV[:, 512:640], r=[QKV], w=[QKB])
                    S.cp("act", VV[cur][:, :], QKV[:, 640:768], r=[QKV], w=[VV[cur]])
                    pb = psb()
                    for i in range(5):
                        S.tr(pb[:, i * 128:(i + 1) * 128], QKB[:, i * 128:(i + 1) * 128], IDB[:], r=[QKB, IDB], w=[pb])
                    S.cp("act", QT2[:, :, :].rearrange("p a b -> p (a b)"), pb[:, 0:512], r=[pb], w=[QT2])
                    S.cp("act", KT[cur][:, :], pb[:, 512:640], r=[pb], w=[KT[cur]])

                for j in range(NP):
                    cur, prv = j % 2, (j - 1) % 2
                    xt = XT2[j % 2]
                    if j == 0:
                        S.dma("sp", xt[:, :], xp[0:128, :], w=[xt])
                        norm_T(xt, 0, HT2, SCR2, HB2, SSx, RSx)
                        p = psf()
                        for kt in range(8):
                            S.mm(p[:, 0:256], HT2[:, kt, :], WR[:, kt, 512:768], start=(kt == 0), stop=(kt == 7), r=[HT2, WR], w=[p])
                        S.cp("act", QKV[:, 512:768], p[:, 0:256], r=[p], w=[QKV])
                        S.memset("dve", QKV[:, 0:512], 0.0, w=[QKV])
                        S.memset("dve", QKV[:, 768:1280], 0.0, w=[QKV])
                        qk3 = QKV[:, 0:640].rearrange("p (a b) -> p a b", b=64)
                        rstd_of(QKV[:, 0:640], 10, 64, EPS, SCR2, SS10, RS10, [QKV])
                        S.tt("dve", qk3, qk3, RS10[:, 0:10].unsqueeze(2).to_broadcast([128, 10, 64]), ALU.mult, r=[QKV, RS10], w=[QKV])
                        S.tt("dve", qk3, qk3, QKG[:, :, :], ALU.mult, r=[QKV, QKG], w=[QKV])
                        cosb = ROPE[:, 0, 0:8].unsqueeze(1).to_broadcast([128, 10, 8])
                        sinb = ROPE[:, 0, 8:16].unsqueeze(1).to_broadcast([128, 10, 8])
                        x1, x2 = qk3[:, :, 0:8], qk3[:, :, 8:16]
                        S.tt("dve", RT[:, 0, :, :], x1, cosb, ALU.mult, r=[QKV, ROPE], w=[RT])
                        S.tt("dve", RT[:, 1, :, :], x2, sinb, ALU.mult, r=[QKV, ROPE], w=[RT])
                        S.tt("dve", RT[:, 2, :, :], x2, cosb, ALU.mult, r=[QKV, ROPE], w=[RT])
                        S.tt("dve", RT[:, 3, :, :], x1, sinb, ALU.mult, r=[QKV, ROPE], w=[RT])
                        S.tt("dve", x1, RT[:, 0, :, :], RT[:, 1, :, :], ALU.subtract, r=[RT], w=[QKV])
                        S.tt("dve", x2, RT[:, 2, :, :], RT[:, 3, :, :], ALU.add, r=[RT], w=[QKV])
                        swa_kv_prep(cur)
                        continue
                    jj = j - 1
                    inproj_rest(xp[j * 128:(j + 1) * 128, :], xt, j)
                    if j == NT:
                        S.dma("sp", swakp_o, QKV[:, 512:640], r=[QKV])
                        S.dma("sp", swavp_o, QKV[:, 640:768], r=[QKV])
                    swa_kv_prep(cur)
                    xq_transposes()
                    for g in range(2):
                        gs_ = slice(g * 64, (g + 1) * 64)
                        for bi, kb in enumerate((prv, cur)):
                            p = psf()
                            S.mm(p[:, :], KT[kb][gs_, :], QT2[gs_, :, :].rearrange("p a b -> p (a b)"), r=[KT[kb], QT2], w=[p])
                            S.act(PE_[:, :], p[:, :], AF.Exp, scale=0.125, r=[p], w=[PE_])
                            if bi == 0:
                                m_ap = (MASKH[:, :] if j == 1 else LOI)
                            else:
                                m_ap = UPI
                            S.tt("dve", PT_[g][bi][:, :].rearrange("p (a b) -> p a b", b=128), PE_[:, :].rearrange("p (a b) -> p a b", b=128),
                                 m_ap.unsqueeze(1).to_broadcast([128, 4, 128]), ALU.mult, r=[PE_, MASK, MASKH], w=[PT_[g][bi]])
                        po, psm = psf(), psf()
                        for bi, kb in enumerate((prv, cur)):
                            S.mm(po[0:64, :], VV[kb][:, gs_], PT_[g][bi][:, :], start=(bi == 0), stop=(bi == 1), r=[VV[kb], PT_[g][bi]], w=[po])
                        for bi in range(2):
                            S.mm(psm[0:64, :], ONESB[:, 0:64], PT_[g][bi][:, :], start=(bi == 0), stop=(bi == 1), r=[ONESB, PT_[g][bi]], w=[psm])
                        S.tt("dve", RD[0:64, :].rearrange("p (a b) -> p a b", b=128), psm[0:64, :].rearrange("p (a b) -> p a b", b=128),
                             SNK[:, g * 4:(g + 1) * 4, :], ALU.add, r=[psm, SNK], w=[RD])
                        S.op("dve", lambda e: e.reciprocal(RD[0:64, :], RD[0:64, :]), [RD], [RD])
                        S.tt("dve", OBT[:, g * 4:(g + 1) * 4, :].rearrange("p a b -> p (a b)"), po[0:64, :], RD[0:64, :], ALU.mult,
                             r=[po, RD], w=[OBT])
                    for mt in range(2):
                        p = psf()
                        for h in range(4):
                            S.mm(p[:, h * 128:(h + 1) * 128], MKT[:, h, mt * 128:(mt + 1) * 128], XQT[:, h, :], r=[MKT, XQT], w=[p])
                        S.act(PM[mt][:, :], p[:, :], AF.Exp, scale=float(128 ** -0.5), r=[p], w=[PM[mt]])
                    po, psm = psf(), psf()
                    for h in range(4):
                        for mt in range(2):
                            S.mm(po[:, h * 128:(h + 1) * 128], MVB[:, mt, h * 128:(h + 1) * 128], PM[mt][:, h * 128:(h + 1) * 128],
                                 start=(mt == 0), stop=(mt == 1), r=[MVB, PM[mt]], w=[po])
                    for mt in range(2):
                        S.mm(psm[:, :], ONESB[:, :], PM[mt][:, :], start=(mt == 0), stop=(mt == 1), r=[ONESB, PM[mt]], w=[psm])
                    S.op("dve", lambda e, p_=psm: e.reciprocal(RD[:, :], p_[:, :]), [psm], [RD])
                    S.tt("dve", OCT[:, :, :].rearrange("p a b -> p (a b)"), po[:, :], RD[:, :], ALU.mult, r=[po, RD], w=[OCT])
                    rwkv_out(jj, True)
                    merge_out(xt, sc_xm[jj])
            S.barrier()
            es2s = ExitStack()
            with es2s:
                SELT = sb(es2s, "selt", [16, 128])
                SEL2 = sb(es2s, "sel2", [128, 16])
                S.dma("sp", SELT[:, :], selt, w=[SELT])
                S.dma("sp", SEL2[:, :], sel2, w=[SEL2])
                KC = sb(es2s, "kc", [128, 16, 128])
                VC = sb(es2s, "vc", [128, 16, 128])
                QR = MG
                SC_ = sb(es2s, "sc_", [128, 8, 16])
                OP_ = sb(es2s, "op_", [128, 520])
                PN = sb(es2s, "pn", [128, 8])
                SN = sb(es2s, "sn", [128, 8])
                DEN = sb(es2s, "den", [128, 8])
                TPn = sb(es2s, "tpn", [128, 8, 64])
                OBS = YPL
                OBSb = OAB
                SCm = sb(es2s, "scm", [128, 4, 32])
                OPs = sb(es2s, "ops", [128, 4])
                OPr = MT_
                OPm = GOL
                TPk3 = XMo[:, :].rearrange("p (a b) -> p a b", b=64)
                TPk = XMo
                TPm3 = BVL[:, :].rearrange("p (a b) -> p a b", b=128)
                TPm = BVL
                KM3 = KC[:, :, :].rearrange("p a b -> p (a b)").rearrange("p (a b) -> p a b", b=512)
                VM3 = VC[:, :, :].rearrange("p a b -> p (a b)").rearrange("p (a b) -> p a b", b=512)
                xt = XT2[0]
                S.dma("sp", KC[:, :, :].rearrange("p a b -> p (a b)"), s_swak.rearrange("b (g i) c -> (b g) (i c)", i=16), w=[KC])
                S.dma("sp", VC[:, :, :].rearrange("p a b -> p (a b)"), s_swav.rearrange("b (g i) c -> (b g) (i c)", i=16), w=[VC])
                S.dma("sp", swaks_o[:, 0:127, :], s_swak[:, 1:128, :])
                S.dma("sp", swavs_o[:, 0:127, :], s_swav[:, 1:128, :])
                inproj_rest(xs, xt, NT + 1)
                S.dma("sp", swaks_o[:, 127, :], QKV[0:16, 512:640], r=[QKV])
                S.dma("sp", swavs_o[:, 127, :], QKV[0:16, 640:768], r=[QKV])
                xq_transposes()
                S.cp("dve", SCR2[0:16, 0:512], QKV[0:16, 0:512], r=[QKV], w=[SCR2])
                S.cp("dve", SCR2[0:16, 512:1024], XQB[0:16, :], r=[XQB], w=[SCR2])
                for hf in range(2):
                    p = psf()
                    S.mm(p[:, :], SELT[:, :], SCR2[0:16, hf * 512:(hf + 1) * 512], r=[SELT, SCR2], w=[p])
                    S.cp("act", QR[:, hf * 512:(hf + 1) * 512], p[:, :], r=[p], w=[QR])
                for h in range(8):
                    kv = h // 4
                    S.tt("dve", TPk3, KC[:, :, kv * 64:(kv + 1) * 64],
                         QR[:, h * 64:(h + 1) * 64].unsqueeze(1).to_broadcast([128, 16, 64]), ALU.mult, r=[KC, QR], w=[TPk])
                    S.red(SC_[:, h, :], TPk3, r=[TPk], w=[SC_])
                S.act(SC_[:, :, :], SC_[:, :, :], AF.Exp, scale=0.125, r=[SC_], w=[SC_])
                S.red(OP_[:, 512:520], SC_[:, :, :], r=[SC_], w=[OP_.sub("s")])
                for h in range(8):
                    kv = h // 4
                    S.tt("dve", TPk3, VC[:, :, kv * 64:(kv + 1) * 64],
                         SC_[:, h, :].unsqueeze(2).to_broadcast([128, 16, 64]), ALU.mult, r=[VC, SC_], w=[TPk])
                    S.red(OP_[:, h * 64:(h + 1) * 64], TPk3.rearrange("p i d -> p d i"), r=[TPk], w=[OP_.sub(h)])
                po, psm = psf(), psf()
                S.mm(po[0:16, :], SEL2[:, :], OP_[:, 0:512], r=[SEL2] + [OP_.sub(h) for h in range(8)], w=[po])
                S.mm(psm[0:16, 0:8], SEL2[:, :], OP_[:, 512:520], r=[SEL2, OP_.sub("s")], w=[psm])
                q3 = QKV[:, 0:512].rearrange("p (a b) -> p a b", b=64)
                for g in range(2):
                    S.tt("dve", TPn[:, g * 4:(g + 1) * 4, :], q3[:, g * 4:(g + 1) * 4, :],
                         QKV[:, 512 + g * 64:512 + (g + 1) * 64].unsqueeze(1).to_broadcast([128, 4, 64]), ALU.mult, r=[QKV], w=[TPn])
                S.red(SN[:, :], TPn[:, :, :], r=[TPn], w=[SN])
                S.act(PN[:, :], SN[:, :], AF.Exp, scale=0.125, r=[SN], w=[PN])
                S.tt("dve", DEN[:, :], PN[:, :], ESK[:, :], ALU.add, r=[PN, ESK], w=[DEN])
                S.tt("dve", DEN[0:16, :], DEN[0:16, :], psm[0:16, 0:8], ALU.add, r=[DEN, psm], w=[DEN])
                S.op("dve", lambda e: e.reciprocal(DEN[:, :], DEN[:, :]), [DEN], [DEN])
                for g in range(2):
                    S.tt("dve", TPn[:, g * 4:(g + 1) * 4, :], PN[:, g * 4:(g + 1) * 4].unsqueeze(2).to_broadcast([128, 4, 64]),
                         QKV[:, 640 + g * 64:640 + (g + 1) * 64].unsqueeze(1).to_broadcast([128, 4, 64]), ALU.mult, r=[PN, QKV], w=[TPn])
                S.cp("dve", OBS[:, :], TPn[:, :, :].rearrange("p a b -> p (a b)"), r=[TPn], w=[OBS])
                S.tt("dve", OBS[0:16, :], OBS[0:16, :], po[0:16, :], ALU.add, r=[OBS, po], w=[OBS])
                S.tt("dve", OBSb[:, :].rearrange("p (a b) -> p a b", b=64), OBS[:, :].rearrange("p (a b) -> p a b", b=64),
                     DEN[:, :].unsqueeze(2).to_broadcast([128, 8, 64]), ALU.mult, r=[OBS, DEN], w=[OBSb])
                pb = psb()
                for h in range(8):
                    S.tr(pb[0:64, h * 128:(h + 1) * 128], OBSb[:, h * 64:(h + 1) * 64], IDB[:], r=[OBSb, IDB], w=[pb])
                S.cp("act", OBT[:, :, :].rearrange("p a b -> p (a b)"), pb[0:64, :], r=[pb], w=[OBT])
                mk4 = s_memk.rearrange("b (g r i) c -> (b g) r (i c)", g=8, r=8, i=4)
                mv4 = s_memv.rearrange("b (g r i) c -> (b g) r (i c)", g=8, r=8, i=4)
                for r_ in range(8):
                    S.dma("sp", KC[:, :, :].rearrange("p a b -> p (a b)"), mk4[:, r_, :], w=[KC])
                    for h in range(4):
                        S.tt("dve", TPm3, KM3[:, :, h * 128:(h + 1) * 128],
                             QR[:, 512 + h * 128:512 + (h + 1) * 128].unsqueeze(1).to_broadcast([128, 4, 128]), ALU.mult, r=[KC, QR], w=[TPm])
                        S.red(SCm[:, h, r_ * 4:(r_ + 1) * 4], TPm3, r=[TPm], w=[SCm])
                S.act(SCm[:, :, :], SCm[:, :, :], AF.Exp, scale=float(128 ** -0.5), r=[SCm], w=[SCm])
                S.red(OPs[:, :], SCm[:, :, :], r=[SCm], w=[OPs])
                for r_ in range(8):
                    S.dma("sp", VC[:, :, :].rearrange("p a b -> p (a b)"), mv4[:, r_, :], w=[VC])
                    for h in range(4):
                        S.tt("dve", TPm3, VM3[:, :, h * 128:(h + 1) * 128],
                             SCm[:, h, r_ * 4:(r_ + 1) * 4].unsqueeze(2).to_broadcast([128, 4, 128]), ALU.mult, r=[VC, SCm], w=[TPm])
                        dst = OPm if r_ == 0 else OPr
                        S.red(dst[:, h * 128:(h + 1) * 128], TPm3.rearrange("p i d -> p d i"), r=[TPm], w=[dst])
                    if r_ > 0:
                        S.tt("dve", OPm[:, 0:512], OPm[:, 0:512], OPr[:, :], ALU.add, r=[OPm, OPr], w=[OPm])
                po, psm = psf(), psf()
                S.mm(po[0:16, :], SEL2[:, :], OPm[:, 0:512], r=[SEL2, OPm], w=[po])
                S.mm(psm[0:16, 0:4], SEL2[:, :], OPs[:, :], r=[SEL2, OPs], w=[psm])
                S.memset("dve", DEN[:, :], 1.0, w=[DEN])
                S.cp("dve", DEN[0:16, 0:4], psm[0:16, 0:4], r=[psm], w=[DEN])
                S.op("dve", lambda e: e.reciprocal(DEN[:, 0:4], DEN[:, 0:4]), [DEN], [DEN])
                S.memset("dve", OBS[:, :], 0.0, w=[OBS])
                S.cp("dve", OBS[0:16, :], po[0:16, :], r=[po], w=[OBS])
                S.tt("dve", OBSb[:, :].rearrange("p (a b) -> p a b", b=128), OBS[:, :].rearrange("p (a b) -> p a b", b=128),
                     DEN[:, 0:4].unsqueeze(2).to_broadcast([128, 4, 128]), ALU.mult, r=[OBS, DEN], w=[OBSb])
                pb = psb()
                for h in range(4):
                    S.tr(pb[:, h * 128:(h + 1) * 128], OBSb[:, h * 128:(h + 1) * 128], IDB[:], r=[OBSb, IDB], w=[pb])
                S.cp("act", OCT[:, :, :].rearrange("p a b -> p (a b)"), pb[:, 0:512], r=[pb], w=[OCT])
                rwkv_out(NT, False)
                merge_out(xt, sc_xm[NT])
        S.barrier()

        es3 = ExitStack()
        with es3:
            WUP = sb(es3, "wup", [128, 8, 4096], BF16)
            WDN = sb(es3, "wdn", [128, 32, 1024], BF16)
            es3w = ExitStack()
            es3w.__enter__()
            STG3 = [sb(es3w, "stg3_%d" % i, [128, 2048]) for i in range(4)]
            s3i = [0]

            def ld3(dst_ap, src_ap, ncol, wtok):
                st = STG3[s3i[0] % 4]
                ce = ("pool", "act", "dve")[s3i[0] % 3]
                s3i[0] += 1
                S.dma("sp", st[:, 0:ncol], src_ap, w=[st])
                S.cp(ce, dst_ap, st[:, 0:ncol], r=[st], w=[wtok])

            for kt in range(8):
                for hf in range(2):
                    ld3(WUP[:, kt, hf * 2048:(hf + 1) * 2048], w_up[kt * 128:(kt + 1) * 128, hf * 2048:(hf + 1) * 2048], 2048, WUP)
            for fc in range(32):
                ld3(WDN[:, fc, :], w_down[fc * 128:(fc + 1) * 128, :], 1024, WDN)
            S.barrier()
            es3w.__exit__(None, None, None)
            XM3 = [sb(es3, "xm3_%d" % i, [128, D]) for i in range(2)]
            SCR3 = sb(es3, "scr3", [128, D])
            HB3 = sb(es3, "hb3", [128, D], BF16)
            SS3 = sb(es3, "ss3", [128, 8])
            RS3 = sb(es3, "rs3", [128, 8])
            H2T = sb(es3, "h2t", [128, 8, 256], BF16)
            S.memset("dve", H2T[:, :, :], 0.0, w=[H2T])
            RL = sb(es3, "rl", [128, 512])
            HID = sb(es3, "hid", [128, 32, 256], BF16)
            YO = [sb(es3, "yo%d" % i, [128, D]) for i in range(2)]
            for t0 in range(0, NT + 1, 2):
                tiles = [t for t in (t0, t0 + 1) if t <= NT]
                for ti, t in enumerate(tiles):
                    xm = XM3[t % 2]
                    S.dma("sp", xm[:, :], sc_xm[t], w=[xm])
                    norm_T(xm, 1, H2T, SCR3, HB3, SS3, RS3, dst=H2T[:, :, ti * 128:(ti + 1) * 128])
                for f2 in range(16):
                    p = psf()
                    for fi in range(2):
                        fc = f2 * 2 + fi
                        for kt in range(8):
                            S.mm(p[:, fi * 256:(fi + 1) * 256], WUP[:, kt, fc * 128:(fc + 1) * 128], H2T[:, kt, :],
                                 start=(kt == 0), stop=(kt == 7), r=[WUP, H2T], w=[p])
                    S.act(RL[:, :], p[:, :], AF.Relu, r=[p], w=[RL])
                    S.tt("dve" if f2 % 2 == 0 else "pool", HID[:, f2 * 2:(f2 + 1) * 2, :].rearrange("p a b -> p (a b)"), RL[:, :], RL[:, :],
                         ALU.mult, r=[RL], w=[HID.sub(f2)])
                for ti, t in enumerate(tiles):
                    xm = XM3[t % 2]
                    yo = YO[t % 2]
                    for cc in range(2):
                        cs = slice(cc * 512, (cc + 1) * 512)
                        p = psf()
                        for fc in range(32):
                            S.mm(p[:, :], HID[:, fc, ti * 128:(ti + 1) * 128], WDN[:, fc, cs], start=(fc == 0), stop=(fc == 31),
                                 r=[HID.sub(fc // 2), WDN], w=[p])
                        S.tt("dve", yo[:, cs], p[:, :], xm[:, cs], ALU.add, r=[p, xm], w=[yo])
                    if t < NT:
                        S.dma("sp", y_o[t * 128:(t + 1) * 128, :], yo[:, :], r=[yo])
                    else:
                        S.dma("sp", ys_o, yo[0:16, :], r=[yo])
        S.barrier()
        print("total ops", S.gseq, {e: len(S.streams[e]) for e in S.ENG}, flush=True)
        import os
        if os.environ.get("KLOG"):
            for g, e, ln in S.oplog[:int(os.environ["KLOG"])]:
                print(g, e, "line", ln)
        S.emit()
    return nc


def build_rest(nc, S, es, L):
    pass


def _host_inputs(NT, c, I):
    f32 = np.float32
    SEQ = I["x_prompt"].shape[1]
    seq, pos = c // 4, c % 4
    t0 = pos * NT * 128
    xp = np.zeros(((NT + 1) * 128, D), f32)
    if pos > 0:
        xp[:] = I["x_prompt"][seq, t0 - 128:t0 + NT * 128]
    else:
        xp[128:] = I["x_prompt"][seq, 0:NT * 128]
    xs = np.zeros((128, D), f32)
    xs[:16] = I["x_sample"][16 * c:16 * c + 16, 0]
    p = np.arange(128)
    ups = (p[:, None] < p[None, :]).astype(f32)
    upi = (p[:, None] <= p[None, :]).astype(f32)
    cmask = np.stack([ups, upi, ups.T.copy(), upi.T.copy(), np.eye(128, dtype=f32)], axis=1)
    ebias = np.stack([-C0H * (p + 1), -C0H * p, C0H * (p + 1), -C0H * (127 - p)], axis=1).astype(f32)
    half = 8
    inv_freq = np.power(np.float32(500000.0), -np.arange(half, dtype=f32) * np.float32(2.0 / 16)).astype(f32)
    rope = np.zeros((128, NT + 2, 16), f32)
    for j in range(NT + 2):
        if j <= NT:
            posj = (t0 - 128 + j * 128 + p).astype(f32)
        else:
            posj = np.full(128, 8192, f32)
        ang = posj[:, None] * inv_freq[None, :]
        rope[:, j, 0:8] = np.cos(ang)
        rope[:, j, 8:16] = np.sin(ang)
    flags = np.zeros((128, 4), f32)
    flags[:, 0] = 1.0 if pos > 0 else 0.0
    for q in range(3):
        flags[:, 1 + q] = 1.0 if q < pos else 0.0
    gains = np.stack([I["norm_mix"][0].reshape(8, 128).T, I["norm_ffn"][0].reshape(8, 128).T,
                      I["mem_norm"][0].reshape(8, 128).T], axis=1).astype(f32)
    vecA = np.concatenate([I["rw_mu"][0], I["rw_w0"][0], I["rw_a0"][0], I["rw_k_k"][0], I["rw_k_a"][0],
                           I["rw_r_k"][0].reshape(-1)])[None, :].astype(f32)
    vecB = np.concatenate([I["rw_ln_w"][0], I["rw_ln_b"][0], I["q_norm"][0], I["k_norm"][0], I["xq_norm"][0],
                           I["xk_norm"][0], I["swa_sinks"][0]])[None, :].astype(f32)
    b0 = 16 * c
    m = {
        "xp": xp, "xs": xs, "cmask": np.ascontiguousarray(cmask), "ebias": ebias, "rope": rope, "flags": flags,
        "gains": np.ascontiguousarray(gains), "vecA": vecA, "vecB": vecB,
        "s_state": I["state_rwkv"][0, b0:b0 + 16].reshape(128, 4096),
        "s_shift": I["state_rwkv_shift"][0, b0:b0 + 16],
        "s_swak": I["cache_swa_k"][0, b0:b0 + 16].reshape(16, 128, 128),
        "s_swav": I["cache_swa_v"][0, b0:b0 + 16].reshape(16, 128, 128),
        "selt": (np.arange(128)[None, :] // 8 == np.arange(16)[:, None]).astype(f32),
        "sel2": (np.arange(128)[:, None] // 8 == np.arange(16)[None, :]).astype(f32),
        "s_memk": I["cache_mem_k"][0, b0:b0 + 16].reshape(16, 256, 512),
        "s_memv": I["cache_mem_v"][0, b0:b0 + 16].reshape(16, 256, 512),
        "memp": I["mem_prompt"][seq],
        "w_in": I["w_in"][0], "w2a": np.concatenate([I["rw_w2"][0], I["rw_a2"][0]], axis=0), "g2": I["rw_g2"][0],
        "w_mkv": I["w_mem_kv"][0],
        "w_br": np.concatenate([I["w_br_a"][0], I["w_br_b"][0], I["w_br_c"][0]], axis=0),
        "w_out": I["w_out"][0], "w_up": I["w_up"][0], "w_down": I["w_down"][0],
    }
    return {k: np.ascontiguousarray(v, dtype=f32) for k, v in m.items()}


_NC_CACHE = {}


def kernel(**inputs):
    I = {k: np.asarray(v) for k, v in inputs.items()}
    B, SEQ, _ = I["x_prompt"].shape
    NT = SEQ // (4 * 128)
    if NT not in _NC_CACHE:
        _NC_CACHE[NT] = build_nc(NT)
    nc = _NC_CACHE[NT]
    in_maps = [_host_inputs(NT, c, I) for c in range(NCORES)]
    res = run_bass_kernel_spmd(nc, in_maps, core_ids=list(range(NCORES)))
    return assemble(res.results, NT)


def assemble(R, NT):
    f32 = np.float32
    SEQ = 4 * NT * 128
    y_prompt = np.zeros((2, SEQ, D), f32)
    for c in range(8):
        y_prompt[c // 4, (c % 4) * NT * 128:(c % 4 + 1) * NT * 128] = R[c]["y"].reshape(NT * 128, D)
    y_sample = np.concatenate([R[c]["ys"].reshape(16, D) for c in range(8)], axis=0).reshape(128, 1, D)
    st_p = np.stack([R[3]["stp"].reshape(8, 64, 64), R[7]["stp"].reshape(8, 64, 64)])[None]
    shift_p = np.stack([R[3]["zlast"].reshape(-1), R[7]["zlast"].reshape(-1)])[None]
    swak_p = np.stack([R[3]["swakp"], R[7]["swakp"]]).reshape(1, 2, 128, 2, 64)
    swav_p = np.stack([R[3]["swavp"], R[7]["swavp"]]).reshape(1, 2, 128, 2, 64)
    memk_p = np.stack([R[0]["memk"], R[4]["memk"]]).reshape(1, 2, 256, 4, 128)
    memv_p = np.stack([R[0]["memv"], R[4]["memv"]]).reshape(1, 2, 256, 4, 128)
    st_s = np.concatenate([R[c]["sts"].reshape(16, 8, 64, 64) for c in range(8)], axis=0)[None]
    shift_s = np.concatenate([R[c]["shifts"].reshape(16, 1792) for c in range(8)], axis=0)[None]
    swak_s = np.concatenate([R[c]["swaks"].reshape(16, 128, 2, 64) for c in range(8)], axis=0)[None]
    swav_s = np.concatenate([R[c]["swavs"].reshape(16, 128, 2, 64) for c in range(8)], axis=0)[None]
    outs = (y_prompt, y_sample, st_p, shift_p, swak_p, swav_p, memk_p, memv_p, st_s, shift_s, swak_s, swav_s)
    return tuple(np.ascontiguousarray(o, dtype=f32) for o in outs)
```

```python
import numpy as np
from contextlib import ExitStack
import concourse.bass as bass
import concourse.mybir as mybir
from concourse.bass_utils import run_bass_kernel_spmd

F32 = mybir.dt.float32
BF16 = mybir.dt.bfloat16
ALU = mybir.AluOpType
AF = mybir.ActivationFunctionType
AX = mybir.AxisListType

D = 1024
NCORES = 8
C0H = float(np.exp(-0.5) / 2.0)
EPS = 1e-5
GN_EPS = 64e-5
SAFE_OPS = 10 ** 9


class Tok:
    __slots__ = ("w", "r")

    def __init__(self):
        self.w = None
        self.r = {}


class Tile:
    def __init__(self, h):
        self.h = h
        self.tok = Tok()
        self.subs = {}

    def sub(self, key):
        if key not in self.subs:
            self.subs[key] = Tok()
        return self.subs[key]

    def __getitem__(self, k):
        return self.h[k]


def _tok(x):
    return x.tok if isinstance(x, Tile) else x


class Sched:
    ENG = ("pe", "act", "dve", "pool", "sp")

    def __init__(self, nc, es, n_dsem=12):
        self.nc = nc
        self.h = {"pe": nc.tensor, "act": nc.scalar, "dve": nc.vector, "pool": nc.gpsimd, "sp": nc.sync}
        self.streams = {e: [] for e in self.ENG}
        self.cnt = {e: 0 for e in self.ENG}
        self.waited = {e: {} for e in self.ENG}
        self.esem = {e: es.enter_context(nc.semaphore("es_" + e)) for e in self.ENG}
        self.dsem = {}
        self.dcnt = {}
        self.dnext = {}
        for q in ("sp", "pool", "act"):
            self.dsem[q] = [es.enter_context(nc.semaphore("ds_%s%d" % (q, i))) for i in range(n_dsem)]
            self.dcnt[q] = [0] * n_dsem
            self.dnext[q] = 0
        self.ccsem = es.enter_context(nc.semaphore("ccsem"))
        self.gseq = 0
        self.oplog = []
        self.gidx = {e: [] for e in self.ENG}

    def _semobj(self, key):
        if key[0] == "e":
            return self.esem[key[1]]
        if key[0] == "d":
            return self.dsem[key[1]][key[2]]
        return self.ccsem

    def _resolve(self, eng, deps):
        waits = []
        best = {}
        for (key, val) in deps:
            if key == ("e", eng) and eng == "pe":
                continue
            if best.get(key, 0) < val:
                best[key] = val
        for key, val in best.items():
            if self.waited[eng].get(key, 0) < val:
                self.waited[eng][key] = val
                waits.append((self._semobj(key), val))
        return waits

    def _deps(self, reads, writes):
        deps = []
        for t in reads:
            t = _tok(t)
            if t.w is not None:
                deps.append(t.w)
        for t in writes:
            t = _tok(t)
            if t.w is not None:
                deps.append(t.w)
            deps.extend(t.r.items())
        return deps

    def _mark(self, me, reads, writes):
        key, val = me
        for t in reads:
            t = _tok(t)
            if t.r.get(key, 0) < val:
                t.r[key] = val
        for t in writes:
            t = _tok(t)
            t.w = me
            t.r = {}

    def op(self, eng, fn, r=(), w=()):
        waits = self._resolve(eng, self._deps(r, w))
        self.cnt[eng] += 1
        me = (("e", eng), self.cnt[eng])
        self._mark(me, r, w)
        self.streams[eng].append((waits, fn, self.esem[eng], 1))
        self.gseq += 1
        self.gidx[eng].append(self.gseq)
        self._log(eng)

    def _log(self, eng):
        import sys
        f = sys._getframe(2)
        while f.f_code.co_name in ("mm", "tr", "act", "tt", "ts", "stt", "red", "cp", "memset", "op", "dma"):
            f = f.f_back
        self.oplog.append((self.gseq, eng, f.f_lineno))

    def dma(self, q, out, in_, r=(), w=()):
        import os
        if q == "pool" and os.environ.get("KNOPOOL"):
            return
        i = self.dnext[q]
        self.dnext[q] = (i + 1) % len(self.dsem[q])
        deps = self._deps(r, w)
        if self.dcnt[q][i] > 0:
            deps.append((("d", q, i), self.dcnt[q][i]))
        waits = self._resolve(q, deps)
        self.dcnt[q][i] += 16
        me = (("d", q, i), self.dcnt[q][i])
        self._mark(me, r, w)
        self.streams[q].append((waits, lambda e, o=out, s=in_: e.dma_start(out=o, in_=s), self.dsem[q][i], 16))
        self.gseq += 1
        self.gidx[q].append(self.gseq)
        self._log("dma-" + q)

    def barrier(self, toks=()):
        deps = []
        for e in self.ENG:
            if self.cnt[e] > 0:
                deps.append((("e", e), self.cnt[e]))
        for q in self.dsem:
            for i, c in enumerate(self.dcnt[q]):
                if c > 0:
                    deps.append((("d", q, i), c))
        for e in self.ENG:
            waits = self._resolve(e, deps)
            if waits:
                self.streams[e].append((waits, None, None, 0))
                self.gidx[e].append(self.gseq)

    def emit(self):
        import os
        nc = self.nc
        lim = int(os.environ.get("KSTOP", "0")) or SAFE_OPS
        totals = {}
        for name in self.ENG:
            for (waits, fn, sem, inc), gi in zip(self.streams[name], self.gidx[name]):
                if gi > lim or fn is None:
                    continue
                k = id(sem)
                totals[k] = (sem, totals.get(k, (sem, 0))[1] + inc)
        with nc.Block() as block:
            def runner(name):
                def run(eng):
                    for (waits, fn, sem, inc), gi in zip(self.streams[name], self.gidx[name]):
                        if gi > lim:
                            break
                        for s, v in waits:
                            eng.wait_ge(s, v)
                        if fn is not None:
                            fn(eng).then_inc(sem, inc)
                    for s, v in totals.values():
                        eng.wait_ge(s, v)
                return run
            block.tensor(runner("pe"))
            block.scalar(runner("act"))
            block.vector(runner("dve"))
            block.gpsimd(runner("pool"))
            block.sync(runner("sp"))

    def mm(self, out, lhsT, rhs, start=True, stop=True, r=(), w=()):
        self.op("pe", lambda e: e.matmul(out, lhsT, rhs, start=start, stop=stop), r, w)

    def tr(self, out, in_, ident, r=(), w=()):
        self.op("pe", lambda e: e.transpose(out, in_, ident), r, w)

    def act(self, out, in_, func, bias=None, scale=None, r=(), w=()):
        kw = {}
        if bias is not None:
            kw["bias"] = bias
        if scale is not None:
            kw["scale"] = scale
        self.op("act", lambda e: e.activation(out, in_, func, **kw), r, w)

    def tt(self, eng, out, in0, in1, op, r=(), w=()):
        self.op(eng, lambda e: e.tensor_tensor(out, in0, in1, op), r, w)

    def ts(self, eng, out, in0, s1, op0, s2=None, op1=None, r=(), w=()):
        if op1 is None:
            self.op(eng, lambda e: e.tensor_scalar(out, in0, s1, None, op0), r, w)
        else:
            self.op(eng, lambda e: e.tensor_scalar(out, in0, s1, s2, op0, op1), r, w)

    def stt(self, out, in0, scalar, in1, op0, op1, r=(), w=()):
        self.op("dve", lambda e: e.scalar_tensor_tensor(out, in0, scalar, in1, op0, op1), r, w)

    def red(self, out, in_, r=(), w=(), op=ALU.add):
        self.op("dve", lambda e: e.tensor_reduce(out, in_, AX.X, op), r, w)

    def cp(self, eng, out, in_, r=(), w=()):
        if eng == "act":
            self.op("act", lambda e: e.activation(out, in_, AF.Copy), r, w)
        else:
            self.op(eng, lambda e: e.tensor_copy(out, in_), r, w)

    def memset(self, eng, ap, val, w=()):
        self.op(eng, lambda e: e.memset(ap, val), (), w)


def build_nc(NT):
    nc = bass.Bass("TRN2", target_bir_lowering=False)
    NP = NT + 1
    NR = NT + 2

    def din(name, shape):
        return nc.dram_tensor(name, list(shape), F32, kind="ExternalInput").ap()

    def dout(name, shape):
        return nc.dram_tensor(name, list(shape), F32, kind="ExternalOutput").ap()

    xp = din("xp", [NP * 128, D])
    xs = din("xs", [128, D])
    cmask = din("cmask", [128, 5, 128])
    ebias = din("ebias", [128, 4])
    rope = din("rope", [128, NR, 16])
    flags = din("flags", [128, 4])
    gains = din("gains", [128, 3, 8])
    vecA = din("vecA", [1, 4352])
    vecB = din("vecB", [1, 1416])
    s_state = din("s_state", [128, 4096])
    s_shift = din("s_shift", [16, 1792])
    s_swak = din("s_swak", [16, 128, 128])
    s_swav = din("s_swav", [16, 128, 128])
    selt = din("selt", [16, 128])
    sel2 = din("sel2", [128, 16])
    s_memk = din("s_memk", [16, 256, 512])
    s_memv = din("s_memv", [16, 256, 512])
    memp = din("memp", [256, D])
    w_in = din("w_in", [D, 6144])
    w2a = din("w2a", [128, 512])
    g2 = din("g2", [128, 512])
    w_mkv = din("w_mkv", [D, 1024])
    w_br = din("w_br", [1536, D])
    w_out = din("w_out", [D, D])
    w_up = din("w_up", [D, 4096])
    w_down = din("w_down", [4096, D])

    y_o = dout("y", [NT * 128, D])
    ys_o = dout("ys", [16, D])
    stp_o = dout("stp", [8, 64, 64])
    zlast_o = dout("zlast", [1, 1792])
    swakp_o = dout("swakp", [128, 128])
    swavp_o = dout("swavp", [128, 128])
    memk_o = dout("memk", [256, 512])
    memv_o = dout("memv", [256, 512])
    sts_o = dout("sts", [128, 4096])
    shifts_o = dout("shifts", [16, 1792])
    swaks_o = dout("swaks", [16, 128, 128])
    swavs_o = dout("swavs", [16, 128, 128])

    sc_yp = nc.dram_tensor("sc_yp", [NT + 1, 128, 512], F32).ap()
    sc_bv = nc.dram_tensor("sc_bv", [NT + 1, 128, 512], F32).ap()
    sc_g = nc.dram_tensor("sc_g", [NT + 1, 128, 512], F32).ap()
    sc_gt = nc.dram_tensor("sc_gt", [NT, 64, 1024], BF16).ap()
    sc_xm = nc.dram_tensor("sc_xm", [NT + 1, 128, D], F32).ap()
    sc_s1 = nc.dram_tensor("sc_s1", [6, 16, 512], F32).ap()
    sc_s2 = nc.dram_tensor("sc_s2", [16, 512], F32).ap()
    sc_q = nc.dram_tensor("sc_q", [16, 1024], F32).ap()
    cc_src = nc.dram_tensor("cc_src", [64, 1024], F32)
    cc_dst = nc.dram_tensor("cc_dst", [4 * 64, 1024], F32)

    es = ExitStack()
    with es:
        S = Sched(nc, es)

        uid = [0]

        def sb(es_, name, shape, dt=F32):
            uid[0] += 1
            return Tile(es_.enter_context(nc.sbuf_tensor("t%d_%s" % (uid[0], name), list(shape), dt)))

        def pst(es_, name, shape, dt=F32):
            return Tile(es_.enter_context(nc.psum_tensor("p_" + name, list(shape), dt)))

        PF = [pst(es, "pf%d" % i, [128, 512], F32) for i in range(6)]
        PB = [pst(es, "pb%d" % i, [128, 1024], BF16) for i in range(2)]
        pfi = [0]
        pbi = [0]

        def psf():
            t = PF[pfi[0] % 5]
            pfi[0] += 1
            return t

        def psb():
            t = PB[pbi[0] % 2]
            pbi[0] += 1
            return t

        MASK = sb(es, "mask", [128, 5, 128])
        EB = sb(es, "eb", [128, 4])
        ROPE = sb(es, "rope", [128, NR, 16])
        FLG = sb(es, "flg", [128, 4])
        GAIN = sb(es, "gain", [128, 3, 8])
        IDB = sb(es, "idb", [128, 128], BF16)
        NEGH = sb(es, "negh", [128, 16])
        ONES2 = sb(es, "ones2", [128, 2])
        ONESB = sb(es, "onesb", [128, 128], BF16)
        S.dma("sp", MASK[:], cmask, w=[MASK])
        S.dma("sp", EB[:], ebias, w=[EB])
        S.dma("sp", ROPE[:], rope, w=[ROPE])
        S.dma("sp", FLG[:], flags, w=[FLG])
        S.dma("sp", GAIN[:], gains, w=[GAIN])
        S.cp("dve", IDB[:], MASK[:, 4, :], r=[MASK], w=[IDB])
        S.memset("dve", NEGH[:], -0.5, w=[NEGH])
        S.memset("dve", ONES2[:], 1.0, w=[ONES2])
        ONESF = sb(es, "onesf", [128, 64])
        CB128 = sb(es, "cb128", [128, 1])
        CBH = sb(es, "cbh", [128, 1])
        S.memset("dve", CBH[:], -C0H, w=[CBH])
        S.memset("dve", CB128[:], -C0H * 128.0, w=[CB128])
        S.memset("dve", ONESF[:], 1.0, w=[ONESF])
        S.memset("dve", ONESB[:], 1.0, w=[ONESB])
        UPS, UPI, LOS, LOI, IDF = (MASK[:, i, :] for i in range(5))
        SINB = sb(es, "sinb", [64, 8, 64], BF16)

        def rstd_of(x_ap, n, gs, eps, scr, ss, rs, r_toks):
            S.act(scr[:, 0:n * gs], x_ap, AF.Square, r=r_toks, w=[scr])
            S.red(ss[:, 0:n], scr[:, 0:n * gs].rearrange("p (a b) -> p a b", b=gs), r=[scr], w=[ss])
            S.ts("dve", ss[:, 0:n], ss[:, 0:n], 1.0 / gs, ALU.mult, eps, ALU.add, r=[ss], w=[ss])
            S.tt("pool", rs[:, 0:n], ss[:, 0:n], NEGH[:, 0:n], ALU.pow, r=[ss, NEGH], w=[rs])

        def norm_T(xt, gi, hT, scr, hb, ss, rs, dst=None):
            rstd_of(xt[:, :], 1, D, EPS, scr, ss, rs, [xt])
            S.act(hb[:, :], xt[:, :], AF.Copy, scale=rs[:, 0:1], r=[xt, rs], w=[hb])
            pb = psb()
            for kt in range(8):
                S.tr(pb[:, kt * 128:(kt + 1) * 128], hb[:, kt * 128:(kt + 1) * 128], IDB[:], r=[hb, IDB], w=[pb])
            S.tt("dve", (hT[:, :, :] if dst is None else dst), pb[:, :].rearrange("p (a b) -> p a b", b=128),
                 GAIN[:, gi, :].unsqueeze(2).to_broadcast([128, 8, 128]), ALU.mult, r=[pb, GAIN], w=[hT])

        es1 = ExitStack()
        with es1:
            WZ = sb(es1, "wz", [128, 8, 1792], BF16)
            W2A = sb(es1, "w2a", [128, 512], BF16)
            G2 = sb(es1, "g2", [128, 512], BF16)
            VA = sb(es1, "va", [128, 4352])
            es1w = ExitStack()
            es1w.__enter__()
            STG = [sb(es1w, "stg%d" % i, [128, 1792]) for i in range(4)]
            stg_i = [0]

            def load_cast(dst_ap, src_ap, ncol, wtok):
                st = STG[stg_i[0] % 4]
                ce = ("pool", "act", "dve")[stg_i[0] % 3]
                stg_i[0] += 1
                S.dma("sp", st[:, 0:ncol], src_ap, w=[st])
                S.cp(ce, dst_ap, st[:, 0:ncol], r=[st], w=[wtok])

            for kt in range(8):
                load_cast(WZ[:, kt, :], w_in[kt * 128:(kt + 1) * 128, 0:1792], 1792, WZ)
            load_cast(W2A[:, :], w2a, 512, W2A)
            load_cast(G2[:, :], g2, 512, G2)
            S.barrier()
            es1w.__exit__(None, None, None)
            S.dma("sp", VA[:], vecA.partition_broadcast(128).rearrange("p a n -> p (a n)"), w=[VA])
            MU = VA[:, 0:1792]
            W0 = VA[:, 1792:2304]
            A0 = VA[:, 2304:2816]
            K_K = VA[:, 2816:3328]
            K_A = VA[:, 3328:3840]
            R_K = VA[:, 3840:4352]

            XT = [sb(es1, "xt%d" % i, [128, D]) for i in range(2)]
            SCR = sb(es1, "scr", [128, D])
            HB = sb(es1, "hb", [128, D], BF16)
            SS = sb(es1, "ss", [128, 8])
            RS = sb(es1, "rs", [128, 8])
            HT = [sb(es1, "ht%d" % i, [128, 8, 128], BF16) for i in range(2)]
            Z = [sb(es1, "z%d" % i, [128, 1792]) for i in range(2)]
            ZP = sb(es1, "zp", [128, 1792])
            ZS = sb(es1, "zs", [128, 1792])
            LC = sb(es1, "lc", [128, 256], BF16)
            LCT = sb(es1, "lct", [128, 256], BF16)
            TW = sb(es1, "tw", [128, 512])
            AA = sb(es1, "aa", [128, 512])
            GO = sb(es1, "go", [128, 512])
            KK = sb(es1, "kk", [128, 512])
            KH = sb(es1, "kh", [128, 512])
            BB = sb(es1, "bb", [128, 512])
            T1 = sb(es1, "t1", [128, 512])
            T2 = sb(es1, "t2", [128, 512])
            BV = sb(es1, "bv", [128, 512])
            SS8 = sb(es1, "ss8", [128, 8])
            RN8 = sb(es1, "rn8", [128, 8])
            BS8 = sb(es1, "bs8", [128, 8])

            def rwkv_pre(zc, zp_ready_toks):
                S.tt("dve", ZS[:, :], ZP[:, :], zc[:, :], ALU.subtract, r=[ZP, zc], w=[ZS])
                S.tt("dve", ZS[:, :], ZS[:, :], MU, ALU.mult, r=[ZS, VA], w=[ZS])
                S.tt("dve", ZS[:, :], ZS[:, :], zc[:, :], ALU.add, r=[ZS, zc], w=[ZS])
                r_ = ZS[:, 0:512]
                k_ = ZS[:, 512:1024]
                S.act(LC[:, 0:64], ZS[:, 1536:1600], AF.Tanh, r=[ZS], w=[LC])
                S.act(LC[:, 64:128], ZS[:, 1600:1664], AF.Copy, r=[ZS], w=[LC])
                S.act(LC[:, 128:256], ZS[:, 1664:1792], AF.Tanh, scale=0.5, r=[ZS], w=[LC])
                S.ts("dve", LC[:, 128:256], LC[:, 128:256], 0.5, ALU.mult, 0.5, ALU.add, r=[LC], w=[LC])
                pb = psb()
                S.tr(pb[:, 0:128], LC[:, 0:128], IDB[:], r=[LC, IDB], w=[pb])
                S.tr(pb[:, 128:256], LC[:, 128:256], IDB[:], r=[LC, IDB], w=[pb])
                S.cp("act", LCT[:, :], pb[:, 0:256], r=[pb], w=[LCT])
                yield
                pw, pa, pg = psf(), psf(), psf()
                S.mm(pw[:, :], LCT[0:64, 0:128], W2A[0:64, :], r=[LCT, W2A], w=[pw])
                S.mm(pa[:, :], LCT[64:128, 0:128], W2A[64:128, :], r=[LCT, W2A], w=[pa])
                S.mm(pg[:, :], LCT[:, 128:256], G2[:, :], r=[LCT, G2], w=[pg])
                S.tt("dve", TW[:, :], pw[:, :], W0, ALU.add, r=[pw, VA], w=[TW])
                S.act(TW[:, :], TW[:, :], AF.Tanh, scale=0.5, r=[TW], w=[TW])
                S.tt("dve", AA[:, :], pa[:, :], A0, ALU.add, r=[pa, VA], w=[AA])
                S.act(AA[:, :], AA[:, :], AF.Tanh, scale=0.5, r=[AA], w=[AA])
                S.ts("dve", AA[:, :], AA[:, :], 0.5, ALU.mult, 0.5, ALU.add, r=[AA], w=[AA])
                S.cp("act", GO[:, :], pg[:, :], r=[pg], w=[GO])
                yield
                S.tt("dve", KK[:, :], k_, K_K, ALU.mult, r=[ZS, VA], w=[KK])
                S.act(T1[:, :], KK[:, :], AF.Square, r=[KK], w=[T1])
                S.red(SS8[:, :], T1[:, :].rearrange("p (a b) -> p a b", b=64), r=[T1], w=[SS8])
                S.ts("dve", SS8[:, :], SS8[:, :], 1e-24, ALU.max, r=[SS8], w=[SS8])
                S.tt("pool", RN8[:, :], SS8[:, :], NEGH[:, 0:8], ALU.pow, r=[SS8, NEGH], w=[RN8])
                S.tt("dve", KK[:, :].rearrange("p (a b) -> p a b", b=64), KK[:, :].rearrange("p (a b) -> p a b", b=64),
                     RN8[:, :].unsqueeze(2).to_broadcast([128, 8, 64]), ALU.mult, r=[KK, RN8], w=[KK])
                S.stt(T1[:, :], AA[:, :], -1.0, K_A, ALU.add, ALU.mult, r=[AA, VA], w=[T1])
                S.stt(KH[:, :], T1[:, :], 1.0, k_, ALU.add, ALU.mult, r=[T1, ZS], w=[KH])
                yield
                S.tt("dve", BB[:, :], KK[:, :], AA[:, :], ALU.mult, r=[KK, AA], w=[BB])
                S.tt("dve", T2[:, :], r_, KH[:, :], ALU.mult, r=[ZS, KH], w=[T2])
                S.tt("dve", T2[:, :], T2[:, :], R_K, ALU.mult, r=[T2, VA], w=[T2])
                S.red(BS8[:, :], T2[:, :].rearrange("p (a b) -> p a b", b=64), r=[T2], w=[BS8])
                S.tt("dve", BV[:, :].rearrange("p (a b) -> p a b", b=64), ZS[:, 1024:1536].rearrange("p (a b) -> p a b", b=64),
                     BS8[:, :].unsqueeze(2).to_broadcast([128, 8, 64]), ALU.mult, r=[ZS, BS8], w=[BV])

            def zproj(xsrc_ap, zc, ht, xt):
                S.dma("sp", xt[:, :], xsrc_ap, w=[xt])
                norm_T(xt, 0, ht, SCR, HB, SS, RS)
                for c, (lo, hi) in enumerate(((0, 512), (512, 1024), (1024, 1536), (1536, 1792))):
                    p = psf()
                    for kt in range(8):
                        S.mm(p[:, 0:hi - lo], ht[:, kt, :], WZ[:, kt, lo:hi], start=(kt == 0), stop=(kt == 7),
                             r=[ht, WZ], w=[p])
                    S.cp("act", zc[:, lo:hi], p[:, 0:hi - lo], r=[p], w=[zc])
                    yield

            es1p = ExitStack()
            with es1p:
                SP = sb(es1p, "sp", [64, 8, 128])
                SPB = sb(es1p, "spb", [64, 8, 128], BF16)
                es1q = ExitStack()
                es1q.__enter__()
                M4 = sb(es1q, "m4", [128, 4, 128])
                S.cp("dve", M4[:, 0:2, :], MASK[:, 0:1, :].to_broadcast([128, 2, 128]), r=[MASK], w=[M4])
                S.cp("dve", M4[:, 2:4, :], MASK[:, 1:2, :].to_broadcast([128, 2, 128]), r=[MASK], w=[M4])
                GI = sb(es1q, "gi", [128, 512])
                GV = sb(es1q, "gv", [128, 512])
                GX = sb(es1q, "gx", [128, 512])
                GE = sb(es1q, "ge", [128, 512])
                TM = sb(es1q, "tm", [128, 7, 512], BF16)
                FT = sb(es1q, "ft", [128, 4, 4, 128], BF16)
                GC = sb(es1q, "gc", [64, 512])
                HW = [sb(es1q, "hw%d" % i, [128, 5, 128], BF16) for i in range(8)]
                IW = [[sb(es1q, "iw%d_%d" % (i, j), [128, 3, 128], BF16) for j in range(2)] for i in range(8)]
                XF = [sb(es1q, "xf%d" % i, [128, 128], BF16) for i in range(8)]
                RH = [sb(es1q, "rh%d" % i, [128, 128], BF16) for i in range(8)]
                WU = [sb(es1q, "wu%d" % i, [128, 128], BF16) for i in range(8)]
                PTs = [sb(es1q, "pt%d" % i, [64, 64]) for i in range(8)]
                HS = [sb(es1q, "hs%d" % i, [64, 64]) for i in range(8)]
                QT = [sb(es1q, "qt%d" % i, [64, 128], BF16) for i in range(8)]
                GTS = [sb(es1q, "gts%d" % i, [64, 8, 128], BF16) for i in range(2)]
                YP = [sb(es1q, "yp%d" % i, [128, 512]) for i in range(2)]

                S.memset("dve", SP[:, :, 0:64], 0.0, w=[SP])
                for h in range(8):
                    S.cp("dve", SP[:, h, 64:128], MASK[0:64, 4, 0:64], r=[MASK], w=[SP])
                S.cp("act", SPB[:, :, :], SP[:, :, :], r=[SP], w=[SPB])

                TMs = [TM, sb(es1q, "tm_b", [128, 7, 512], BF16)]
                FTs = [FT, sb(es1q, "ft_b", [128, 4, 4, 128], BF16)]
                GCs = [GC, sb(es1q, "gc_b", [64, 512])]

                def pre_gen(j):
                    TM, FT, GC = TMs[j % 2], FTs[j % 2], GCs[j % 2]
                    zc = Z[j % 2]
                    yield from zproj(xp[j * 128:(j + 1) * 128, :], zc, HT[j % 2], XT[j % 2])
                    if j == 0:
                        return
                    zprev = Z[(j - 1) % 2]
                    S.dma("sp", ZP[1:128, :], zc[0:127, :], r=[zc], w=[ZP])
                    S.dma("sp", ZP[0:1, :], zprev[127:128, :], r=[zprev], w=[ZP])
                    if j == NT:
                        S.dma("sp", zlast_o, zc[127:128, :], r=[zc])
                    yield from rwkv_pre(zc, None)
                    jj = j - 1
                    S.dma("sp", sc_bv[jj], BV[:, :], r=[BV])
                    S.dma("sp", sc_g[jj], GO[:, :], r=[GO])
                    p_i, p_s, p_r = psf(), psf(), psf()
                    S.mm(p_i[:, :], UPI, TW[:, :], r=[MASK, TW], w=[p_i])
                    S.mm(p_s[:, :], UPS, TW[:, :], r=[MASK, TW], w=[p_s])
                    S.mm(p_r[:, :], LOS, TW[:, :], r=[MASK, TW], w=[p_r])
                    S.act(GI[:, :], p_i[:, :], AF.Exp, bias=EB[:, 0:1], scale=-C0H, r=[p_i, EB], w=[GI])
                    S.act(GV[:, :], p_i[:, :], AF.Exp, bias=EB[:, 2:3], scale=C0H, r=[p_i, EB], w=[GV])
                    S.act(GX[:, :], p_s[:, :], AF.Exp, bias=EB[:, 1:2], scale=-C0H, r=[p_s, EB], w=[GX])
                    S.act(GE[:, :], p_r[:, :], AF.Exp, bias=EB[:, 3:4], scale=-C0H, r=[p_r, EB], w=[GE])
                    yield
                    pc = psf()
                    for h in range(8):
                        S.mm(pc[0:64, h * 64:(h + 1) * 64], TW[:, h * 64:(h + 1) * 64], ONESF[:, :], r=[TW, ONESF], w=[pc])
                    S.act(GC[:, :], pc[0:64, :], AF.Exp, bias=CB128[0:64, 0:1], scale=-C0H, r=[pc, CB128], w=[GC])
                    S.tt("dve", TM[:, 0, :], ZS[:, 0:512], GI[:, :], ALU.mult, r=[ZS, GI], w=[TM.sub(0)])
                    S.stt(TM[:, 1, :], KK[:, :], -1.0, GX[:, :], ALU.mult, ALU.mult, r=[KK, GX], w=[TM.sub(1)])
                    S.tt("dve", TM[:, 2, :], BB[:, :], GV[:, :], ALU.mult, r=[BB, GV], w=[TM.sub(2)])
                    S.tt("pool", TM[:, 3, :], KH[:, :], GV[:, :], ALU.mult, r=[KH, GV], w=[TM.sub(3)])
                    S.tt("pool", TM[:, 4, :], BB[:, :], GE[:, :], ALU.mult, r=[BB, GE], w=[TM.sub(4)])
                    S.tt("pool", TM[:, 5, :], KH[:, :], GE[:, :], ALU.mult, r=[KH, GE], w=[TM.sub(5)])
                    S.cp("act", TM[:, 6, :], ZS[:, 1024:1536], r=[ZS], w=[TM.sub(6)])
                    yield
                    srcslot = (1, 0, 2, 3)
                    for half in range(2):
                        pb = psb()
                        for hpp in range(2):
                            hp = half * 2 + hpp
                            for sl in range(4):
                                o = (hpp * 4 + sl) * 128
                                S.tr(pb[:, o:o + 128], TM[:, srcslot[sl], hp * 128:(hp + 1) * 128], IDB[:],
                                     r=[TM.sub(srcslot[sl]), IDB], w=[pb])
                        S.cp("act" if half == 0 else "dve",
                             FT[:, half * 2:half * 2 + 2, :, :].rearrange("p a b c -> p (a b c)"), pb[:, :],
                             r=[pb], w=[FT.sub(half)])
                        yield

                def stages(j, step):
                    TM, FT, GC = TMs[j % 2], FTs[j % 2], GCs[j % 2]
                    jj = j - 1
                    ypar = YP[jj % 2]
                    gts = GTS[jj % 2]
                    p_y = PF[5]
                    H8 = range(8)

                    def hv(h):
                        hp, base = h // 2, 64 * (h % 2)
                        return hp, FT.sub(hp // 2), slice(h * 64, (h + 1) * 64), slice(base, base + 64)

                    for h in H8:
                        hp, fs, hsl, ps_ = hv(h)
                        hw = HW[h]
                        aT, rT, bT, kT = (FT[ps_, hp, i, :] for i in range(4))
                        pA = psf()
                        S.mm(pA[:, 0:128], bT, aT, r=[fs], w=[pA])
                        S.mm(pA[:, 128:256], kT, aT, r=[fs], w=[pA])
                        S.mm(pA[:, 256:384], bT, rT, r=[fs], w=[pA])
                        S.mm(pA[:, 384:512], kT, rT, r=[fs], w=[pA])
                        pN = psf()
                        S.mm(pN[:, 0:128], aT, bT, r=[fs], w=[pN])
                        S.tt("dve", hw[:, 1:5, :], pA[:, :].rearrange("p (a b) -> p a b", b=128), M4[:, :, :], ALU.mult,
                             r=[pA, M4], w=[hw])
                        S.tt("dve", hw[:, 0, :], pN[:, 0:128], LOS, ALU.mult, r=[pN, MASK], w=[hw])
                    step()
                    curs = {}
                    for h in H8:
                        hw = HW[h]
                        pI = psf()
                        S.mm(pI[:, 0:128], hw[:, 1, :], hw[:, 0, :], r=[hw], w=[pI])
                        S.mm(pI[:, 128:256], hw[:, 0, :], hw[:, 1, :], r=[hw], w=[pI])
                        cur = IW[h][0]
                        S.cp("act", cur[:, 0:2, :].rearrange("p a b -> p (a b)"), pI[:, 0:256], r=[pI], w=[cur])
                        S.tt("pool", cur[:, 2, :], hw[:, 1, :], IDF, ALU.add, r=[hw, MASK], w=[cur])
                        curs[h] = cur
                    step()
                    for lev in range(1, 6):
                        step()
                        for h in H8:
                            cur = curs[h]
                            nxt = IW[h][lev % 2]
                            pI = psf()
                            S.mm(pI[:, 0:128], cur[:, 1, :], cur[:, 0, :], r=[cur], w=[pI])
                            S.mm(pI[:, 128:384], cur[:, 0, :], cur[:, 1:3, :].rearrange("p a b -> p (a b)"), r=[cur], w=[pI])
                            S.cp("act", nxt[:, 0:2, :].rearrange("p a b -> p (a b)"), pI[:, 0:256], r=[pI], w=[nxt])
                            S.tt("dve", nxt[:, 2, :], cur[:, 2, :], pI[:, 256:384], ALU.add, r=[cur, pI], w=[nxt])
                            curs[h] = nxt
                    step()
                    for h in H8:
                        cur = curs[h]
                        pI = psf()
                        S.mm(pI[:, 0:128], cur[:, 0, :], cur[:, 2, :], r=[cur], w=[pI])
                        S.tt("dve", XF[h][:, :], cur[:, 2, :], pI[:, 0:128], ALU.add, r=[cur, pI], w=[XF[h]])
                    step()
                    for h in H8:
                        hp, fs, hsl, ps_ = hv(h)
                        hw, rh = HW[h], RH[h]
                        pV = psf()
                        S.mm(pV[:, 0:64], hw[:, 2, :], TM[:, 6, hsl], r=[hw, TM.sub(6)], w=[pV])
                        S.cp("pool", rh[:, 0:64], TM[:, 1, hsl], r=[TM.sub(1)], w=[rh])
                        S.cp("act", rh[:, 64:128], pV[:, 0:64], r=[pV], w=[rh])
                    step()
                    for h in H8:
                        pW = psf()
                        S.mm(pW[:, 0:128], XF[h][:, :], RH[h][:, :], r=[XF[h], RH[h]], w=[pW])
                        S.cp("act", WU[h][:, :], pW[:, 0:128], r=[pW], w=[WU[h]])
                    step()
                    for h in H8:
                        hp, fs, hsl, ps_ = hv(h)
                        hw, wu, pts, hs, qt = HW[h], WU[h], PTs[h], HS[h], QT[h]
                        pP = psf()
                        S.mm(pP[0:64, 0:64], wu[:, 0:64], TM[:, 4, hsl], r=[wu, TM.sub(4)], w=[pP])
                        S.mm(pP[0:64, 64:128], TM[:, 4, hsl], wu[:, 64:128], start=True, stop=False, r=[wu, TM.sub(4)], w=[pP])
                        S.mm(pP[0:64, 64:128], TM[:, 5, hsl], TM[:, 6, hsl], start=False, stop=True,
                             r=[TM.sub(5), TM.sub(6)], w=[pP])
                        S.mm(pP[0:64, 128:256], wu[:, 0:64], hw[:, 3, :], start=True, stop=False, r=[wu, hw], w=[pP])
                        S.mm(pP[0:64, 128:256], TM[:, 0, hsl], IDB[:, :], start=False, stop=True, r=[TM.sub(0), IDB], w=[pP])
                        S.stt(pts[:, :], MASK[0:64, 4, 0:64], GC[:, h * 64:h * 64 + 1], pP[0:64, 0:64], ALU.mult, ALU.add,
                              r=[MASK, GC, pP], w=[pts])
                        S.cp("dve", hs[:, :], pP[0:64, 64:128], r=[pP], w=[hs])
                        S.cp("dve", qt[:, :], pP[0:64, 128:256], r=[pP], w=[qt])
                    step()
                    for h in H8:
                        hp, fs, hsl, ps_ = hv(h)
                        hw, wu, qt = HW[h], WU[h], QT[h]
                        S.mm(p_y[:, hsl], hw[:, 3, :], wu[:, 64:128], start=True, stop=False, r=[hw, wu], w=[p_y])
                        S.mm(p_y[:, hsl], hw[:, 4, :], TM[:, 6, hsl], start=False, stop=False, r=[hw, TM.sub(6)], w=[p_y])
                        S.mm(p_y[:, hsl], qt[:, :], SPB[:, h, 0:64], start=False, stop=True, r=[qt, SPB.sub(h)], w=[p_y])
                    step()
                    for h in H8:
                        qt = QT[h]
                        pG = psf()
                        S.mm(pG[0:64, 0:128], SPB[:, h, 64:128], qt[:, :], r=[SPB.sub(h), qt], w=[pG])
                        S.cp("act", gts[:, h, :], pG[0:64, 0:128], r=[pG], w=[gts])
                    step()
                    for h in H8:
                        pts, hs = PTs[h], HS[h]
                        pS = psf()
                        S.mm(pS[0:64, 0:128], pts[:, :], SP[:, h, :], r=[pts, SP.sub(h)], w=[pS])
                        S.tt("dve", SP[:, h, 0:64], pS[0:64, 0:64], hs[:, :], ALU.add, r=[pS, hs], w=[SP.sub(h)])
                        S.cp("act", SP[:, h, 64:128], pS[0:64, 64:128], r=[pS], w=[SP.sub(h)])
                        S.cp("pool", SPB[:, h, :], SP[:, h, :], r=[SP.sub(h)], w=[SPB.sub(h)])
                    S.cp("act", ypar[:, :], p_y[:, :], r=[p_y], w=[ypar])
                    S.dma("sp", sc_yp[jj], ypar[:, :], r=[ypar])
                    S.dma("sp", sc_gt[jj], gts[:, :, :].rearrange("p a b -> p (a b)"), r=[gts])


                for _ in pre_gen(0):
                    pass
                for _ in pre_gen(1):
                    pass
                for j in range(1, NP):
                    g = pre_gen(j + 1) if j + 1 < NP else iter(())
                    stages(j, lambda g=g: next(g, None))
                    for _ in g:
                        pass
                S.barrier()
                es1q.__exit__(None, None, None)
                EX = sb(es1p, "ex", [64, 8, 128])
                EXA = sb(es1p, "exa", [64, 4, 8, 128])
                SIN = sb(es1p, "sin", [64, 8, 64])
                SC1 = sb(es1p, "sc1", [64, 8, 64])
                SFT = sb(es1p, "sft", [64, 8, 64])
                hall = [SP.sub(h) for h in range(8)]
                S.cp("dve", EX[:, :, 0:64], SP[:, :, 0:64], r=hall, w=[EX])
                pT = psf()
                for h in range(8):
                    S.tr(pT[0:64, h * 64:(h + 1) * 64], SP[:, h, 64:128], MASK[0:64, 4, 0:64], r=hall + [MASK], w=[pT])
                S.cp("dve", EX[:, :, 64:128], pT[0:64, :].rearrange("p (a b) -> p a b", b=64), r=[pT], w=[EX])
                S.dma("pool", cc_src.ap(), EX[:, :, :].rearrange("p a b -> p (a b)"), r=[EX], w=[EXA.sub("src")])
                deps = S._deps([EXA.sub("src")], [EXA.sub("dst")])
                waits = S._resolve("pool", deps)
                S.cnt["pool"] += 1
                me = (("e", "pool"), S.cnt["pool"])
                S._mark(me, [EXA.sub("src")], [EXA.sub("dst")])
                S.streams["pool"].append((waits, lambda e: e.collective_compute(
                    "AllGather", ALU.bypass, replica_groups=[[0, 1, 2, 3], [4, 5, 6, 7]],
                    ins=[cc_src.ap()], outs=[cc_dst.ap()]), S.esem["pool"], 1))
                S.gseq += 1
                S.gidx["pool"].append(S.gseq)
                S.dma("pool", EXA[:, :, :, :].rearrange("p r a b -> p r (a b)"),
                      cc_dst.ap().rearrange("(r p) n -> p r n", p=64), r=[EXA.sub("dst")], w=[EXA])
                S.memset("dve", SIN[:, :, :], 0.0, w=[SIN])
                for q in range(3):
                    pc_ = psf()
                    for h in range(8):
                        S.mm(pc_[0:64, h * 64:(h + 1) * 64], EXA[:, q, h, 64:128], SIN[:, h, :], r=[EXA, SIN], w=[pc_])
                    S.tt("dve", SC1[:, :, :], pc_[0:64, :].rearrange("p (a b) -> p a b", b=64), EXA[:, q, :, 0:64], ALU.add,
                         r=[pc_, EXA], w=[SC1])
                    S.tt("dve", SC1[:, :, :], SC1[:, :, :], SIN[:, :, :], ALU.subtract, r=[SC1, SIN], w=[SC1])
                    S.stt(SIN[:, :, :].rearrange("p a b -> p (a b)"), SC1[:, :, :].rearrange("p a b -> p (a b)"),
                          FLG[0:64, 1 + q:2 + q], SIN[:, :, :].rearrange("p a b -> p (a b)"), ALU.mult, ALU.add,
                          r=[SC1, SIN, FLG], w=[SIN])
                pf_ = psf()
                for h in range(8):
                    S.mm(pf_[0:64, h * 64:(h + 1) * 64], EX[:, h, 64:128], SIN[:, h, :], r=[EX, SIN], w=[pf_])
                S.tt("dve", SFT[:, :, :], pf_[0:64, :].rearrange("p (a b) -> p a b", b=64), EX[:, :, 0:64], ALU.add,
                     r=[pf_, EX], w=[SFT])
                pf2 = psf()
                for h in range(8):
                    S.tr(pf2[0:64, h * 64:(h + 1) * 64], SFT[:, h, :], MASK[0:64, 4, 0:64], r=[SFT, MASK], w=[pf2])
                S.cp("dve", SC1[:, :, :], pf2[0:64, :].rearrange("p (a b) -> p a b", b=64), r=[pf2], w=[SC1])
                S.dma("sp", stp_o.rearrange("h v k -> v h k"), SC1[:, :, :], r=[SC1])
                S.cp("act", SINB[:, :, :], SIN[:, :, :], r=[SIN], w=[SINB])
            S.barrier()
            es1s = ExitStack()
            with es1s:
                SX = sb(es1s, "sx", [128, 6, 512])
                R6 = sb(es1s, "r6", [128, 6, 64])
                ST = sb(es1s, "st", [128, 64, 64])
                TP = sb(es1s, "tp", [128, 64, 64])
                SA = sb(es1s, "sa", [128, 64])
                YV = sb(es1s, "yv", [128, 64])
                YS = sb(es1s, "ysr", [128, 512])
                WD = sb(es1s, "wd", [128, 512])
                zc = Z[0]
                S.dma("sp", ST[:, :, :].rearrange("p a b -> p (a b)"), s_state, w=[ST])
                for _ in zproj(xs, zc, HT[0], XT[0]):
                    pass
                S.dma("sp", shifts_o, zc[0:16, :], r=[zc])
                S.memset("dve", ZP[:, :], 0.0, w=[ZP])
                S.dma("sp", ZP[0:16, :], s_shift, w=[ZP])
                for _ in rwkv_pre(zc, None):
                    pass
                S.dma("sp", sc_bv[NT], BV[:, :], r=[BV])
                S.dma("sp", sc_g[NT], GO[:, :], r=[GO])
                S.act(WD[:, :], TW[:, :], AF.Exp, bias=CBH[:, 0:1], scale=-C0H, r=[TW, CBH], w=[WD])
                for i, src in enumerate((ZS[:, 0:512], WD[:, :], KH[:, :], ZS[:, 1024:1536], KK[:, :], BB[:, :])):
                    S.cp("dve" if i % 2 == 0 else "pool", SX[:, i, :], src, r=[ZS, WD, KH, KK, BB], w=[SX])
                S.dma("sp", sc_s1.rearrange("i b n -> b i n"), SX[0:16, :, :], r=[SX], w=[R6.sub("d")])
                S.dma("sp", R6[:, :, :], sc_s1.rearrange("i b (h d) -> (b h) i d", d=64), r=[R6.sub("d")], w=[R6])
                r_, w_, k_, v_, kk_, b_ = (R6[:, i, :] for i in range(6))

                def bv(ap):
                    return ap.unsqueeze(1).to_broadcast([128, 64, 64])

                def bk(ap):
                    return ap.unsqueeze(2).to_broadcast([128, 64, 64])

                S.tt("dve", TP[:, :, :], ST[:, :, :], bv(kk_), ALU.mult, r=[ST, R6], w=[TP])
                S.red(SA[:, :], TP[:, :, :], r=[TP], w=[SA])
                S.tt("pool", ST[:, :, :], ST[:, :, :], bv(w_), ALU.mult, r=[ST, R6, TP], w=[ST])
                S.tt("dve", TP[:, :, :], bk(SA[:, :]), bv(b_), ALU.mult, r=[SA, R6], w=[TP])
                S.tt("dve", ST[:, :, :], ST[:, :, :], TP[:, :, :], ALU.subtract, r=[ST, TP], w=[ST])
                S.tt("pool", TP[:, :, :], bk(v_), bv(k_), ALU.mult, r=[R6, ST], w=[TP])
                S.tt("dve", ST[:, :, :], ST[:, :, :], TP[:, :, :], ALU.add, r=[ST, TP], w=[ST])
                S.dma("sp", sts_o, ST[:, :, :].rearrange("p a b -> p (a b)"), r=[ST])
                S.tt("dve", TP[:, :, :], ST[:, :, :], bv(r_), ALU.mult, r=[ST, R6], w=[TP])
                S.red(YV[:, :], TP[:, :, :], r=[TP], w=[YV])
                S.dma("sp", sc_s2.rearrange("b (h d) -> (b h) d", d=64), YV[:, :], r=[YV], w=[YS.sub("d")])
                S.memset("dve", YS[:, :], 0.0, w=[YS])
                S.dma("sp", YS[0:16, :], sc_s2, r=[YS.sub("d")], w=[YS])
                S.dma("sp", sc_yp[NT], YS[:, :], r=[YS])
        S.barrier()

        MKT = sb(es, "mkt", [128, 4, 256], BF16)
        MVB = sb(es, "mvb", [128, 2, 512], BF16)
        VB = sb(es, "vb", [128, 1416])
        S.dma("sp", VB[:], vecB.partition_broadcast(128).rearrange("p a n -> p (a n)"), w=[VB])
        LN_W, LN_B = VB[:, 0:512], VB[:, 512:1024]
        XQG, XKG = VB[:, 1152:1280], VB[:, 1280:1408]
        es0 = ExitStack()
        with es0:
            WM = sb(es0, "wm", [128, 8, 1024], BF16)
            STG0 = [sb(es0, "stg0_%d" % i, [128, 1024]) for i in range(2)]
            for kt in range(8):
                st = STG0[kt % 2]
                S.dma("sp", st[:, :], w_mkv[kt * 128:(kt + 1) * 128, :], w=[st])
                S.cp(("pool", "act", "dve")[kt % 3], WM[:, kt, :], st[:, :], r=[st], w=[WM])
            XM0 = sb(es0, "xm0", [128, D])
            SCR0 = sb(es0, "scr0", [128, D])
            HB0 = sb(es0, "hb0", [128, D], BF16)
            SS0 = sb(es0, "ss0", [128, 8])
            RS0 = sb(es0, "rs0", [128, 8])
            HT0 = sb(es0, "ht0", [128, 8, 128], BF16)
            MKF = sb(es0, "mkf", [128, 512])
            MVF = sb(es0, "mvf", [128, 512])
            MKB = sb(es0, "mkb", [128, 512], BF16)
            for mt in range(2):
                S.dma("sp", XM0[:, :], memp[mt * 128:(mt + 1) * 128, :], w=[XM0])
                norm_T(XM0, 2, HT0, SCR0, HB0, SS0, RS0)
                pk, pv_ = psf(), psf()
                for kt in range(8):
                    S.mm(pk[:, :], HT0[:, kt, :], WM[:, kt, 0:512], start=(kt == 0), stop=(kt == 7), r=[HT0, WM], w=[pk])
                for kt in range(8):
                    S.mm(pv_[:, :], HT0[:, kt, :], WM[:, kt, 512:1024], start=(kt == 0), stop=(kt == 7), r=[HT0, WM], w=[pv_])
                S.cp("act", MVF[:, :], pv_[:, :], r=[pv_], w=[MVF])
                S.dma("sp", memv_o[mt * 128:(mt + 1) * 128, :], MVF[:, :], r=[MVF])
                S.cp("pool", MVB[:, mt, :], MVF[:, :], r=[MVF], w=[MVB])
                rstd_of(pk[:, :], 4, 128, EPS, SCR0, SS0, RS0, [pk])
                S.tt("dve", MKF[:, :].rearrange("p (a b) -> p a b", b=128), pk[:, :].rearrange("p (a b) -> p a b", b=128),
                     RS0[:, 0:4].unsqueeze(2).to_broadcast([128, 4, 128]), ALU.mult, r=[pk, RS0], w=[MKF])
                S.tt("dve", MKF[:, :].rearrange("p (a b) -> p a b", b=128), MKF[:, :].rearrange("p (a b) -> p a b", b=128),
                     XKG.unsqueeze(1).to_broadcast([128, 4, 128]), ALU.mult, r=[MKF, VB], w=[MKF])
                S.dma("sp", memk_o[mt * 128:(mt + 1) * 128, :], MKF[:, :], r=[MKF])
                S.cp("act", MKB[:, :], MKF[:, :], r=[MKF], w=[MKB])
                pb = psb()
                for h in range(4):
                    S.tr(pb[:, h * 128:(h + 1) * 128], MKB[:, h * 128:(h + 1) * 128], IDB[:], r=[MKB, IDB], w=[pb])
                S.cp("act", MKT[:, :, mt * 128:(mt + 1) * 128], pb[:, 0:512].rearrange("p (a b) -> p a b", b=128), r=[pb], w=[MKT])
        S.barrier()

        es2 = ExitStack()
        with es2:
            WR = sb(es2, "wr", [128, 8, 4352], BF16)
            WAC = sb(es2, "wac", [128, 8, 1024], BF16)
            WBB = sb(es2, "wbb", [64, 8, 1024], BF16)
            WO = sb(es2, "wo", [128, 8, 1024], BF16)
            es2w = ExitStack()
            es2w.__enter__()
            STG2 = [sb(es2w, "stg2_%d" % i, [128, 1088]) for i in range(6)]
            sgi = [0]

            def ldc(dst_ap, src_ap, np_, ncol, wtok):
                st = STG2[sgi[0] % 6]
                ce = ("pool", "act", "dve")[sgi[0] % 3]
                sgi[0] += 1
                S.dma("sp", st[0:np_, 0:ncol], src_ap, w=[st])
                S.cp(ce, dst_ap, st[0:np_, 0:ncol], r=[st], w=[wtok])

            for kt in range(8):
                for hf in range(4):
                    ldc(WR[:, kt, hf * 1088:(hf + 1) * 1088], w_in[kt * 128:(kt + 1) * 128, 1792 + hf * 1088:1792 + (hf + 1) * 1088], 128, 1088, WR)
            for i in range(4):
                ldc(WAC[:, i, :], w_br[i * 128:(i + 1) * 128, :], 128, 1024, WAC)
                ldc(WAC[:, 4 + i, :], w_br[1024 + i * 128:1024 + (i + 1) * 128, :], 128, 1024, WAC)
            for h in range(8):
                ldc(WBB[:, h, :], w_br[512 + h * 64:512 + (h + 1) * 64, :], 64, 1024, WBB)
            for kt in range(8):
                ldc(WO[:, kt, :], w_out[kt * 128:(kt + 1) * 128, :], 128, 1024, WO)
            S.barrier()
            es2w.__exit__(None, None, None)

            XT2 = [sb(es2, "xt2", [128, D])] * 2
            HB2 = sb(es2, "hb2", [128, D], BF16)
            SSx = sb(es2, "ssx", [128, 8])
            RSx = sb(es2, "rsx", [128, 8])
            HT2 = sb(es2, "ht2", [128, 8, 128], BF16)
            QKV = sb(es2, "qkv", [128, 1280])
            GTH = sb(es2, "gth", [128, 3072], BF16)
            QKG = sb(es2, "qkg", [128, 10, 64])
            SS10 = sb(es2, "ss10", [128, 16])
            RS10 = sb(es2, "rs10", [128, 16])
            RT = sb(es2, "rt", [128, 4, 10, 8])
            QKB = sb(es2, "qkb", [128, 640], BF16)
            XQB = sb(es2, "xqb", [128, 512], BF16)
            XQT = sb(es2, "xqt", [128, 4, 128], BF16)
            ESK = sb(es2, "esk", [128, 8])
            OBT = sb(es2, "obt", [64, 8, 128], BF16)
            OCT = sb(es2, "oct", [128, 4, 128], BF16)
            OAT = sb(es2, "oat", [128, 4, 128], BF16)
            YPL = sb(es2, "ypl", [128, 512])
            BVL = sb(es2, "bvl", [128, 512])
            GOL = sb(es2, "gol", [128, 512])
            OAB = sb(es2, "oab", [128, 512], BF16)
            ST8 = [sb(es2, "st8_%d" % i, [128, 8]) for i in range(4)]
            MG = sb(es2, "mg", [128, 1024])
            SCR2 = MG
            MT_ = sb(es2, "mt_", [128, 512])
            MGB = sb(es2, "mgb", [128, 1024], BF16)
            MGT = sb(es2, "mgt", [128, 8, 128], BF16)
            XMo = sb(es2, "xmo", [128, D])
            GNE = sb(es2, "gne", [128, 1])
            S.cp("dve", QKG[:, 0:8, :], VB[:, 1024:1088].unsqueeze(1).to_broadcast([128, 8, 64]), r=[VB], w=[QKG])
            S.cp("dve", QKG[:, 8:10, :], VB[:, 1088:1152].unsqueeze(1).to_broadcast([128, 2, 64]), r=[VB], w=[QKG])
            S.act(ESK[:, :], VB[:, 1408:1416], AF.Exp, r=[VB], w=[ESK])

            def inproj_rest(x_src, xt, ropej):
                S.dma("sp", xt[:, :], x_src, w=[xt])
                norm_T(xt, 0, HT2, SCR2, HB2, SSx, RSx)
                for (lo, hi, dlo) in ((0, 512, 0), (512, 768, 512), (768, 1280, 768)):
                    p = psf()
                    for kt in range(8):
                        S.mm(p[:, 0:hi - lo], HT2[:, kt, :], WR[:, kt, lo:hi], start=(kt == 0), stop=(kt == 7), r=[HT2, WR], w=[p])
                    S.cp("act", QKV[:, dlo:dlo + hi - lo], p[:, 0:hi - lo], r=[p], w=[QKV])
                for gc in range(6):
                    p = psf()
                    for kt in range(8):
                        S.mm(p[:, :], HT2[:, kt, :], WR[:, kt, 1280 + gc * 512:1280 + (gc + 1) * 512], start=(kt == 0), stop=(kt == 7),
                             r=[HT2, WR], w=[p])
                    S.act(GTH[:, gc * 512:(gc + 1) * 512], p[:, :], AF.Tanh, scale=0.5, r=[p], w=[GTH])
                qk3 = QKV[:, 0:640].rearrange("p (a b) -> p a b", b=64)
                rstd_of(QKV[:, 0:640], 10, 64, EPS, SCR2, SS10, RS10, [QKV])
                S.tt("dve", qk3, qk3, RS10[:, 0:10].unsqueeze(2).to_broadcast([128, 10, 64]), ALU.mult, r=[QKV, RS10], w=[QKV])
                S.tt("dve", qk3, qk3, QKG[:, :, :], ALU.mult, r=[QKV, QKG], w=[QKV])
                cosb = ROPE[:, ropej, 0:8].unsqueeze(1).to_broadcast([128, 10, 8])
                sinb = ROPE[:, ropej, 8:16].unsqueeze(1).to_broadcast([128, 10, 8])
                x1, x2 = qk3[:, :, 0:8], qk3[:, :, 8:16]
                S.tt("dve", RT[:, 0, :, :], x1, cosb, ALU.mult, r=[QKV, ROPE], w=[RT])
                S.tt("dve", RT[:, 1, :, :], x2, sinb, ALU.mult, r=[QKV, ROPE], w=[RT])
                S.tt("dve", RT[:, 2, :, :], x2, cosb, ALU.mult, r=[QKV, ROPE], w=[RT])
                S.tt("dve", RT[:, 3, :, :], x1, sinb, ALU.mult, r=[QKV, ROPE], w=[RT])
                S.tt("dve", x1, RT[:, 0, :, :], RT[:, 1, :, :], ALU.subtract, r=[RT], w=[QKV])
                S.tt("dve", x2, RT[:, 2, :, :], RT[:, 3, :, :], ALU.add, r=[RT], w=[QKV])
                xq3 = QKV[:, 768:1280].rearrange("p (a b) -> p a b", b=128)
                rstd_of(QKV[:, 768:1280], 4, 128, EPS, SCR2, SS10, RS10, [QKV])
                S.tt("dve", xq3, xq3, RS10[:, 0:4].unsqueeze(2).to_broadcast([128, 4, 128]), ALU.mult, r=[QKV, RS10], w=[QKV])
                S.tt("dve", XQB[:, :].rearrange("p (a b) -> p a b", b=128), xq3, XQG.unsqueeze(1).to_broadcast([128, 4, 128]),
                     ALU.mult, r=[QKV, VB], w=[XQB])

            def rwkv_out(jj, with_state):
                S.dma("sp", YPL[:, :], sc_yp[jj], w=[YPL])
                S.dma("sp", BVL[:, :], sc_bv[jj], w=[BVL])
                S.dma("sp", GOL[:, :], sc_g[jj], w=[GOL])
                if with_state:
                    S.dma("sp", GTL[:, :, :].rearrange("p a b -> p (a b)"), sc_gt[jj], w=[GTL])
                    p = psf()
                    for h in range(8):
                        S.mm(p[:, h * 64:(h + 1) * 64], GTL[:, h, :], SINB[:, h, :], r=[GTL, SINB], w=[p])
                    S.tt("dve", XMo[:, 0:512], p[:, :], YPL[:, :], ALU.add, r=[p, YPL], w=[XMo])
                    ysrc = XMo
                else:
                    ysrc = YPL
                y3 = ysrc[:, 0:512].rearrange("p (a b) -> p a b", b=64)
                sm, sq, mn, vr = ST8
                S.red(sm[:, :], y3, r=[ysrc], w=[sm])
                S.act(XMo[:, 512:1024], ysrc[:, 0:512], AF.Square, r=[ysrc], w=[XMo])
                S.red(sq[:, :], XMo[:, 512:1024].rearrange("p (a b) -> p a b", b=64), r=[XMo], w=[sq])
                S.ts("dve", mn[:, :], sm[:, :], 1.0 / 64, ALU.mult, r=[sm], w=[mn])
                S.tt("dve", vr[:, :], mn[:, :], mn[:, :], ALU.mult, r=[mn], w=[vr])
                S.stt(vr[:, :], sq[:, :], 1.0 / 64, vr[:, :], ALU.mult, ALU.subtract, r=[sq, vr], w=[vr])
                S.ts("dve", vr[:, :], vr[:, :], GN_EPS, ALU.add, r=[vr], w=[vr])
                S.tt("pool", sq[:, :], vr[:, :], NEGH[:, 0:8], ALU.pow, r=[vr, NEGH], w=[sq])
                yq3 = XMo[:, 512:1024].rearrange("p (a b) -> p a b", b=64)
                S.tt("dve", yq3, y3, mn[:, :].unsqueeze(2).to_broadcast([128, 8, 64]), ALU.subtract, r=[ysrc, mn], w=[XMo])
                S.tt("dve", yq3, yq3, sq[:, :].unsqueeze(2).to_broadcast([128, 8, 64]), ALU.mult, r=[XMo, sq], w=[XMo])
                S.tt("pool", XMo[:, 512:1024], XMo[:, 512:1024], LN_W, ALU.mult, r=[XMo, VB], w=[XMo])
                S.tt("pool", XMo[:, 512:1024], XMo[:, 512:1024], LN_B, ALU.add, r=[XMo, VB], w=[XMo])
                S.tt("dve", XMo[:, 512:1024], XMo[:, 512:1024], BVL[:, :], ALU.add, r=[XMo, BVL], w=[XMo])
                S.tt("dve", OAB[:, :], XMo[:, 512:1024], GOL[:, :], ALU.mult, r=[XMo, GOL], w=[OAB])
                pb = psb()
                for i in range(4):
                    S.tr(pb[:, i * 128:(i + 1) * 128], OAB[:, i * 128:(i + 1) * 128], IDB[:], r=[OAB, IDB], w=[pb])
                S.cp("act", OAT[:, :, :].rearrange("p a b -> p (a b)"), pb[:, 0:512], r=[pb], w=[OAT])

            def xq_transposes():
                pb = psb()
                for h in range(4):
                    S.tr(pb[:, h * 128:(h + 1) * 128], XQB[:, h * 128:(h + 1) * 128], IDB[:], r=[XQB, IDB], w=[pb])
                S.cp("act", XQT[:, :, :].rearrange("p a b -> p (a b)"), pb[:, 0:512], r=[pb], w=[XQT])

            def merge_out(xt, xm_dst_ap, extra_r=()):
                for cc in range(2):
                    cs = slice(cc * 512, (cc + 1) * 512)
                    pa_, pb_, pc_ = psf(), psf(), psf()
                    for k in range(4):
                        S.mm(pa_[:, :], OAT[:, k, :], WAC[:, k, cs], start=(k == 0), stop=(k == 3), r=[OAT, WAC], w=[pa_])
                    for h in range(8):
                        S.mm(pb_[:, :], OBT[:, h, :], WBB[:, h, cs], start=(h == 0), stop=(h == 7), r=[OBT, WBB], w=[pb_])
                    for k in range(4):
                        S.mm(pc_[:, :], OCT[:, k, :], WAC[:, 4 + k, cs], start=(k == 0), stop=(k == 3), r=[OCT, WAC], w=[pc_])
                    S.stt(MG[:, cs], GTH[:, cc * 512:(cc + 1) * 512], 1.0, pa_[:, :], ALU.add, ALU.mult, r=[GTH, pa_], w=[MG.sub(cc)])
                    S.stt(MT_[:, :], GTH[:, 1024 + cc * 512:1024 + (cc + 1) * 512], 1.0, pb_[:, :], ALU.add, ALU.mult, r=[GTH, pb_], w=[MT_])
                    S.tt("pool", MG[:, cs], MG[:, cs], MT_[:, :], ALU.add, r=[MG.sub(cc), MT_], w=[MG.sub(cc)])
                    S.stt(MT_[:, :], GTH[:, 2048 + cc * 512:2048 + (cc + 1) * 512], 1.0, pc_[:, :], ALU.add, ALU.mult, r=[GTH, pc_], w=[MT_])
                    S.tt("pool", MGB[:, cs], MG[:, cs], MT_[:, :], ALU.add, r=[MG.sub(cc), MT_], w=[MGB])
                pb = psb()
                for kt in range(8):
                    S.tr(pb[:, kt * 128:(kt + 1) * 128], MGB[:, kt * 128:(kt + 1) * 128], IDB[:], r=[MGB, IDB], w=[pb])
                S.cp("act", MGT[:, :, :].rearrange("p a b -> p (a b)"), pb[:, :], r=[pb], w=[MGT])
                for cc in range(2):
                    cs = slice(cc * 512, (cc + 1) * 512)
                    p = psf()
                    for kt in range(8):
                        S.mm(p[:, :], MGT[:, kt, :], WO[:, kt, cs], start=(kt == 0), stop=(kt == 7), r=[MGT, WO], w=[p])
                    S.stt(XMo[:, cs], p[:, :], 0.5, xt[:, cs], ALU.mult, ALU.add, r=[p, xt], w=[XMo])
                S.dma("sp", xm_dst_ap, XMo[:, :], r=[XMo])

            GTL = sb(es2, "gtl", [64, 8, 128], BF16)

            es2p = ExitStack()
            with es2p:
                QT2 = sb(es2p, "qt2", [128, 4, 128], BF16)
                KT = [sb(es2p, "kt%d" % i, [128, 128], BF16) for i in range(2)]
                VV = [sb(es2p, "vv%d" % i, [128, 128], BF16) for i in range(2)]
                PE_ = sb(es2p, "pe_", [128, 512])
                PT_ = [[sb(es2p, "pt_%d_%d" % (g, b), [128, 512], BF16) for b in range(2)] for g in range(2)]
                PM = [sb(es2p, "pm%d" % i, [128, 512], BF16) for i in range(2)]
                RD = sb(es2p, "rd", [128, 512])
                MASKH = sb(es2p, "maskh", [128, 128])
                SNK = sb(es2p, "snk", [64, 8, 128])
                S.cp("dve", SNK[:, :, :], ESK[0:64, :].unsqueeze(2).to_broadcast([64, 8, 128]), r=[ESK], w=[SNK])
                S.ts("dve", MASKH[:, :], LOI, FLG[:, 0:1], ALU.mult, r=[MASK, FLG], w=[MASKH])

                def swa_kv_prep(cur):
                    S.cp("dve", QKB[:, 0:512].rearrange("p (h g d) -> p h g d", h=4, g=2),
                         QKV[:, 0:512].rearrange("p (g h d) -> p h g d", g=2, h=4), r=[QKV], w=[QKB])
                    S.cp("act", QKB[:, 512:640], QKV[:, 512:640], r=[QKV], w=[QKB])
                    S.cp("act", VV[cur][:, :], QKV[:, 640:768], r=[QKV], w=[VV[cur]])
                    pb = psb()
                    for i in range(5):
                        S.tr(pb[:, i * 128:(i + 1) * 128], QKB[:, i * 128:(i + 1) * 128], IDB[:], r=[QKB, IDB], w=[pb])
                    S.cp("act", QT2[:, :, :].rearrange("p a b -> p (a b)"), pb[:, 0:512], r=[pb], w=[QT2])
                    S.cp("act", KT[cur][:, :], pb[:, 512:640], r=[pb], w=[KT[cur]])

                for j in range(NP):
                    cur, prv = j % 2, (j - 1) % 2
                    xt = XT2[j % 2]
                    if j == 0:
                        S.dma("sp", xt[:, :], xp[0:128, :], w=[xt])
                        norm_T(xt, 0, HT2, SCR2, HB2, SSx, RSx)
                        p = psf()
                        for kt in range(8):
                            S.mm(p[:, 0:256], HT2[:, kt, :], WR[:, kt, 512:768], start=(kt == 0), stop=(kt == 7), r=[HT2, WR], w=[p])
                        S.cp("act", QKV[:, 512:768], p[:, 0:256], r=[p], w=[QKV])
                        S.memset("dve", QKV[:, 0:512], 0.0, w=[QKV])
                        S.memset("dve", QKV[:, 768:1280], 0.0, w=[QKV])
                        qk3 = QKV[:, 0:640].rearrange("p (a b) -> p a b", b=64)
                        rstd_of(QKV[:, 0:640], 10, 64, EPS, SCR2, SS10, RS10, [QKV])
                        S.tt("dve", qk3, qk3, RS10[:, 0:10].unsqueeze(2).to_broadcast([128, 10, 64]), ALU.mult, r=[QKV, RS10], w=[QKV])
                        S.tt("dve", qk3, qk3, QKG[:, :, :], ALU.mult, r=[QKV, QKG], w=[QKV])
                        cosb = ROPE[:, 0, 0:8].unsqueeze(1).to_broadcast([128, 10, 8])
                        sinb = ROPE[:, 0, 8:16].unsqueeze(1).to_broadcast([128, 10, 8])
                        x1, x2 = qk3[:, :, 0:8], qk3[:, :, 8:16]
                        S.tt("dve", RT[:, 0, :, :], x1, cosb, ALU.mult, r=[QKV, ROPE], w=[RT])
                        S.tt("dve", RT[:, 1, :, :], x2, sinb, ALU.mult, r=[QKV, ROPE], w=[RT])
                        S.tt("dve", RT[:, 2, :, :], x2, cosb, ALU.mult, r=[QKV, ROPE], w=[RT])
                        S.tt("dve", RT[:, 3, :, :], x1, sinb, ALU.mult, r=[QKV, ROPE], w=[RT])
                        S.tt("dve", x1, RT[:, 0, :, :], RT[:, 1, :, :], ALU.subtract, r=[RT], w=[QKV])
                        S.tt("dve", x2, RT[:, 2, :, :], RT[:, 3, :, :], ALU.add, r=[RT], w=[QKV])
                        swa_kv_prep(cur)
                        continue
                    jj = j - 1
                    inproj_rest(xp[j * 128:(j + 1) * 128, :], xt, j)
                    if j == NT:
                        S.dma("sp", swakp_o, QKV[:, 512:640], r=[QKV])
                        S.dma("sp", swavp_o, QKV[:, 640:768], r=[QKV])
                    swa_kv_prep(cur)
                    xq_transposes()
                    for g in range(2):
                        gs_ = slice(g * 64, (g + 1) * 64)
                        for bi, kb in enumerate((prv, cur)):
                            p = psf()
                            S.mm(p[:, :], KT[kb][gs_, :], QT2[gs_, :, :].rearrange("p a b -> p (a b)"), r=[KT[kb], QT2], w=[p])
                            S.act(PE_[:, :], p[:, :], AF.Exp, scale=0.125, r=[p], w=[PE_])
                            if bi == 0:
                                m_ap = (MASKH[:, :] if j == 1 else LOI)
                            else:
                                m_ap = UPI
                            S.tt("dve", PT_[g][bi][:, :].rearrange("p (a b) -> p a b", b=128), PE_[:, :].rearrange("p (a b) -> p a b", b=128),
                                 m_ap.unsqueeze(1).to_broadcast([128, 4, 128]), ALU.mult, r=[PE_, MASK, MASKH], w=[PT_[g][bi]])
                        po, psm = psf(), psf()
                        for bi, kb in enumerate((prv, cur)):
                            S.mm(po[0:64, :], VV[kb][:, gs_], PT_[g][bi][:, :], start=(bi == 0), stop=(bi == 1), r=[VV[kb], PT_[g][bi]], w=[po])
                        for bi in range(2):
                            S.mm(psm[0:64, :], ONESB[:, 0:64], PT_[g][bi][:, :], start=(bi == 0), stop=(bi == 1), r=[ONESB, PT_[g][bi]], w=[psm])
                        S.tt("dve", RD[0:64, :].rearrange("p (a b) -> p a b", b=128), psm[0:64, :].rearrange("p (a b) -> p a b", b=128),
                             SNK[:, g * 4:(g + 1) * 4, :], ALU.add, r=[psm, SNK], w=[RD])
                        S.op("dve", lambda e: e.reciprocal(RD[0:64, :], RD[0:64, :]), [RD], [RD])
                        S.tt("dve", OBT[:, g * 4:(g + 1) * 4, :].rearrange("p a b -> p (a b)"), po[0:64, :], RD[0:64, :], ALU.mult,
                             r=[po, RD], w=[OBT])
                    for mt in range(2):
                        p = psf()
                        for h in range(4):
                            S.mm(p[:, h * 128:(h + 1) * 128], MKT[:, h, mt * 128:(mt + 1) * 128], XQT[:, h, :], r=[MKT, XQT], w=[p])
                        S.act(PM[mt][:, :], p[:, :], AF.Exp, scale=float(128 ** -0.5), r=[p], w=[PM[mt]])
                    po, psm = psf(), psf()
                    for h in range(4):
                        for mt in range(2):
                            S.mm(po[:, h * 128:(h + 1) * 128], MVB[:, mt, h * 128:(h + 1) * 128], PM[mt][:, h * 128:(h + 1) * 128],
                                 start=(mt == 0), stop=(mt == 1), r=[MVB, PM[mt]], w=[po])
                    for mt in range(2):
                        S.mm(psm[:, :], ONESB[:, :], PM[mt][:, :], start=(mt == 0), stop=(mt == 1), r=[ONESB, PM[mt]], w=[psm])
                    S.op("dve", lambda e, p_=psm: e.reciprocal(RD[:, :], p_[:, :]), [psm], [RD])
                    S.tt("dve", OCT[:, :, :].rearrange("p a b -> p (a b)"), po[:, :], RD[:, :], ALU.mult, r=[po, RD], w=[OCT])
                    rwkv_out(jj, True)
                    merge_out(xt, sc_xm[jj])
            S.barrier()
            es2s = ExitStack()
            with es2s:
                SELT = sb(es2s, "selt", [16, 128])
                SEL2 = sb(es2s, "sel2", [128, 16])
                S.dma("sp", SELT[:, :], selt, w=[SELT])
                S.dma("sp", SEL2[:, :], sel2, w=[SEL2])
                KC = sb(es2s, "kc", [128, 16, 128])
                VC = sb(es2s, "vc", [128, 16, 128])
                QR = MG
                SC_ = sb(es2s, "sc_", [128, 8, 16])
                OP_ = sb(es2s, "op_", [128, 520])
                PN = sb(es2s, "pn", [128, 8])
                SN = sb(es2s, "sn", [128, 8])
                DEN = sb(es2s, "den", [128, 8])
                TPn = sb(es2s, "tpn", [128, 8, 64])
                OBS = YPL
                OBSb = OAB
                SCm = sb(es2s, "scm", [128, 4, 32])
                OPs = sb(es2s, "ops", [128, 4])
                OPr = MT_
                OPm = GOL
                TPk3 = XMo[:, :].rearrange("p (a b) -> p a b", b=64)
                TPk = XMo
                TPm3 = BVL[:, :].rearrange("p (a b) -> p a b", b=128)
                TPm = BVL
                KM3 = KC[:, :, :].rearrange("p a b -> p (a b)").rearrange("p (a b) -> p a b", b=512)
                VM3 = VC[:, :, :].rearrange("p a b -> p (a b)").rearrange("p (a b) -> p a b", b=512)
                xt = XT2[0]
                S.dma("sp", KC[:, :, :].rearrange("p a b -> p (a b)"), s_swak.rearrange("b (g i) c -> (b g) (i c)", i=16), w=[KC])
                S.dma("sp", VC[:, :, :].rearrange("p a b -> p (a b)"), s_swav.rearrange("b (g i) c -> (b g) (i c)", i=16), w=[VC])
                S.dma("sp", swaks_o[:, 0:127, :], s_swak[:, 1:128, :])
                S.dma("sp", swavs_o[:, 0:127, :], s_swav[:, 1:128, :])
                inproj_rest(xs, xt, NT + 1)
                S.dma("sp", swaks_o[:, 127, :], QKV[0:16, 512:640], r=[QKV])
                S.dma("sp", swavs_o[:, 127, :], QKV[0:16, 640:768], r=[QKV])
                xq_transposes()
                S.cp("dve", SCR2[0:16, 0:512], QKV[0:16, 0:512], r=[QKV], w=[SCR2])
                S.cp("dve", SCR2[0:16, 512:1024], XQB[0:16, :], r=[XQB], w=[SCR2])
                for hf in range(2):
                    p = psf()
                    S.mm(p[:, :], SELT[:, :], SCR2[0:16, hf * 512:(hf + 1) * 512], r=[SELT, SCR2], w=[p])
                    S.cp("act", QR[:, hf * 512:(hf + 1) * 512], p[:, :], r=[p], w=[QR])
                for h in range(8):
                    kv = h // 4
                    S.tt("dve", TPk3, KC[:, :, kv * 64:(kv + 1) * 64],
                         QR[:, h * 64:(h + 1) * 64].unsqueeze(1).to_broadcast([128, 16, 64]), ALU.mult, r=[KC, QR], w=[TPk])
                    S.red(SC_[:, h, :], TPk3, r=[TPk], w=[SC_])
                S.act(SC_[:, :, :], SC_[:, :, :], AF.Exp, scale=0.125, r=[SC_], w=[SC_])
                S.red(OP_[:, 512:520], SC_[:, :, :], r=[SC_], w=[OP_.sub("s")])
                for h in range(8):
                    kv = h // 4
                    S.tt("dve", TPk3, VC[:, :, kv * 64:(kv + 1) * 64],
                         SC_[:, h, :].unsqueeze(2).to_broadcast([128, 16, 64]), ALU.mult, r=[VC, SC_], w=[TPk])
                    S.red(OP_[:, h * 64:(h + 1) * 64], TPk3.rearrange("p i d -> p d i"), r=[TPk], w=[OP_.sub(h)])
                po, psm = psf(), psf()
                S.mm(po[0:16, :], SEL2[:, :], OP_[:, 0:512], r=[SEL2] + [OP_.sub(h) for h in range(8)], w=[po])
                S.mm(psm[0:16, 0:8], SEL2[:, :], OP_[:, 512:520], r=[SEL2, OP_.sub("s")], w=[psm])
                q3 = QKV[:, 0:512].rearrange("p (a b) -> p a b", b=64)
                for g in range(2):
                    S.tt("dve", TPn[:, g * 4:(g + 1) * 4, :], q3[:, g * 4:(g + 1) * 4, :],
                         QKV[:, 512 + g * 64:512 + (g + 1) * 64].unsqueeze(1).to_broadcast([128, 4, 64]), ALU.mult, r=[QKV], w=[TPn])
                S.red(SN[:, :], TPn[:, :, :], r=[TPn], w=[SN])
                S.act(PN[:, :], SN[:, :], AF.Exp, scale=0.125, r=[SN], w=[PN])
                S.tt("dve", DEN[:, :], PN[:, :], ESK[:, :], ALU.add, r=[PN, ESK], w=[DEN])
                S.tt("dve", DEN[0:16, :], DEN[0:16, :], psm[0:16, 0:8], ALU.add, r=[DEN, psm], w=[DEN])
                S.op("dve", lambda e: e.reciprocal(DEN[:, :], DEN[:, :]), [DEN], [DEN])
                for g in range(2):
                    S.tt("dve", TPn[:, g * 4:(g + 1) * 4, :], PN[:, g * 4:(g + 1) * 4].unsqueeze(2).to_broadcast([128, 4, 64]),
                         QKV[:, 640 + g * 64:640 + (g + 1) * 64].unsqueeze(1).to_broadcast([128, 4, 64]), ALU.mult, r=[PN, QKV], w=[TPn])
                S.cp("dve", OBS[:, :], TPn[:, :, :].rearrange("p a b -> p (a b)"), r=[TPn], w=[OBS])
                S.tt("dve", OBS[0:16, :], OBS[0:16, :], po[0:16, :], ALU.add, r=[OBS, po], w=[OBS])
                S.tt("dve", OBSb[:, :].rearrange("p (a b) -> p a b", b=64), OBS[:, :].rearrange("p (a b) -> p a b", b=64),
                     DEN[:, :].unsqueeze(2).to_broadcast([128, 8, 64]), ALU.mult, r=[OBS, DEN], w=[OBSb])
                pb = psb()
                for h in range(8):
                    S.tr(pb[0:64, h * 128:(h + 1) * 128], OBSb[:, h * 64:(h + 1) * 64], IDB[:], r=[OBSb, IDB], w=[pb])
                S.cp("act", OBT[:, :, :].rearrange("p a b -> p (a b)"), pb[0:64, :], r=[pb], w=[OBT])
                mk4 = s_memk.rearrange("b (g r i) c -> (b g) r (i c)", g=8, r=8, i=4)
                mv4 = s_memv.rearrange("b (g r i) c -> (b g) r (i c)", g=8, r=8, i=4)
                for r_ in range(8):
                    S.dma("sp", KC[:, :, :].rearrange("p a b -> p (a b)"), mk4[:, r_, :], w=[KC])
                    for h in range(4):
                        S.tt("dve", TPm3, KM3[:, :, h * 128:(h + 1) * 128],
                             QR[:, 512 + h * 128:512 + (h + 1) * 128].unsqueeze(1).to_broadcast([128, 4, 128]), ALU.mult, r=[KC, QR], w=[TPm])
                        S.red(SCm[:, h, r_ * 4:(r_ + 1) * 4], TPm3, r=[TPm], w=[SCm])
                S.act(SCm[:, :, :], SCm[:, :, :], AF.Exp, scale=float(128 ** -0.5), r=[SCm], w=[SCm])
                S.red(OPs[:, :], SCm[:, :, :], r=[SCm], w=[OPs])
                for r_ in range(8):
                    S.dma("sp", VC[:, :, :].rearrange("p a b -> p (a b)"), mv4[:, r_, :], w=[VC])
                    for h in range(4):
                        S.tt("dve", TPm3, VM3[:, :, h * 128:(h + 1) * 128],
                             SCm[:, h, r_ * 4:(r_ + 1) * 4].unsqueeze(2).to_broadcast([128, 4, 128]), ALU.mult, r=[VC, SCm], w=[TPm])
                        dst = OPm if r_ == 0 else OPr
                        S.red(dst[:, h * 128:(h + 1) * 128], TPm3.rearrange("p i d -> p d i"), r=[TPm], w=[dst])
                    if r_ > 0:
                        S.tt("dve", OPm[:, 0:512], OPm[:, 0:512], OPr[:, :], ALU.add, r=[OPm, OPr], w=[OPm])
                po, psm = psf(), psf()
                S.mm(po[0:16, :], SEL2[:, :], OPm[:, 0:512], r=[SEL2, OPm], w=[po])
                S.mm(psm[0:16, 0:4], SEL2[:, :], OPs[:, :], r=[SEL2, OPs], w=[psm])
                S.memset("dve", DEN[:, :], 1.0, w=[DEN])
                S.cp("dve", DEN[0:16, 0:4], psm[0:16, 0:4], r=[psm], w=[DEN])
                S.op("dve", lambda e: e.reciprocal(DEN[:, 0:4], DEN[:, 0:4]), [DEN], [DEN])
                S.memset("dve", OBS[:, :], 0.0, w=[OBS])
                S.cp("dve", OBS[0:16, :], po[0:16, :], r=[po], w=[OBS])
                S.tt("dve", OBSb[:, :].rearrange("p (a b) -> p a b", b=128), OBS[:, :].rearrange("p (a b) -> p a b", b=128),
                     DEN[:, 0:4].unsqueeze(2).to_broadcast([128, 4, 128]), ALU.mult, r=[OBS, DEN], w=[OBSb])
                pb = psb()
                for h in range(4):
                    S.tr(pb[:, h * 128:(h + 1) * 128], OBSb[:, h * 128:(h + 1) * 128], IDB[:], r=[OBSb, IDB], w=[pb])
                S.cp("act", OCT[:, :, :].rearrange("p a b -> p (a b)"), pb[:, 0:512], r=[pb], w=[OCT])
                rwkv_out(NT, False)
                merge_out(xt, sc_xm[NT])
        S.barrier()

        es3 = ExitStack()
        with es3:
            WUP = sb(es3, "wup", [128, 8, 4096], BF16)
            WDN = sb(es3, "wdn", [128, 32, 1024], BF16)
            es3w = ExitStack()
            es3w.__enter__()
            STG3 = [sb(es3w, "stg3_%d" % i, [128, 2048]) for i in range(4)]
            s3i = [0]

            def ld3(dst_ap, src_ap, ncol, wtok):
                st = STG3[s3i[0] % 4]
                ce = ("pool", "act", "dve")[s3i[0] % 3]
                s3i[0] += 1
                S.dma("sp", st[:, 0:ncol], src_ap, w=[st])
                S.cp(ce, dst_ap, st[:, 0:ncol], r=[st], w=[wtok])

            for kt in range(8):
                for hf in range(2):
                    ld3(WUP[:, kt, hf * 2048:(hf + 1) * 2048], w_up[kt * 128:(kt + 1) * 128, hf * 2048:(hf + 1) * 2048], 2048, WUP)
            for fc in range(32):
                ld3(WDN[:, fc, :], w_down[fc * 128:(fc + 1) * 128, :], 1024, WDN)
            S.barrier()
            es3w.__exit__(None, None, None)
            XM3 = [sb(es3, "xm3_%d" % i, [128, D]) for i in range(2)]
            SCR3 = sb(es3, "scr3", [128, D])
            HB3 = sb(es3, "hb3", [128, D], BF16)
            SS3 = sb(es3, "ss3", [128, 8])
            RS3 = sb(es3, "rs3", [128, 8])
            H2T = sb(es3, "h2t", [128, 8, 256], BF16)
            S.memset("dve", H2T[:, :, :], 0.0, w=[H2T])
            RL = sb(es3, "rl", [128, 512])
            HID = sb(es3, "hid", [128, 32, 256], BF16)
            YO = [sb(es3, "yo%d" % i, [128, D]) for i in range(2)]
            for t0 in range(0, NT + 1, 2):
                tiles = [t for t in (t0, t0 + 1) if t <= NT]
                for ti, t in enumerate(tiles):
                    xm = XM3[t % 2]
                    S.dma("sp", xm[:, :], sc_xm[t], w=[xm])
                    norm_T(xm, 1, H2T, SCR3, HB3, SS3, RS3, dst=H2T[:, :, ti * 128:(ti + 1) * 128])
                for f2 in range(16):
                    p = psf()
                    for fi in range(2):
                        fc = f2 * 2 + fi
                        for kt in range(8):
                            S.mm(p[:, fi * 256:(fi + 1) * 256], WUP[:, kt, fc * 128:(fc + 1) * 128], H2T[:, kt, :],
                                 start=(kt == 0), stop=(kt == 7), r=[WUP, H2T], w=[p])
                    S.act(RL[:, :], p[:, :], AF.Relu, r=[p], w=[RL])
                    S.tt("dve" if f2 % 2 == 0 else "pool", HID[:, f2 * 2:(f2 + 1) * 2, :].rearrange("p a b -> p (a b)"), RL[:, :], RL[:, :],
                         ALU.mult, r=[RL], w=[HID.sub(f2)])
                for ti, t in enumerate(tiles):
                    xm = XM3[t % 2]
                    yo = YO[t % 2]
                    for cc in range(2):
                        cs = slice(cc * 512, (cc + 1) * 512)
                        p = psf()
                        for fc in range(32):
                            S.mm(p[:, :], HID[:, fc, ti * 128:(ti + 1) * 128], WDN[:, fc, cs], start=(fc == 0), stop=(fc == 31),
                                 r=[HID.sub(fc // 2), WDN], w=[p])
                        S.tt("dve", yo[:, cs], p[:, :], xm[:, cs], ALU.add, r=[p, xm], w=[yo])
                    if t < NT:
                        S.dma("sp", y_o[t * 128:(t + 1) * 128, :], yo[:, :], r=[yo])
                    else:
                        S.dma("sp", ys_o, yo[0:16, :], r=[yo])
        S.barrier()
        print("total ops", S.gseq, {e: len(S.streams[e]) for e in S.ENG}, flush=True)
        import os
        if os.environ.get("KLOG"):
            for g, e, ln in S.oplog[:int(os.environ["KLOG"])]:
                print(g, e, "line", ln)
        S.emit()
    return nc


def build_rest(nc, S, es, L):
    pass


def _host_inputs(NT, c, I):
    f32 = np.float32
    SEQ = I["x_prompt"].shape[1]
    seq, pos = c // 4, c % 4
    t0 = pos * NT * 128
    xp = np.zeros(((NT + 1) * 128, D), f32)
    if pos > 0:
        xp[:] = I["x_prompt"][seq, t0 - 128:t0 + NT * 128]
    else:
        xp[128:] = I["x_prompt"][seq, 0:NT * 128]
    xs = np.zeros((128, D), f32)
    xs[:16] = I["x_sample"][16 * c:16 * c + 16, 0]
    p = np.arange(128)
    ups = (p[:, None] < p[None, :]).astype(f32)
    upi = (p[:, None] <= p[None, :]).astype(f32)
    cmask = np.stack([ups, upi, ups.T.copy(), upi.T.copy(), np.eye(128, dtype=f32)], axis=1)
    ebias = np.stack([-C0H * (p + 1), -C0H * p, C0H * (p + 1), -C0H * (127 - p)], axis=1).astype(f32)
    half = 8
    inv_freq = np.power(np.float32(500000.0), -np.arange(half, dtype=f32) * np.float32(2.0 / 16)).astype(f32)
    rope = np.zeros((128, NT + 2, 16), f32)
    for j in range(NT + 2):
        if j <= NT:
            posj = (t0 - 128 + j * 128 + p).astype(f32)
        else:
            posj = np.full(128, 8192, f32)
        ang = posj[:, None] * inv_freq[None, :]
        rope[:, j, 0:8] = np.cos(ang)
        rope[:, j, 8:16] = np.sin(ang)
    flags = np.zeros((128, 4), f32)
    flags[:, 0] = 1.0 if pos > 0 else 0.0
    for q in range(3):
        flags[:, 1 + q] = 1.0 if q < pos else 0.0
    gains = np.stack([I["norm_mix"][0].reshape(8, 128).T, I["norm_ffn"][0].reshape(8, 128).T,
                      I["mem_norm"][0].reshape(8, 128).T], axis=1).astype(f32)
    vecA = np.concatenate([I["rw_mu"][0], I["rw_w0"][0], I["rw_a0"][0], I["rw_k_k"][0], I["rw_k_a"][0],
                           I["rw_r_k"][0].reshape(-1)])[None, :].astype(f32)
    vecB = np.concatenate([I["rw_ln_w"][0], I["rw_ln_b"][0], I["q_norm"][0], I["k_norm"][0], I["xq_norm"][0],
                           I["xk_norm"][0], I["swa_sinks"][0]])[None, :].astype(f32)
    b0 = 16 * c
    m = {
        "xp": xp, "xs": xs, "cmask": np.ascontiguousarray(cmask), "ebias": ebias, "rope": rope, "flags": flags,
        "gains": np.ascontiguousarray(gains), "vecA": vecA, "vecB": vecB,
        "s_state": I["state_rwkv"][0, b0:b0 + 16].reshape(128, 4096),
        "s_shift": I["state_rwkv_shift"][0, b0:b0 + 16],
        "s_swak": I["cache_swa_k"][0, b0:b0 + 16].reshape(16, 128, 128),
        "s_swav": I["cache_swa_v"][0, b0:b0 + 16].reshape(16, 128, 128),
        "selt": (np.arange(128)[None, :] // 8 == np.arange(16)[:, None]).astype(f32),
        "sel2": (np.arange(128)[:, None] // 8 == np.arange(16)[None, :]).astype(f32),
        "s_memk": I["cache_mem_k"][0, b0:b0 + 16].reshape(16, 256, 512),
        "s_memv": I["cache_mem_v"][0, b0:b0 + 16].reshape(16, 256, 512),
        "memp": I["mem_prompt"][seq],
        "w_in": I["w_in"][0], "w2a": np.concatenate([I["rw_w2"][0], I["rw_a2"][0]], axis=0), "g2": I["rw_g2"][0],
        "w_mkv": I["w_mem_kv"][0],
        "w_br": np.concatenate([I["w_br_a"][0], I["w_br_b"][0], I["w_br_c"][0]], axis=0),
        "w_out": I["w_out"][0], "w_up": I["w_up"][0], "w_down": I["w_down"][0],
    }
    return {k: np.ascontiguousarray(v, dtype=f32) for k, v in m.items()}


_NC_CACHE = {}


def kernel(**inputs):
    I = {k: np.asarray(v) for k, v in inputs.items()}
    B, SEQ, _ = I["x_prompt"].shape
    NT = SEQ // (4 * 128)
    if NT not in _NC_CACHE:
        _NC_CACHE[NT] = build_nc(NT)
    nc = _NC_CACHE[NT]
    in_maps = [_host_inputs(NT, c, I) for c in range(NCORES)]
    res = run_bass_kernel_spmd(nc, in_maps, core_ids=list(range(NCORES)))
    return assemble(res.results, NT)


def assemble(R, NT):
    f32 = np.float32
    SEQ = 4 * NT * 128
    y_prompt = np.zeros((2, SEQ, D), f32)
    for c in range(8):
        y_prompt[c // 4, (c % 4) * NT * 128:(c % 4 + 1) * NT * 128] = R[c]["y"].reshape(NT * 128, D)
    y_sample = np.concatenate([R[c]["ys"].reshape(16, D) for c in range(8)], axis=0).reshape(128, 1, D)
    st_p = np.stack([R[3]["stp"].reshape(8, 64, 64), R[7]["stp"].reshape(8, 64, 64)])[None]
    shift_p = np.stack([R[3]["zlast"].reshape(-1), R[7]["zlast"].reshape(-1)])[None]
    swak_p = np.stack([R[3]["swakp"], R[7]["swakp"]]).reshape(1, 2, 128, 2, 64)
    swav_p = np.stack([R[3]["swavp"], R[7]["swavp"]]).reshape(1, 2, 128, 2, 64)
    memk_p = np.stack([R[0]["memk"], R[4]["memk"]]).reshape(1, 2, 256, 4, 128)
    memv_p = np.stack([R[0]["memv"], R[4]["memv"]]).reshape(1, 2, 256, 4, 128)
    st_s = np.concatenate([R[c]["sts"].reshape(16, 8, 64, 64) for c in range(8)], axis=0)[None]
    shift_s = np.concatenate([R[c]["shifts"].reshape(16, 1792) for c in range(8)], axis=0)[None]
    swak_s = np.concatenate([R[c]["swaks"].reshape(16, 128, 2, 64) for c in range(8)], axis=0)[None]
    swav_s = np.concatenate([R[c]["swavs"].reshape(16, 128, 2, 64) for c in range(8)], axis=0)[None]
    outs = (y_prompt, y_sample, st_p, shift_p, swak_p, swav_p, memk_p, memv_p, st_s, shift_s, swak_s, swav_s)
    return tuple(np.ascontiguousarray(o, dtype=f32) for o in outs)
```
